# Optimizing a Trainium2 kernel written in Bass

```python
import jax, jax.numpy as jnp
from jax import lax
import numpy as np


D_MODEL = 1024
BATCH = 16
SEQ = 4096
DEPTH = 1
DEC_BATCH = 8
DEC_SEQ = 16
PAST_LEN = 2048

CHUNK = 64
LEFT_CHUNKS = 8
BAND = LEFT_CHUNKS * CHUNK
N_HEADS = 16
HEAD_DIM = 64
ATTN_WIDTH = N_HEADS * HEAD_DIM
ATTN_SCALE = HEAD_DIM ** -0.5
MAX_REL = 128
LRU_WIDTH = 1024
LRU_BLOCKS = 16
LRU_BLOCK = LRU_WIDTH // LRU_BLOCKS
CONV_W = 4
LRU_C = 8.0
PEER_HEADS = 8
N_KEYS = 128
N_EXPERTS = N_KEYS * N_KEYS
D_KEY = 256
HALF_KEY = D_KEY // 2
TOPK_HALF = 16
TOPK = 16
TOKEN_BLOCK = 128
EPS = 1e-6
NEG_INF = -1e30
IN_COLS = 3 * ATTN_WIDTH + 2 * LRU_WIDTH + 2 * D_MODEL
SPLITS = (ATTN_WIDTH, 2 * ATTN_WIDTH, 3 * ATTN_WIDTH, 3 * ATTN_WIDTH + LRU_WIDTH,
          3 * ATTN_WIDTH + 2 * LRU_WIDTH, 3 * ATTN_WIDTH + 2 * LRU_WIDTH + D_MODEL)

kernel_name = 'hybrid_stream_band_rglru_peer'


def rmsnorm(x, g):
    xf = x.astype(jnp.float32)
    y = xf * lax.rsqrt(jnp.mean(xf * xf, axis=-1, keepdims=True) + EPS)
    return (y * g.astype(jnp.float32)).astype(x.dtype)


def rel_bias(rel_table, q_pos, k_pos):
    idx = jnp.clip(q_pos[:, None] - k_pos[None, :], -MAX_REL, MAX_REL) + MAX_REL
    return rel_table[:, idx].astype(jnp.float32)


def band_attention_prompt(q, k, v, rel_table):
    bsz, seq = q.shape[0], q.shape[1]
    n_chunks = seq // CHUNK
    span = BAND + CHUNK
    pad = ((0, 0), (BAND, 0), (0, 0), (0, 0))
    k_pad = jnp.pad(k, pad)
    v_pad = jnp.pad(v, pad)
    offs = jnp.arange(span)
    bias = rel_bias(rel_table, jnp.arange(CHUNK) + BAND, offs)
    q_chunks = jnp.moveaxis(q.reshape(bsz, n_chunks, CHUNK, N_HEADS, HEAD_DIM), 1, 0)

    def one_chunk(args):
        c, q_c = args
        start = c * CHUNK
        k_b = lax.dynamic_slice_in_dim(k_pad, start, span, axis=1)
        v_b = lax.dynamic_slice_in_dim(v_pad, start, span, axis=1)
        valid = (start - BAND + offs) >= 0
        s = jnp.einsum('bqhd,bkhd->bhqk', q_c, k_b).astype(jnp.float32) * ATTN_SCALE + bias
        s = jnp.where(valid[None, None, None, :], s, NEG_INF)
        p = jax.nn.softmax(s, axis=-1).astype(v_b.dtype)
        return jnp.einsum('bhqk,bkhd->bqhd', p, v_b)

    o = lax.map(one_chunk, (jnp.arange(n_chunks), q_chunks))
    return jnp.moveaxis(o, 0, 1).reshape(bsz, seq, ATTN_WIDTH)


def band_attention_sample(q, k_new, v_new, cache_k, cache_v, rel_table):
    bsz, t_new = q.shape[0], q.shape[1]
    rows = cache_k.shape[1]
    k_all = jnp.concatenate([cache_k, k_new], axis=1)
    v_all = jnp.concatenate([cache_v, v_new], axis=1)
    q_pos = PAST_LEN + jnp.arange(t_new)
    k_pos = PAST_LEN - rows + jnp.arange(rows + t_new)
    bias = rel_bias(rel_table, q_pos, k_pos)
    s = jnp.einsum('bqhd,bkhd->bhqk', q, k_all).astype(jnp.float32) * ATTN_SCALE + bias
    p = jax.nn.softmax(s, axis=-1).astype(v_all.dtype)
    return jnp.einsum('bhqk,bkhd->bqhd', p, v_all).reshape(bsz, t_new, ATTN_WIDTH)


def causal_conv(x, conv_state, conv_w, conv_b):
    t = x.shape[1]
    xp = jnp.concatenate([conv_state, x], axis=1)
    y = conv_b
    for i in range(CONV_W):
        y = y + xp[:, i:i + t] * conv_w[i]
    return y, xp[:, xp.shape[1] - (CONV_W - 1):]


def _linear_combine(c1, c2):
    a1, b1 = c1
    a2, b2 = c2
    return a1 * a2, a2 * b1 + b2


def rg_lru(x, h0, w_r, b_r, w_i, b_i, lam):
    bsz, t, width = x.shape
    xf = x.astype(jnp.float32)
    xb = xf.reshape(bsz, t, LRU_BLOCKS, LRU_BLOCK)
    r = jax.nn.sigmoid(jnp.einsum('btgi,gij->btgj', xb, w_r.astype(jnp.float32)).reshape(bsz, t, width)
                       + b_r.astype(jnp.float32))
    gi = jax.nn.sigmoid(jnp.einsum('btgi,gij->btgj', xb, w_i.astype(jnp.float32)).reshape(bsz, t, width)
                        + b_i.astype(jnp.float32))
    log_a = -LRU_C * r * jax.nn.softplus(-lam.astype(jnp.float32))
    a = jnp.exp(log_a)
    b = jnp.sqrt(-jnp.expm1(2.0 * log_a)) * (gi * xf)
    b = b.at[:, 0].add(a[:, 0] * h0.astype(jnp.float32))
    _, h = lax.associative_scan(_linear_combine, (a, b), axis=1)
    return h.astype(x.dtype), h[:, -1].astype(x.dtype)


def mixer_sublayer(x, conv_state, h0, attend, norm_g, w_in, conv_w, conv_b, w_r, b_r, w_i, b_i, lam,
                   w_ba, w_bl, w_o):
    bsz, t = x.shape[0], x.shape[1]
    xn = rmsnorm(x, norm_g)
    proj = jnp.einsum('btd,dc->btc', xn, w_in)
    q, k, v, x_r, g_branch, gate_a, gate_r = jnp.split(proj, SPLITS, axis=-1)
    q = q.reshape(bsz, t, N_HEADS, HEAD_DIM)
    k = k.reshape(bsz, t, N_HEADS, HEAD_DIM)
    v = v.reshape(bsz, t, N_HEADS, HEAD_DIM)
    o_attn = attend(q, k, v)
    x_conv, conv_new = causal_conv(x_r, conv_state, conv_w, conv_b)
    h, h_last = rg_lru(x_conv, h0, w_r, b_r, w_i, b_i, lam)
    o_lru = h * jax.nn.gelu(g_branch)
    merged = jax.nn.sigmoid(gate_a) * (o_attn @ w_ba) + jax.nn.sigmoid(gate_r) * (o_lru @ w_bl)
    return x + merged @ w_o, k, v, conv_new, h_last


def peer_tokens(xt, wq, keys1, keys2, expert_u, expert_v):
    t = xt.shape[0]
    q = (xt @ wq).reshape(t, PEER_HEADS, 2, HALF_KEY)
    s1 = jnp.einsum('thd,hnd->thn', q[:, :, 0], keys1).astype(jnp.float32)
    s2 = jnp.einsum('thd,hnd->thn', q[:, :, 1], keys2).astype(jnp.float32)
    v1, i1 = lax.top_k(s1, TOPK_HALF)
    v2, i2 = lax.top_k(s2, TOPK_HALF)
    cand_s = (v1[..., :, None] + v2[..., None, :]).reshape(t, PEER_HEADS, TOPK_HALF * TOPK_HALF)
    cand_i = (i1[..., :, None] * N_KEYS + i2[..., None, :]).reshape(t, PEER_HEADS, TOPK_HALF * TOPK_HALF)
    top_s, pos = lax.top_k(cand_s, TOPK)
    e_idx = jnp.take_along_axis(cand_i, pos, axis=-1)
    g = jax.nn.softmax(top_s, axis=-1)
    u_sel = expert_u[e_idx]
    act = jax.nn.gelu(jnp.einsum('thkd,td->thk', u_sel, xt).astype(jnp.float32))
    return jnp.einsum('thk,thkd->td', (g * act).astype(xt.dtype), expert_v[e_idx])


def peer_sublayer(x, norm_g, wq, keys1, keys2, expert_u, expert_v, blocked):
    xt = rmsnorm(x, norm_g).reshape(-1, D_MODEL)
    if blocked:
        out = lax.map(lambda xb: peer_tokens(xb, wq, keys1, keys2, expert_u, expert_v),
                      xt.reshape(-1, TOKEN_BLOCK, D_MODEL))
    else:
        out = peer_tokens(xt, wq, keys1, keys2, expert_u, expert_v)
    return x + out.reshape(x.shape)


def setup_inputs(seed: int = 0) -> dict:
    key = jax.random.key(seed)
    ks = jax.random.split(key, 26)
    kv_rows = min(BAND, PAST_LEN)

    def nrm(k, shape, scale):
        return jax.random.normal(k, shape, jnp.float32) * scale

    u = jax.random.uniform(ks[15], (DEPTH, LRU_WIDTH), jnp.float32, 0.9, 0.999)
    a_base = u ** (1.0 / LRU_C)
    lam = jnp.log(a_base) - jnp.log1p(-a_base)
    return {
        'x_prompt': nrm(ks[0], (BATCH, SEQ, D_MODEL), 1.0),
        'x_sample': nrm(ks[1], (DEC_BATCH, DEC_SEQ, D_MODEL), 1.0),
        'cache_k': nrm(ks[2], (DEPTH, DEC_BATCH, kv_rows, N_HEADS, HEAD_DIM), 1.0),
        'cache_v': nrm(ks[3], (DEPTH, DEC_BATCH, kv_rows, N_HEADS, HEAD_DIM), 1.0),
        'state_conv': nrm(ks[4], (DEPTH, DEC_BATCH, CONV_W - 1, LRU_WIDTH), 1.0),
        'state_lru': nrm(ks[5], (DEPTH, DEC_BATCH, LRU_WIDTH), 0.5),
        'norm_mix': 1.0 + nrm(ks[6], (DEPTH, D_MODEL), 0.02),
        'w_in': nrm(ks[7], (DEPTH, D_MODEL, IN_COLS), D_MODEL ** -0.5),
        'rel_table': nrm(ks[8], (DEPTH, N_HEADS, 2 * MAX_REL + 1), 0.1),
        'conv_w': nrm(ks[9], (DEPTH, CONV_W, LRU_WIDTH), CONV_W ** -0.5),
        'conv_b': nrm(ks[10], (DEPTH, LRU_WIDTH), 0.01),
        'lru_wr': nrm(ks[11], (DEPTH, LRU_BLOCKS, LRU_BLOCK, LRU_BLOCK), LRU_BLOCK ** -0.5),
        'lru_br': nrm(ks[12], (DEPTH, LRU_WIDTH), 0.01),
        'lru_wi': nrm(ks[13], (DEPTH, LRU_BLOCKS, LRU_BLOCK, LRU_BLOCK), LRU_BLOCK ** -0.5),
        'lru_bi': nrm(ks[14], (DEPTH, LRU_WIDTH), 0.01),
        'lru_lambda': lam,
        'w_branch_attn': nrm(ks[16], (DEPTH, ATTN_WIDTH, D_MODEL), ATTN_WIDTH ** -0.5),
        'w_branch_lru': nrm(ks[17], (DEPTH, LRU_WIDTH, D_MODEL), LRU_WIDTH ** -0.5),
        'w_out': nrm(ks[18], (DEPTH, D_MODEL, D_MODEL), D_MODEL ** -0.5),
        'norm_ffn': 1.0 + nrm(ks[19], (DEPTH, D_MODEL), 0.02),
        'peer_wq': nrm(ks[20], (DEPTH, D_MODEL, PEER_HEADS * D_KEY), D_MODEL ** -0.5),
        'peer_keys1': nrm(ks[21], (DEPTH, PEER_HEADS, N_KEYS, HALF_KEY), HALF_KEY ** -0.5),
        'peer_keys2': nrm(ks[22], (DEPTH, PEER_HEADS, N_KEYS, HALF_KEY), HALF_KEY ** -0.5),
        'peer_u': nrm(ks[23], (DEPTH, N_EXPERTS, D_MODEL), D_MODEL ** -0.5),
        'peer_v': nrm(ks[24], (DEPTH, N_EXPERTS, D_MODEL), D_MODEL ** -0.5),
        'norm_final': 1.0 + nrm(ks[25], (D_MODEL,), 0.02),
    }


def reference(x_prompt, x_sample, cache_k, cache_v, state_conv, state_lru, norm_mix, w_in, rel_table,
              conv_w, conv_b, lru_wr, lru_br, lru_wi, lru_bi, lru_lambda, w_branch_attn, w_branch_lru,
              w_out, norm_ffn, peer_wq, peer_keys1, peer_keys2, peer_u, peer_v, norm_final):
    xp = x_prompt
    xs = x_sample
    k_prompt_l, v_prompt_l, conv_prompt_l, lru_prompt_l = [], [], [], []
    k_sample_l, v_sample_l, conv_sample_l, lru_sample_l = [], [], [], []
    for l in range(DEPTH):
        mix_w = (norm_mix[l], w_in[l], conv_w[l], conv_b[l], lru_wr[l], lru_br[l], lru_wi[l], lru_bi[l],
                 lru_lambda[l], w_branch_attn[l], w_branch_lru[l], w_out[l])
        ffn_w = (norm_ffn[l], peer_wq[l], peer_keys1[l], peer_keys2[l], peer_u[l], peer_v[l])
        rt = rel_table[l]
        ck = cache_k[l]
        cv = cache_v[l]
        conv0 = jnp.zeros((xp.shape[0], CONV_W - 1, LRU_WIDTH), xp.dtype)
        h_zero = jnp.zeros((xp.shape[0], LRU_WIDTH), xp.dtype)
        xp, k_p, v_p, c_p, h_p = mixer_sublayer(
            xp, conv0, h_zero, lambda q, k, v: band_attention_prompt(q, k, v, rt), *mix_w)
        xp = peer_sublayer(xp, *ffn_w, True)
        xs, k_s, v_s, c_s, h_s = mixer_sublayer(
            xs, state_conv[l], state_lru[l],
            lambda q, k, v: band_attention_sample(q, k, v, ck, cv, rt), *mix_w)
        xs = peer_sublayer(xs, *ffn_w, False)
        rows_p = min(BAND, k_p.shape[1])
        k_prompt_l.append(k_p[:, k_p.shape[1] - rows_p:])
        v_prompt_l.append(v_p[:, v_p.shape[1] - rows_p:])
        conv_prompt_l.append(c_p)
        lru_prompt_l.append(h_p)
        k_sample_l.append(k_s)
        v_sample_l.append(v_s)
        conv_sample_l.append(c_s)
        lru_sample_l.append(h_s)
    y_prompt = rmsnorm(xp, norm_final)
    y_sample = rmsnorm(xs, norm_final)
    return (y_prompt, y_sample,
            jnp.stack(k_prompt_l), jnp.stack(v_prompt_l), jnp.stack(conv_prompt_l), jnp.stack(lru_prompt_l),
            jnp.stack(k_sample_l), jnp.stack(v_sample_l), jnp.stack(conv_sample_l), jnp.stack(lru_sample_l))
```

```python
import os
import numpy as np
from contextlib import ExitStack
import concourse.bass as bass
import concourse.mybir as mybir
from concourse.bass_utils import run_bass_kernel_spmd

F32 = mybir.dt.float32
BF16 = mybir.dt.bfloat16
U32 = mybir.dt.uint32
AF = mybir.ActivationFunctionType
OP = mybir.AluOpType
AX = mybir.AxisListType

D = 1024
NH = 16
DH = 64
EPS = 1e-6
NGRP_W = 24
NGRP_UV = 64
NGRP = NGRP_W + NGRP_UV
SLOT = 4096
MASKV = -30000.0
CH = 30000
MINGAP = int(os.environ.get('K_MINGAP', '4'))


class SemGroup:
    def __init__(self):
        self.sem = None
        self.cnt = 0
        self.last = None


class Buf:
    def __init__(self, name, const=False, sg=None, region=None):
        self.name = name
        self.last_w = None
        self.readers = []
        self.const = const
        self.sg = sg if sg is not None else SemGroup()
        self.region = region
        self.aliases = []


class Op:
    __slots__ = ("eng", "fn", "dma", "sig", "deps", "ord", "sg", "cum", "pos")


class Prog:
    ENGS = ("sp", "act", "dve", "pool", "pe")

    def __init__(self, nc):
        self.nc = nc
        self.ops = []
        self.region_bufs = []
        self.out_dmas = []
        self.eng_n = {}

    def buf(self, name, const=False, sg=None, region=None):
        b = Buf(name, const, sg, region)
        if region is not None:
            for o in self.region_bufs:
                if o.region[0] == region[0] and o.region[1] < region[2] and region[1] < o.region[2]:
                    o.aliases.append(b)
                    b.aliases.append(o)
            self.region_bufs.append(b)
        return b

    def add(self, eng, fn, reads=(), writes=(), dma=False, out=False):
        op = Op()
        op.eng, op.fn, op.dma, op.sig, op.deps, op.ord = eng, fn, dma, False, [], 0
        op.sg, op.cum = None, 0
        self.eng_n[eng] = self.eng_n.get(eng, 0) + 1
        op.pos = self.eng_n[eng]
        strong = {}
        weak = {}
        for b in reads:
            if b.last_w is not None:
                strong[id(b.last_w)] = b.last_w
            for a in b.aliases:
                if a.last_w is not None:
                    strong[id(a.last_w)] = a.last_w
        for b in writes:
            for bb in [b] + b.aliases:
                if bb.last_w is not None:
                    strong[id(bb.last_w)] = bb.last_w
                for r in bb.readers:
                    weak[id(r)] = r
        if dma:
            pb = (list(writes) + list(reads))[0]
            sg = pb.sg
            if sg.last is not None and sg.last is not op:
                strong[id(sg.last)] = sg.last
        for k, d in weak.items():
            if k in strong:
                continue
            strong[k] = d
        best = {}
        for d in strong.values():
            if d is op:
                continue
            key, rank = (("d", id(d.sg)), d.cum) if d.dma else (("e", d.eng), d.pos)
            if key not in best or best[key][0] < rank:
                best[key] = (rank, d)
        for _, d in best.values():
            if (not d.dma) and (not dma) and d.eng == eng and eng == "pe":
                continue
            if (not d.dma) and (not dma) and d.eng == eng and eng in ("dve", "act") and op.pos - d.pos - 1 >= MINGAP:
                continue
            if not d.dma:
                d.sig = True
            op.deps.append(d)
        for b in reads:
            if not b.const:
                b.readers.append(op)
        for b in writes:
            b.last_w = op
            b.readers = []
        if dma:
            sg.cnt += 1
            sg.last = op
            op.sg, op.cum = sg, sg.cnt
            if out:
                self.out_dmas.append(op)
        self.ops.append(op)
        return op

    def emit(self, es):
        nc = self.nc
        cnt = {e: 0 for e in self.ENGS}
        sgs = {}
        for op in self.ops:
            if op.dma:
                sgs[id(op.sg)] = op.sg
            elif op.sig:
                cnt[op.eng] += 1
                op.ord = cnt[op.eng]
        esem = {}
        for e in self.ENGS:
            n = (cnt[e] + CH - 1) // CH
            esem[e] = [es.enter_context(nc.semaphore("se_%s_%d" % (e, i))) for i in range(n)]
        for i, sg in enumerate(sgs.values()):
            sg.sem = es.enter_context(nc.semaphore("sd_%d" % i))
        fin = Op()
        fin.eng, fin.fn, fin.dma, fin.sig, fin.deps, fin.ord = "sp", None, False, False, list(self.out_dmas), 0
        fin.sg, fin.cum, fin.pos = None, 0, 0
        self.ops.append(fin)
        block = es.enter_context(nc.Block())
        ops = self.ops

        def run(ename, eng):
            seen_e = {}
            seen_d = {}
            for op in ops:
                if op.eng != ename:
                    continue
                for d in op.deps:
                    if d.dma:
                        k = id(d.sg)
                        if seen_d.get(k, 0) >= d.cum:
                            continue
                        seen_d[k] = d.cum
                        eng.wait_ge(d.sg.sem, 16 * d.cum)
                    else:
                        if seen_e.get(d.eng, 0) >= d.ord:
                            continue
                        seen_e[d.eng] = d.ord
                        eng.wait_ge(esem[d.eng][(d.ord - 1) // CH], (d.ord - 1) % CH + 1)
                if op.fn is None:
                    continue
                inst = op.fn(eng)
                if op.dma:
                    inst.then_inc(op.sg.sem, 16)
                elif op.sig:
                    inst.then_inc(esem[ename][(op.ord - 1) // CH], 1)

        @block.sync
        def _(e):
            run("sp", e)

        @block.scalar
        def _(e):
            run("act", e)

        @block.vector
        def _(e):
            run("dve", e)

        @block.gpsimd
        def _(e):
            run("pool", e)

        @block.tensor
        def _(e):
            run("pe", e)


def build_nc(n_seq, seq_len, with_sample=True, do_peer=True, debug_stop=None, peer_stage=4):
    T = 256
    n_tiles = seq_len // T
    nc = bass.Bass("TRN2", target_bir_lowering=False)
    es = ExitStack()
    P = Prog(nc)

    def din(name, shape, dt=F32):
        return nc.dram_tensor(name, list(shape), dt, kind="ExternalInput").ap()

    def dout(name, shape, dt=F32):
        return nc.dram_tensor(name, list(shape), dt, kind="ExternalOutput").ap()

    xp = din("xp", [n_seq * seq_len, D])
    xs = din("xs", [16, D])
    ck = din("ck", [512, D])
    cv = din("cv", [512, D])
    sconv = din("sconv", [128, 8, 3])
    slru = din("slru", [128, 8])
    pvec_d = din("pvec", [128, 10, 8])
    g3_d = din("g3", [1, D])
    keysT_d = din("keysT", [128, 2 * 8 * 128])
    bd_d = din("bd", [128, 2 * 8 * 128])
    tb_d = din("tb", [128, 3 * 16 * 128])
    cvec_d = din("cvec", [1, 16 * 128])
    ident_d = din("ident", [128, 128])
    iota_d = din("iota", [128, 128])
    img = din("img", [NGRP * 256, 2048])

    yp = dout("yp", [n_seq * seq_len, D])
    ys = dout("ys", [16, D])
    kp_o = dout("kp", [n_seq * 512, D])
    vp_o = dout("vp", [n_seq * 512, D])
    convp_o = dout("convp", [n_seq, 128, 8, 3])
    lrup_o = dout("lrup", [n_seq, 128, 8])
    ks_o = dout("ks", [16, D])
    vs_o = dout("vs", [16, D])
    convs_o = dout("convs", [128, 8, 3])
    lrus_o = dout("lrus", [128, 8])

    scr = nc.dram_tensor("scr", [NGRP * 256, 2048], BF16, kind="Internal").ap()
    tbs = nc.dram_tensor("tbs", [128, 3 * 16 * 128], BF16, kind="Internal").ap()

    def sb(name, shape, dt):
        return es.enter_context(nc.sbuf_tensor("s_" + name, list(shape), dt))

    ident_f = sb("ident_f", [128, 128], F32)
    ident_b = sb("ident_b", [128, 128], BF16)
    iota_f = sb("iota_f", [128, 128], F32)
    ones_b = sb("ones_b", [128, 2], BF16)
    pv = sb("pv", [128, 10, 8], F32)
    nsp8 = sb("nsp8", [128, 8], F32)
    nsp16 = sb("nsp16", [128, 8], F32)
    g3bc = sb("g3bc", [128, D], F32)
    keysT = sb("keysT", [128, 2, 8, 128], BF16)
    bdb = sb("bdb", [128, 2, 8, 128], BF16)
    cvec = sb("cvec", [1, 16, 128], BF16)
    hstate = sb("hstate", [128, 8], F32)
    scratch1 = sb("scratch1", [128, 4], F32)
    NSLOT = 3
    ring = [sb("ring%d" % i, [128, SLOT], BF16) for i in range(NSLOT)]
    xbuf = [sb("xbuf%d" % i, [128, 2, D], F32) for i in range(2)]
    xnT = sb("xnT", [128, 8, T], BF16)
    kT = sb("kT", [128, 8, 6 * 128], BF16)
    Vb = sb("Vb", [128, 6, D], BF16)
    arA = sb("arA", [128, 16384], F32)
    BW = 12288
    arB = sb("arB", [128, BW], F32)

    bank = [es.enter_context(nc.psum_tensor("bank%d" % i, [128, 512], F32)) for i in range(8)]
    bankB = [P.buf("bank%d" % i) for i in range(8)]

    class V:
        pass

    def viewA(off, n, dt, name):
        ap = arA[:, off:off + n]
        if dt == BF16:
            ap = ap.bitcast(BF16)
        return ap, P.buf(name, region=("A", off, off + n))

    def viewB(off, n, dt, name):
        assert off + n <= BW, (name, off, n)
        ap = arB[:, off:off + n]
        if dt != F32:
            ap = ap.bitcast(dt)
        return ap, P.buf(name, region=("B", off, off + n))

    xc, xc_b = viewA(0, 2048, F32, "xc")
    rr, rr_b = viewA(2048, 2048, F32, "rr")
    gi, gi_b = viewA(4096, 2048, F32, "gi")
    t1, t1_b = viewA(6144, 2048, F32, "t1")
    gg, gg_b = viewA(8192, 2048, F32, "gg")
    sga, sga_b = viewA(10240, 2048, F32, "sga")
    sgr, sgr_b = viewA(12288, 2048, F32, "sgr")
    xcb, xcb_b = viewA(14336, 1024, BF16, "xcb")
    olT, olT_b = viewA(15360, 1024, BF16, "olT")
    Gsb, Gsb_b = viewA(0, 16384, BF16, "Gsb")

    def r3(ap, a):
        return ap.rearrange("p (a b) -> p a b", a=a)

    xc3, rr3, gi3, t13, gg3, sga3, sgr3, xcb3, olT3 = [r3(a, 8) for a in (xc, rr, gi, t1, gg, sga, sgr, xcb, olT)]
    Gsb3 = Gsb.rearrange("p (t j) -> p t j", j=128)

    o = 0
    xn, xn_b = viewB(o, 512, BF16, "xn"); o += 512
    junk, junk_b = viewB(o, 512, BF16, "junk"); o += 512
    qT, qT_b = viewB(o, 1024, BF16, "qT"); o += 1024
    oat, oat_b = viewB(o, 512, BF16, "oat"); o += 512
    oaT, oaT_b = viewB(o, 1024, BF16, "oaT"); o += 1024
    PT0, PT0_b = viewB(o, 256, BF16, "PT0"); o += 256
    PT1, PT1_b = viewB(o, 256, BF16, "PT1"); o += 256
    mT, mT_b = viewB(o, 1024, BF16, "mT"); o += 1024
    xrT, xrT_b = viewB(o, 8 * 260, F32, "xrT"); o += 8 * 260
    kvst, kvst_b = viewB(o, 1024, F32, "kvst"); o += 1024
    rden, rden_b = viewB(o, 16, F32, "rden"); o += 16
    tbb, tbb_b = viewB(o, 3072, BF16, "tbb"); o += 3072
    mixer_B_end = o
    qT3, oaT3, mT3 = r3(qT, 8), r3(oaT, 8), r3(mT, 8)
    xrT3 = xrT.rearrange("p (c t) -> p c t", c=8)
    tbb4 = tbb.rearrange("p (r h q) -> p r h q", r=3, h=16)
    PT = [(PT0, PT0_b), (PT1, PT1_b)]
    o = 1024
    qpT, qpT_b = viewB(o, 2048, BF16, "qpT"); o += 2048
    ssb, ssb_b = viewB(o, 1024, F32, "ssb"); o += 1024; ssb_off = o - 1024
    swork, swork_b = viewB(o, 256, F32, "swork"); o += 256
    v12, v12_b = viewB(o, 256, F32, "v12"); o += 256; v12_off = o - 256
    i12u, i12u_b = viewB(o, 256, U32, "i12u"); o += 256; i12u_off = o - 256
    i12f, i12f_b = viewB(o, 256, F32, "i12f"); o += 256
    cand, cand_b = viewB(o, 2048, F32, "cand"); o += 2048; cand_off = o - 2048
    tsv, tsv_b = viewB(o, 128, F32, "tsv"); o += 128; tsv_off = o - 128
    posu, posu_b = viewB(o, 128, U32, "posu"); o += 128; posu_off = o - 128
    abu, abu_b = viewB(o, 256, U32, "abu"); o += 256
    abf, abf_b = viewB(o, 256, F32, "abf"); o += 256
    eq, eq_b = viewB(o, 1024, BF16, "eq"); o += 1024
    sel, sel_b = viewB(o, 384, F32, "sel"); o += 384
    dd, dd_b = viewB(o, 128, F32, "dd"); o += 128
    zz, zz_b = viewB(o, 16, F32, "zz"); o += 16
    idxT, idxT_b = viewB(o, 768, F32, "idxT"); o += 768
    NOH = 8
    OHi, OHi_b = viewB(o, NOH * 64, BF16, "OHi"); o += NOH * 64
    OHj, OHj_b = viewB(o, NOH * 64, BF16, "OHj"); o += NOH * 64
    ge0, ge0_b = viewB(o, 128, BF16, "ge0"); o += 128
    ge1, ge1_b = viewB(o, 128, BF16, "ge1"); o += 128
    wd0, wd0_b = viewB(o, 128, BF16, "wd0"); o += 128
    wd1, wd1_b = viewB(o, 128, BF16, "wd1"); o += 128
    rstd, rstd_b = viewB(o, 8, F32, "rstd"); o += 8
    ssq, ssq_b = viewB(o, 8, F32, "ssq"); o += 8
    peer_B_end = o
    assert max(mixer_B_end, peer_B_end) <= BW
    qpT3 = r3(qpT, 16)
    ssb3 = r3(ssb, 8)
    v124 = v12.rearrange("p (s h k) -> p s h k", s=2, h=8)
    i12u4 = i12u.rearrange("p (s h k) -> p s h k", s=2, h=8)
    i12f4 = i12f.rearrange("p (s h k) -> p s h k", s=2, h=8)
    cand3 = cand.rearrange("p (h c) -> p h c", h=8)
    cand4 = cand.rearrange("p (h a b) -> p h a b", h=8, a=16)
    tsv3 = r3(tsv, 8)
    posu3 = r3(posu, 8)
    abu3 = r3(abu, 2)
    abf3 = r3(abf, 2)
    eq4 = eq.rearrange("p (h k a) -> p h k a", h=8, k=16)
    sel3 = r3(sel, 3)
    dd3 = r3(dd, 8)
    idxT3 = r3(idxT, 3)
    OHi3 = OHi.rearrange("p (t i) -> p t i", i=128)
    OHj3 = OHj.rearrange("p (t i) -> p t i", i=128)
    GE = [(ge0, ge0_b), (ge1, ge1_b)]
    WD = [(wd0, wd0_b), (wd1, wd1_b)]
    def hb(name, off, n):
        return P.buf(name, region=("B", off, off + n))

    ssbh_b = [hb("ssbh%d" % h, ssb_off + h * 128, 128) for h in range(8)]
    v12h_b = [[hb("v12h%d_%d" % (sd, h), v12_off + sd * 128 + h * 16, 16) for h in range(8)] for sd in range(2)]
    i12h_b = [[hb("i12h%d_%d" % (sd, h), i12u_off + sd * 128 + h * 16, 16) for h in range(8)] for sd in range(2)]
    candh_b = [hb("candh%d" % h, cand_off + h * 256, 256) for h in range(8)]
    tsvh_b = [hb("tsvh%d" % h, tsv_off + h * 16, 16) for h in range(8)]
    posh_b = [hb("posh%d" % h, posu_off + h * 16, 16) for h in range(8)]
    rs_t = sb("rs_t", [128, 8], F32)
    rstd, ssq = rs_t[:, 0:4], rs_t[:, 4:8]
    rstd_b, ssq_b = P.buf("rstd"), P.buf("ssq")

    consts_b = P.buf("consts", const=True)
    ring_b = [P.buf("ring%d" % i) for i in range(NSLOT)]
    xbuf_b = [P.buf("xbuf%d" % i) for i in range(2)]
    xnT_b = P.buf("xnT")
    kT_b = [P.buf("kT%d" % i) for i in range(6)]
    Vb_b = [P.buf("Vb%d" % i) for i in range(6)]
    hst_b = P.buf("hstate")
    hist_b = P.buf("xr_hist")
    g3_b = P.buf("g3bc", const=True)
    cast_sg = [SemGroup() for _ in range(6)]
    scr_b = [P.buf("scr%d" % g, sg=cast_sg[g % 6]) for g in range(NGRP)]
    tbs_b = P.buf("tbs", sg=cast_sg[0])
    misc_sg = SemGroup()

    def dma(q, out_ap, in_ap, reads, writes, out=False):
        return P.add(q, lambda e: e.dma_start(out=out_ap, in_=in_ap), reads=reads, writes=writes, dma=True, out=out)

    cst = P.buf("cst_load")
    dma("sp", ident_f[:], ident_d[:, :], [], [cst])
    dma("sp", iota_f[:], iota_d[:, :], [], [cst])
    dma("sp", pv[:], pvec_d[:, :, :], [], [cst])
    dma("sp", g3bc[:], g3_d.partition_broadcast(128), [], [cst])
    P.add("dve", lambda e: e.tensor_copy(out=ident_b[:], in_=ident_f[:]), reads=[cst], writes=[consts_b])
    P.add("dve", lambda e: e.memset(ones_b[:], 1.0), reads=[], writes=[consts_b])
    P.add("act", lambda e: e.activation(out=nsp8[:], in_=pv[:, 7, :], func=AF.Exp, scale=-1.0), reads=[cst], writes=[consts_b])
    P.add("act", lambda e: e.activation(out=nsp16[:], in_=nsp8[:], func=AF.Ln, bias=1.0), reads=[consts_b], writes=[consts_b])
    P.add("dve", lambda e: e.tensor_scalar(out=nsp8[:], in0=nsp16[:], scalar1=-8.0, scalar2=None, op0=OP.mult), reads=[consts_b], writes=[consts_b])
    P.add("dve", lambda e: e.tensor_scalar(out=nsp16[:], in0=nsp16[:], scalar1=-16.0, scalar2=None, op0=OP.mult), reads=[consts_b], writes=[consts_b])
    cstc = P.buf("cst_cast")
    dma("pool", keysT[:].rearrange("p s h n -> p (s h n)"), keysT_d[:, :], [consts_b], [cstc])
    dma("pool", bdb[:].rearrange("p s h n -> p (s h n)"), bd_d[:, :], [consts_b], [cstc])
    dma("pool", cvec[:].rearrange("p h n -> p (h n)"), cvec_d[:, :], [consts_b], [cstc])
    dma("pool", tbs.rearrange("p (a b) -> (p a) b", b=2048), tb_d.rearrange("p (a b) -> (p a) b", b=2048), [], [tbs_b])
    for g in range(NGRP):
        dma("pool", scr[g * 256:(g + 1) * 256, :], img[g * 256:(g + 1) * 256, :], [], [scr_b[g]])
    castdone = P.buf("castdone")
    dma("sp", scratch1[0:1, 0:1], ident_d[0:1, 0:1], scr_b + [tbs_b, cstc, cst, consts_b], [castdone])
    CR = [cst, cstc, consts_b, castdone]

    stream = {"next": 0, "total": 0, "plan": []}

    def stream_issue():
        n = stream["next"]
        if n >= stream["total"]:
            return
        g = stream["plan"][n]
        s = n % NSLOT
        src = scr[g * 256:(g + 1) * 256, :].rearrange("(p two) c -> p (two c)", two=2)
        dma("sp", ring[s][:], src, [scr_b[g]], [ring_b[s]])
        stream["next"] = n + 1

    use = {"n": 0}

    def stream_get(hold=0):
        n = use["n"]
        while stream["next"] < min(n + NSLOT - hold, stream["total"]):
            stream_issue()
        use["n"] = n + 1
        s = n % NSLOT
        return ring[s], ring_b[s]

    mmflip = {"i": 0}

    def mm_bank():
        i = mmflip["i"]
        mmflip["i"] = 1 - i
        return i

    def pe_mm(out_ap, lhsT, rhs, start, stop, reads, writes):
        P.add("pe", lambda e: e.matmul(out_ap, lhsT=lhsT, rhs=rhs, start=start, stop=stop), reads=reads, writes=writes)

    def pe_tr(out_ap, in_ap, ident_ap, reads, writes):
        P.add("pe", lambda e: e.transpose(out_ap, in_ap, ident_ap), reads=reads, writes=writes)

    def rmsnorm_T(xb, xb_buf, subs, gidx, dstT, dstT_buf):
        for si, (p0, npn) in enumerate(subs):
            P.add("act", lambda e, si=si, npn=npn: e.activation(out=junk[0:npn, :], in_=xb[0:npn, si, :], func=AF.Square,
                                                                accum_out=ssq[0:npn, si:si + 1]),
                  reads=[xb_buf], writes=[junk_b, ssq_b])
            P.add("act", lambda e, si=si, npn=npn: e.activation(out=rstd[0:npn, si:si + 1], in_=ssq[0:npn, si:si + 1], func=AF.Sqrt,
                                                                scale=1.0 / D, bias=eps_t[0:npn, 0:1]),
                  reads=[ssq_b] + CR, writes=[rstd_b])
            P.add("dve", lambda e, si=si, npn=npn: e.reciprocal(out=rstd[0:npn, si:si + 1], in_=rstd[0:npn, si:si + 1]),
                  reads=[rstd_b], writes=[rstd_b])
            P.add("dve", lambda e, si=si, npn=npn: e.tensor_scalar(out=xn[0:npn, :], in0=xb[0:npn, si, :], scalar1=rstd[0:npn, si:si + 1],
                                                                   scalar2=None, op0=OP.mult),
                  reads=[xb_buf, rstd_b], writes=[xn_b])
            tbs2 = [bank[7][:, :].bitcast(BF16), bank[6][:, :].bitcast(BF16)]
            for k in range(8):
                pe_tr(tbs2[k % 2][:, k * 128:k * 128 + npn], xn[0:npn, k * 128:(k + 1) * 128], ident_b[0:npn, 0:npn],
                      [xn_b] + CR, [bankB[7 - k % 2]])
            for k in range(8):
                tb = tbs2[k % 2]
                P.add("act" if k % 2 else "dve",
                      (lambda e, k=k, npn=npn, p0=p0, tb=tb: e.activation(out=dstT[:, k, p0:p0 + npn], in_=tb[:, k * 128:k * 128 + npn],
                                                                          func=AF.Copy, scale=pv[:, gidx, k:k + 1])) if k % 2 else
                      (lambda e, k=k, npn=npn, p0=p0, tb=tb: e.tensor_scalar(out=dstT[:, k, p0:p0 + npn], in0=tb[:, k * 128:k * 128 + npn],
                                                                             scalar1=pv[:, gidx, k:k + 1], scalar2=None, op0=OP.mult)),
                      reads=[bankB[7 - k % 2]] + CR, writes=[dstT_buf])

    eps_t = sb("eps_t", [128, 2], F32)
    P.add("dve", lambda e: e.memset(eps_t[:], EPS), reads=[castdone], writes=[consts_b])

    def proj_fm(srcT, srcT_buf, NT, epilogue):
        for half in range(2):
            W, Wb = stream_get()
            W3 = W[:, :].rearrange("p (k c) -> p k c", k=8)
            for cc in range(4):
                c = half * 4 + cc
                bi = mm_bank()
                ps = bank[bi][:, 0:NT]
                for k in range(8):
                    pe_mm(ps, W3[:, k, cc * 128:(cc + 1) * 128], srcT[:, k, 0:NT], k == 0, k == 7,
                          [Wb, srcT_buf], [bankB[bi]])
                epilogue(c, ps, bankB[bi])

    def proj_tm(srcT, srcT_buf, subs, W2, epilogue):
        for si, (p0, npn) in enumerate(subs):
            for half in range(2):
                W, Wb = W2[half]
                W3 = W[:, :].rearrange("p (k c) -> p k c", k=8)
                bi = mm_bank()
                ps = bank[bi][0:npn, :]
                for k in range(8):
                    pe_mm(ps, srcT[:, k, p0:p0 + npn], W3[:, k, :], k == 0, k == 7, [Wb, srcT_buf], [bankB[bi]])
                epilogue(si, half, ps, bankB[bi])

    def run_tile(xsrc, ydst, xbi, subs, gblk0, first_valid_blk, kv_out, conv_out, lru_out, is_last_of_seq):
        NT = sum(n for _, n in subs)
        xb, xb_buf = xbuf[xbi], xbuf_b[xbi]
        nsub = len(subs)
        if nsub == 2:
            dma("sp", xb[:], xsrc.rearrange("(s p) d -> p s d", p=128), [castdone], [xb_buf])
        else:
            dma("sp", xb[0:NT, 0, :], xsrc, [castdone], [xb_buf])
        dma("sp", tbb, tbs[:, :], [tbs_b], [tbb_b])
        rmsnorm_T(xb, xb_buf, subs, 8, xnT, xnT_b)

        def ep_q(c, ps, bb):
            P.add("act", lambda e: e.activation(out=qT3[:, c, 0:NT], in_=ps, func=AF.Copy, scale=0.125), reads=[bb], writes=[qT_b])

        proj_fm(xnT, xnT_b, NT, ep_q)
        kgroups = []
        slots_k = [(gblk0 + i) % 6 for i in range(nsub)]

        def ep_k(c, ps, bb):
            for si, (p0, npn) in enumerate(subs):
                s = slots_k[si]
                P.add("dve", lambda e, s=s, p0=p0, npn=npn: e.tensor_copy(out=kT[:, c, s * 128:s * 128 + npn], in_=ps[:, p0:p0 + npn]),
                      reads=[bb], writes=[kT_b[s]])

        for half in range(2):
            W, Wb = stream_get()
            kgroups.append((W, Wb))
            W3 = W[:, :].rearrange("p (k c) -> p k c", k=8)
            for cc in range(4):
                c = half * 4 + cc
                bi = mm_bank()
                ps = bank[bi][:, 0:NT]
                for k in range(8):
                    pe_mm(ps, W3[:, k, cc * 128:(cc + 1) * 128], xnT[:, k, 0:NT], k == 0, k == 7, [Wb, xnT_b], [bankB[bi]])
                ep_k(c, ps, bankB[bi])
            if kv_out is not None:
                for si, (p0, npn) in enumerate(subs):
                    bi = mm_bank()
                    ps = bank[bi][0:npn, :]
                    for k in range(8):
                        pe_mm(ps, xnT[:, k, p0:p0 + npn], W3[:, k, :], k == 0, k == 7, [Wb, xnT_b], [bankB[bi]])
                    P.add("act", lambda e, ps=ps, npn=npn: e.activation(out=kvst[0:npn, 0:512], in_=ps, func=AF.Copy), reads=[bankB[bi]], writes=[kvst_b])
                    dst = kv_out[0][kv_out[2] + p0:kv_out[2] + p0 + npn, half * 512:(half + 1) * 512]
                    dma("sp", dst, kvst[0:npn, 0:512], [kvst_b], [], out=True)
        vg = [stream_get(), stream_get(1)]

        def ep_v(si, half, ps, bb):
            p0, npn = subs[si]
            s = slots_k[si]
            P.add("act", lambda e: e.activation(out=Vb[0:npn, s, half * 512:(half + 1) * 512], in_=ps, func=AF.Copy), reads=[bb], writes=[Vb_b[s]])
            if kv_out is not None:
                P.add("act", lambda e: e.activation(out=kvst[0:npn, 512:1024], in_=ps, func=AF.Copy), reads=[bb], writes=[kvst_b])
                dst = kv_out[1][kv_out[2] + p0:kv_out[2] + p0 + npn, half * 512:(half + 1) * 512]
                dma("sp", dst, kvst[0:npn, 512:1024], [kvst_b], [], out=True)

        proj_tm(xnT, xnT_b, subs, vg, ep_v)

        def ep_xr(c, ps, bb):
            P.add("dve", lambda e: e.tensor_copy(out=xrT3[:, c, 3:3 + NT], in_=ps), reads=[bb], writes=[xrT_b])

        proj_fm(xnT, xnT_b, NT, ep_xr)

        def ep_g(c, ps, bb):
            P.add("act", lambda e: e.activation(out=gg3[:, c, 0:NT], in_=ps, func=AF.Gelu_apprx_tanh), reads=[bb], writes=[gg_b])

        proj_fm(xnT, xnT_b, NT, ep_g)

        def ep_ga(c, ps, bb):
            P.add("act", lambda e: e.activation(out=sga3[:, c, 0:NT], in_=ps, func=AF.Sigmoid), reads=[bb], writes=[sga_b])

        proj_fm(xnT, xnT_b, NT, ep_ga)

        def ep_gr(c, ps, bb):
            P.add("act", lambda e: e.activation(out=sgr3[:, c, 0:NT], in_=ps, func=AF.Sigmoid), reads=[bb], writes=[sgr_b])

        proj_fm(xnT, xnT_b, NT, ep_gr)

        for c in range(8):
            P.add("dve", lambda e, c=c: e.tensor_scalar(out=xc3[:, c, 0:NT], in0=xrT3[:, c, 3:3 + NT], scalar1=pv[:, 3, c:c + 1],
                                                        scalar2=pv[:, 4, c:c + 1], op0=OP.mult, op1=OP.add),
                  reads=[xrT_b] + CR, writes=[xc_b])
            for i in range(3):
                P.add("dve", lambda e, c=c, i=i: e.scalar_tensor_tensor(out=xc3[:, c, 0:NT], in0=xrT3[:, c, i:i + NT], scalar=pv[:, i, c:c + 1],
                                                                        in1=xc3[:, c, 0:NT], op0=OP.mult, op1=OP.add),
                      reads=[xrT_b, xc_b] + CR, writes=[xc_b])
        if conv_out is not None:
            P.add("act", lambda e: e.activation(out=kvst[:, 0:24].rearrange("p (c t) -> p c t", c=8), in_=xrT3[:, :, NT:NT + 3], func=AF.Copy),
                  reads=[xrT_b], writes=[kvst_b])
            dma("sp", conv_out, kvst[:, 0:24].rearrange("p (c t) -> p c t", c=8), [kvst_b], [], out=True)
        if not is_last_of_seq:
            P.add("pool", lambda e: e.tensor_copy(out=hist_t[:], in_=xrT3[:, :, NT:NT + 3]), reads=[xrT_b], writes=[hist_b])
        P.add("act", lambda e: e.activation(out=xcb3[:, :, 0:NT], in_=xc3[:, :, 0:NT], func=AF.Copy), reads=[xc_b], writes=[xcb_b])
        for c in range(8):
            bi = mm_bank()
            pe_mm(bank[bi][:, 0:NT], bdb[:, 0, c, :], xcb3[:, c, 0:NT], True, True, [xcb_b] + CR, [bankB[bi]])
            pe_mm(bank[bi][:, 256:256 + NT], bdb[:, 1, c, :], xcb3[:, c, 0:NT], True, True, [xcb_b] + CR, [bankB[bi]])
            P.add("act", lambda e, c=c, bi=bi: e.activation(out=rr3[:, c, 0:NT], in_=bank[bi][:, 0:NT], func=AF.Sigmoid, bias=pv[:, 5, c:c + 1]),
                  reads=[bankB[bi]] + CR, writes=[rr_b])
            P.add("act", lambda e, c=c, bi=bi: e.activation(out=gi3[:, c, 0:NT], in_=bank[bi][:, 256:256 + NT], func=AF.Sigmoid, bias=pv[:, 6, c:c + 1]),
                  reads=[bankB[bi]] + CR, writes=[gi_b])
        for c in range(8):
            P.add("act", lambda e, c=c: e.activation(out=t13[:, c, 0:NT], in_=rr3[:, c, 0:NT], func=AF.Exp, scale=nsp16[:, c:c + 1]),
                  reads=[rr_b] + CR, writes=[t1_b])
        for c in range(8):
            P.add("act", lambda e, c=c: e.activation(out=rr3[:, c, 0:NT], in_=rr3[:, c, 0:NT], func=AF.Exp, scale=nsp8[:, c:c + 1]),
                  reads=[rr_b, t1_b] + CR, writes=[rr_b])
        P.add("dve", lambda e: e.tensor_scalar(out=t13[:, :, 0:NT], in0=t13[:, :, 0:NT], scalar1=-1.0, scalar2=1.0, op0=OP.mult, op1=OP.add),
              reads=[t1_b], writes=[t1_b])
        P.add("dve", lambda e: e.tensor_scalar(out=t13[:, :, 0:NT], in0=t13[:, :, 0:NT], scalar1=1e-30, scalar2=None, op0=OP.max),
              reads=[t1_b], writes=[t1_b])
        P.add("act", lambda e: e.activation(out=t13[:, :, 0:NT], in_=t13[:, :, 0:NT], func=AF.Sqrt), reads=[t1_b], writes=[t1_b])
        P.add("dve", lambda e: e.tensor_tensor(out=t13[:, :, 0:NT], in0=t13[:, :, 0:NT], in1=gi3[:, :, 0:NT], op=OP.mult), reads=[t1_b, gi_b], writes=[t1_b])
        P.add("dve", lambda e: e.tensor_tensor(out=t13[:, :, 0:NT], in0=t13[:, :, 0:NT], in1=xc3[:, :, 0:NT], op=OP.mult), reads=[t1_b, xc_b], writes=[t1_b])
        for c in range(8):
            P.add("dve", lambda e, c=c: e.tensor_tensor_scan(out=gi3[:, c, 0:NT], data0=rr3[:, c, 0:NT], data1=t13[:, c, 0:NT],
                                                             initial=hstate[:, c:c + 1], op0=OP.mult, op1=OP.add),
                  reads=[rr_b, t1_b, hst_b, gi_b], writes=[gi_b])
        P.add("dve", lambda e: e.tensor_copy(out=hstate[:, :], in_=gi3[:, :, NT - 1]), reads=[gi_b], writes=[hst_b])
        if lru_out is not None:
            dma("sp", lru_out, hstate[:, :], [hst_b], [], out=True)
        P.add("dve", lambda e: e.tensor_tensor(out=olT3[:, :, 0:NT], in0=gi3[:, :, 0:NT], in1=gg3[:, :, 0:NT], op=OP.mult), reads=[gi_b, gg_b], writes=[olT_b])

        expn = {"n": 0}
        for qi, (p0, npn) in enumerate(subs):
            gq = gblk0 + qi
            blocks = []
            for r in range(5):
                gb = gq - 4 + r
                if gb < first_valid_blk:
                    continue
                nk = npn if r == 4 else 128
                blocks.append((r, gb % 6, nk))
            items = [(h, r, s, nk) for h in range(NH) for (r, s, nk) in blocks]
            groups = [items[i:i + 4] for i in range(0, len(items), 4)]
            firstr = blocks[0][0]
            lastr = blocks[-1][0]
            for grp in groups:
                bi = 2 + (expn["n"] % 2)
                pt, pt_b = PT[expn["n"] % 2]
                expn["n"] += 1
                for gi_, (h, r, s, nk) in enumerate(grp):
                    c, base = h // 2, (h % 2) * 64
                    out_ap = bank[bi][0:nk, gi_ * npn:(gi_ + 1) * npn]
                    pe_mm(out_ap, kT[base:base + 64, c, s * 128:s * 128 + nk], qT3[base:base + 64, c, p0:p0 + npn], True, False,
                          [kT_b[s], qT_b], [bankB[bi]])
                    if r in (0, 3, 4):
                        ti = {0: 0, 3: 1, 4: 2}[r]
                        pe_mm(out_ap, ident_b[:, 0:nk], tbb4[:, ti, h, 0:npn], False, True, [tbb_b] + CR, [bankB[bi]])
                    else:
                        pe_mm(out_ap, onesrow[0:1, 0:nk], cvec[0:1, h, 0:npn], False, True, CR, [bankB[bi]])
                ng = len(grp)
                nkmax = max(nk for (_, _, _, nk) in grp)
                P.add("act", lambda e, bi=bi, pt=pt, ng=ng, nkmax=nkmax, npn=npn: e.activation(out=pt[0:nkmax, 0:ng * npn], in_=bank[bi][0:nkmax, 0:ng * npn], func=AF.Exp),
                      reads=[bankB[bi]], writes=[pt_b])
                for gi_, (h, r, s, nk) in enumerate(grp):
                    ob = 4 + h // 8
                    pe_mm(bank[ob][0:npn, (h % 8) * 64:(h % 8 + 1) * 64], pt[0:nk, gi_ * npn:(gi_ + 1) * npn], Vb[0:nk, s, h * 64:(h + 1) * 64],
                          r == firstr, r == lastr, [pt_b, Vb_b[s]], [bankB[ob]])
                    pe_mm(bank[6][0:npn, h:h + 1], pt[0:nk, gi_ * npn:(gi_ + 1) * npn], ones_b[0:nk, 0:1],
                          r == firstr, r == lastr, [pt_b] + CR, [bankB[6]])
            P.add("dve", lambda e, npn=npn: e.reciprocal(out=rden[0:npn, 0:16], in_=bank[6][0:npn, 0:16]), reads=[bankB[6]], writes=[rden_b])
            for hb in range(2):
                P.add("dve", lambda e, npn=npn, hb=hb: e.tensor_tensor(
                    out=oat[0:npn, hb * 512:(hb + 1) * 512].rearrange("p (h d) -> p h d", h=8),
                    in0=bank[4 + hb][0:npn, :].rearrange("p (h d) -> p h d", h=8),
                    in1=rden[0:npn, hb * 8:(hb + 1) * 8].unsqueeze(2).to_broadcast([npn, 8, 64]), op=OP.mult),
                    reads=[bankB[4 + hb], rden_b], writes=[oat_b])
            tb = bank[7][:, :].bitcast(BF16)
            for k in range(8):
                pe_tr(tb[:, k * 128:k * 128 + npn], oat[0:npn, k * 128:(k + 1) * 128], ident_b[0:npn, 0:npn], [oat_b] + CR, [bankB[7]])
            P.add("act", lambda e, npn=npn, p0=p0: e.activation(out=oaT3[:, :, p0:p0 + npn], in_=tb.rearrange("p (k t) -> p k t", k=8)[:, :, 0:npn], func=AF.Copy),
                  reads=[bankB[7]], writes=[oaT_b])

        def ep_a(c, ps, bb):
            P.add("dve", lambda e: e.tensor_tensor(out=sga3[:, c, 0:NT], in0=ps, in1=sga3[:, c, 0:NT], op=OP.mult), reads=[bb, sga_b], writes=[sga_b])

        proj_fm(oaT3, oaT_b, NT, ep_a)

        def ep_l(c, ps, bb):
            P.add("dve", lambda e: e.tensor_tensor(out=sgr3[:, c, 0:NT], in0=ps, in1=sgr3[:, c, 0:NT], op=OP.mult), reads=[bb, sgr_b], writes=[sgr_b])

        proj_fm(olT3, olT_b, NT, ep_l)
        P.add("dve", lambda e: e.tensor_tensor(out=mT3[:, :, 0:NT], in0=sga3[:, :, 0:NT], in1=sgr3[:, :, 0:NT], op=OP.add), reads=[sga_b, sgr_b], writes=[mT_b])
        wo = [stream_get(), stream_get(1)]

        def ep_o(si, half, ps, bb):
            p0, npn = subs[si]
            P.add("dve", lambda e: e.tensor_tensor(out=xb[0:npn, si, half * 512:(half + 1) * 512], in0=ps, in1=xb[0:npn, si, half * 512:(half + 1) * 512], op=OP.add),
                  reads=[bb, xb_buf], writes=[xb_buf])

        proj_tm(mT3, mT_b, subs, wo, ep_o)

        if do_peer:
            peer(xb, xb_buf, subs, NT)
        else:
            for _ in range(4 + NGRP_UV):
                stream_get()

        for si, (p0, npn) in enumerate(subs):
            P.add("act", lambda e, si=si, npn=npn: e.activation(out=junk[0:npn, :], in_=xb[0:npn, si, :], func=AF.Square, accum_out=ssq[0:npn, si:si + 1]),
                  reads=[xb_buf], writes=[junk_b, ssq_b])
            P.add("act", lambda e, si=si, npn=npn: e.activation(out=rstd[0:npn, si:si + 1], in_=ssq[0:npn, si:si + 1], func=AF.Sqrt, scale=1.0 / D, bias=eps_t[0:npn, 0:1]),
                  reads=[ssq_b] + CR, writes=[rstd_b])
            P.add("dve", lambda e, si=si, npn=npn: e.reciprocal(out=rstd[0:npn, si:si + 1], in_=rstd[0:npn, si:si + 1]), reads=[rstd_b], writes=[rstd_b])
            P.add("dve", lambda e, si=si, npn=npn: e.scalar_tensor_tensor(out=xb[0:npn, si, :], in0=xb[0:npn, si, :], scalar=rstd[0:npn, si:si + 1], in1=g3bc[0:npn, :],
                                                                          op0=OP.mult, op1=OP.mult),
                  reads=[xb_buf, rstd_b] + CR, writes=[xb_buf])
        if nsub == 2:
            dma("sp", ydst.rearrange("(s p) d -> p s d", p=128), xb[:], [xb_buf], [], out=True)
        else:
            dma("sp", ydst, xb[0:NT, 0, :], [xb_buf], [], out=True)

    hist_t = sb("hist_t", [128, 8, 3], F32)
    onesrow = sb("onesrow", [1, 128], BF16)
    P.add("dve", lambda e: e.memset(onesrow[:], 1.0), reads=[castdone], writes=[consts_b])

    def peer(xb, xb_buf, subs, NT):
        rmsnorm_T(xb, xb_buf, subs, 9, xnT, xnT_b)
        cbase = {"c": 0}

        def ep_qp(c, ps, bb):
            cc = cbase["c"] + c
            P.add("act" if c % 2 else "dve",
                  (lambda e: e.activation(out=qpT3[:, cc, 0:NT], in_=ps, func=AF.Copy)) if c % 2 else
                  (lambda e: e.tensor_copy(out=qpT3[:, cc, 0:NT], in_=ps)), reads=[bb], writes=[qpT_b])

        proj_fm(xnT, xnT_b, NT, ep_qp)
        cbase["c"] = 8
        proj_fm(xnT, xnT_b, NT, ep_qp)
        if peer_stage < 2:
            for _ in range(NGRP_UV):
                stream_get()
            return
        for si, (p0, npn) in enumerate(subs):
            for side in range(2):
                for hh in range(2):
                    bi = mm_bank()
                    for h4 in range(4):
                        h = hh * 4 + h4
                        pe_mm(bank[bi][0:npn, h4 * 128:(h4 + 1) * 128], qpT3[:, 2 * h + side, p0:p0 + npn], keysT[:, side, h, :], True, True,
                              [qpT_b] + CR, [bankB[bi]])
                    P.add("act", lambda e, bi=bi, hh=hh, npn=npn: e.activation(out=ssb[0:npn, hh * 512:(hh + 1) * 512], in_=bank[bi][0:npn, :], func=AF.Copy),
                          reads=[bankB[bi]], writes=ssbh_b[hh * 4:(hh + 1) * 4])
                for lo in (0, 8):
                    for h in range(8):
                        P.add("dve", lambda e, h=h, npn=npn, side=side, lo=lo: e.max(out=v124[0:npn, side, h, lo:lo + 8], in_=ssb3[0:npn, h, :]),
                              reads=[ssbh_b[h]], writes=[v12h_b[side][h]])
                    for h in range(8):
                        P.add("dve", lambda e, h=h, npn=npn, side=side, lo=lo: e.max_index(out=i12u4[0:npn, side, h, lo:lo + 8], in_max=v124[0:npn, side, h, lo:lo + 8], in_values=ssb3[0:npn, h, :]),
                              reads=[ssbh_b[h], v12h_b[side][h]], writes=[i12h_b[side][h]])
                    if lo == 0:
                        for h in range(8):
                            P.add("dve", lambda e, h=h, npn=npn, side=side: e.match_replace(out=ssb3[0:npn, h, :], in_to_replace=v124[0:npn, side, h, 0:8], in_values=ssb3[0:npn, h, :], imm_value=-1e30),
                                  reads=[ssbh_b[h], v12h_b[side][h]], writes=[ssbh_b[h]])
            allv = [b for sd in v12h_b for b in sd]
            alli = [b for sd in i12h_b for b in sd]
            P.add("dve", lambda e, npn=npn: e.tensor_copy(out=i12f[0:npn, :], in_=i12u[0:npn, :]), reads=alli, writes=[i12f_b])
            P.add("dve", lambda e, npn=npn: e.tensor_tensor(out=cand4[0:npn], in0=v124[0:npn, 0].unsqueeze(3).to_broadcast([npn, 8, 16, 16]),
                                                            in1=v124[0:npn, 1].unsqueeze(2).to_broadcast([npn, 8, 16, 16]), op=OP.add),
                  reads=allv, writes=candh_b)
            for lo in (0, 8):
                for h in range(8):
                    P.add("dve", lambda e, h=h, npn=npn, lo=lo: e.max(out=tsv3[0:npn, h, lo:lo + 8], in_=cand3[0:npn, h, :]), reads=[candh_b[h]], writes=[tsvh_b[h]])
                for h in range(8):
                    P.add("dve", lambda e, h=h, npn=npn, lo=lo: e.max_index(out=posu3[0:npn, h, lo:lo + 8], in_max=tsv3[0:npn, h, lo:lo + 8], in_values=cand3[0:npn, h, :]),
                          reads=[candh_b[h], tsvh_b[h]], writes=[posh_b[h]])
                if lo == 0:
                    for h in range(8):
                        P.add("dve", lambda e, h=h, npn=npn: e.match_replace(out=cand3[0:npn, h, :], in_to_replace=tsv3[0:npn, h, 0:8], in_values=cand3[0:npn, h, :], imm_value=-1e30),
                              reads=[candh_b[h], tsvh_b[h]], writes=[candh_b[h]])
            P.add("dve", lambda e, npn=npn: e.tensor_scalar(out=abu3[0:npn, 0, :], in0=posu[0:npn, :], scalar1=4, scalar2=None, op0=OP.logical_shift_right),
                  reads=posh_b, writes=[abu_b])
            P.add("dve", lambda e, npn=npn: e.tensor_scalar(out=abu3[0:npn, 1, :], in0=posu[0:npn, :], scalar1=15, scalar2=None, op0=OP.bitwise_and),
                  reads=posh_b, writes=[abu_b])
            P.add("dve", lambda e, npn=npn: e.tensor_copy(out=abf[0:npn, :], in_=abu[0:npn, :]), reads=[abu_b], writes=[abf_b])
            for side in range(2):
                P.add("dve", lambda e, npn=npn, side=side: e.tensor_tensor(
                    out=eq4[0:npn], in0=iota_f[0:npn, 0:16].unsqueeze(1).unsqueeze(1).to_broadcast([npn, 8, 16, 16]),
                    in1=abf3[0:npn, side, :].rearrange("p (h k) -> p h k", h=8).unsqueeze(3).to_broadcast([npn, 8, 16, 16]), op=OP.is_equal),
                    reads=[abf_b] + CR, writes=[eq_b])
                P.add("dve", lambda e, npn=npn, side=side: e.tensor_tensor(
                    out=eq4[0:npn], in0=eq4[0:npn], in1=i12f4[0:npn, side].unsqueeze(2).to_broadcast([npn, 8, 16, 16]), op=OP.mult),
                    reads=[eq_b, i12f_b], writes=[eq_b])
                P.add("dve", lambda e, npn=npn, side=side: e.tensor_reduce(out=sel3[0:npn, side, :].rearrange("p (h k) -> p h k", h=8), in_=eq4[0:npn], axis=AX.X, op=OP.add),
                      reads=[eq_b], writes=[sel_b])
            P.add("dve", lambda e, npn=npn: e.tensor_tensor(out=dd3[0:npn], in0=tsv3[0:npn], in1=tsv3[0:npn, :, 0:1].to_broadcast([npn, 8, 16]), op=OP.subtract),
                  reads=tsvh_b, writes=[dd_b])
            P.add("act", lambda e, npn=npn: e.activation(out=dd[0:npn, :], in_=dd[0:npn, :], func=AF.Exp), reads=[dd_b], writes=[dd_b])
            P.add("dve", lambda e, npn=npn: e.tensor_reduce(out=zz[0:npn, 0:8], in_=dd3[0:npn], axis=AX.X, op=OP.add), reads=[dd_b], writes=[zz_b])
            P.add("dve", lambda e, npn=npn: e.reciprocal(out=zz[0:npn, 8:16], in_=zz[0:npn, 0:8]), reads=[zz_b], writes=[zz_b])
            P.add("dve", lambda e, npn=npn: e.tensor_tensor(out=sel3[0:npn, 2, :].rearrange("p (h k) -> p h k", h=8), in0=dd3[0:npn],
                                                            in1=zz[0:npn, 8:16].unsqueeze(2).to_broadcast([npn, 8, 16]), op=OP.mult),
                  reads=[dd_b, zz_b], writes=[sel_b])
            for q in range(3):
                pe_tr(bank[6][:, q * 128:q * 128 + npn], sel3[0:npn, q, :], ident_f[0:npn, 0:npn], [sel_b] + CR, [bankB[6]])
            P.add("act", lambda e, npn=npn, p0=p0: e.activation(out=idxT3[:, :, p0:p0 + npn], in_=bank[6][:, 0:384].rearrange("p (q t) -> p q t", q=3)[:, :, 0:npn], func=AF.Copy),
                  reads=[bankB[6]], writes=[idxT_b])
        if peer_stage < 3:
            for _ in range(NGRP_UV):
                stream_get()
            return
        gflip = 0
        for t0 in range(0, NT, NOH):
            nt = min(NOH, NT - t0)
            P.add("dve", lambda e, t0=t0, nt=nt: e.tensor_tensor(out=OHi3[:, 0:nt, :], in0=iota_f[:, :].unsqueeze(1).to_broadcast([128, nt, 128]),
                                                                   in1=idxT3[:, 0, t0:t0 + nt].unsqueeze(2).to_broadcast([128, nt, 128]), op=OP.is_equal),
                  reads=[idxT_b] + CR, writes=[OHi_b])
            P.add("dve", lambda e, t0=t0, nt=nt: e.tensor_tensor(out=OHj3[:, 0:nt, :], in0=iota_f[:, :].unsqueeze(1).to_broadcast([128, nt, 128]),
                                                                  in1=idxT3[:, 1, t0:t0 + nt].unsqueeze(2).to_broadcast([128, nt, 128]), op=OP.is_equal),
                  reads=[idxT_b] + CR, writes=[OHj_b])
            P.add("dve", lambda e, t0=t0, nt=nt: e.tensor_tensor(out=OHj3[:, 0:nt, :], in0=OHj3[:, 0:nt, :],
                                                                  in1=idxT3[:, 2, t0:t0 + nt].unsqueeze(2).to_broadcast([128, nt, 128]), op=OP.mult),
                  reads=[idxT_b, OHj_b], writes=[OHj_b])
            for tq in range(0, nt, 4):
                bi = 6 + gflip
                gflip = 1 - gflip
                n4 = min(4, nt - tq)
                for q in range(n4):
                    pe_mm(bank[bi][:, q * 128:(q + 1) * 128], OHi3[:, tq + q, :], OHj3[:, tq + q, :], True, True, [OHi_b, OHj_b], [bankB[bi]])
                P.add("act", lambda e, bi=bi, n4=n4, tt=t0 + tq: e.activation(out=Gsb3[:, tt:tt + n4, :], in_=bank[bi][:, 0:n4 * 128].rearrange("p (t j) -> p t j", j=128), func=AF.Copy),
                      reads=[bankB[bi]], writes=[Gsb_b])
        if peer_stage < 4:
            for _ in range(NGRP_UV):
                stream_get()
            return
        nsub = len(subs)
        for gidx in range(NGRP_UV):
            W, Wb = stream_get()
            U4 = W[:, 0:2048].rearrange("p (k j i) -> p k j i", k=8, j=2)
            V3 = W[:, 2048:4096].rearrange("p (j d) -> p j d", j=2)
            for jj in range(2):
                j = gidx * 2 + jj
                bi = j % 2
                ge, ge_b = GE[j % 2]
                wd, wd_b = WD[j % 2]
                ps = bank[bi][:, 0:NT]
                for k in range(8):
                    pe_mm(ps, U4[:, k, jj, :], xnT[:, k, 0:NT], k == 0, k == 7, [Wb, xnT_b], [bankB[bi]])
                P.add("act", lambda e, ps=ps, ge=ge: e.activation(out=ge[:, 0:NT], in_=ps, func=AF.Gelu_apprx_tanh), reads=[bankB[bi]], writes=[ge_b])
                P.add("dve", lambda e, ge=ge, wd=wd, j=j: e.tensor_tensor(out=wd[:, 0:NT], in0=ge[:, 0:NT], in1=Gsb3[:, 0:NT, j], op=OP.mult),
                      reads=[ge_b, Gsb_b], writes=[wd_b])
                for si, (p0, npn) in enumerate(subs):
                    for half in range(2):
                        ob = 2 + si * 2 + half
                        pe_mm(bank[ob][0:npn, :], wd[:, p0:p0 + npn], V3[:, jj, half * 512:(half + 1) * 512], j == 0, j == 127,
                              [wd_b, Wb], [bankB[ob]])
        for si, (p0, npn) in enumerate(subs):
            for half in range(2):
                ob = 2 + si * 2 + half
                P.add("dve", lambda e, ob=ob, si=si, npn=npn, half=half: e.tensor_tensor(out=xb[0:npn, si, half * 512:(half + 1) * 512], in0=bank[ob][0:npn, :],
                                                                                         in1=xb[0:npn, si, half * 512:(half + 1) * 512], op=OP.add),
                      reads=[bankB[ob], xb_buf], writes=[xb_buf])

    n_total_tiles = n_seq * n_tiles + (1 if with_sample else 0)
    stream["plan"] = [g for _ in range(n_total_tiles) for g in range(NGRP)]
    stream["total"] = len(stream["plan"])

    ti_global = 0
    if debug_stop == 'setup':
        P.add('sp', None, reads=scr_b + [tbs_b, cstc, cst, consts_b])
        n_seq = 0
        with_sample = False
    for s in range(n_seq):
        P.add("dve", lambda e: e.memset(hstate[:], 0.0), reads=[castdone], writes=[hst_b])
        P.add("dve", lambda e: e.memset(xrT3[:, :, 0:3], 0.0), reads=[castdone], writes=[xrT_b])
        for ti in range(n_tiles):
            tok0 = s * seq_len + ti * T
            last = ti == n_tiles - 1
            kv_out = None
            if seq_len - (ti + 1) * T < 512:
                row0 = s * 512 + (ti * T - (seq_len - 512)) if seq_len >= 512 else None
                kv_out = (kp_o, vp_o, row0)
            if ti > 0:
                P.add("pool", lambda e: e.tensor_copy(out=xrT3[:, :, 0:3], in_=hist_t[:]), reads=[hist_b], writes=[xrT_b])
            run_tile(xp[tok0:tok0 + T, :], yp[tok0:tok0 + T, :], ti_global % 2, [(0, 128), (128, 128)], 2 * ti, 0,
                     kv_out, convp_o[s] if last else None, lrup_o[s] if last else None, last)
            ti_global += 1
    if with_sample:
        for blk in range(4):
            dma("sp", xbuf[ti_global % 2][:, 0, :], ck[blk * 128:(blk + 1) * 128, :], [], [xbuf_b[ti_global % 2]])
            P.add("dve", lambda e: e.tensor_copy(out=xn[:, :], in_=xbuf[ti_global % 2][:, 0, :]), reads=[xbuf_b[ti_global % 2]], writes=[xn_b])
            tb = bank[7][:, :].bitcast(BF16)
            for k in range(8):
                pe_tr(tb[:, k * 128:(k + 1) * 128], xn[:, k * 128:(k + 1) * 128], ident_b[:, :], [xn_b] + CR, [bankB[7]])
            P.add("act", lambda e, blk=blk: e.activation(out=kT[:, :, blk * 128:(blk + 1) * 128], in_=tb.rearrange("p (k t) -> p k t", k=8), func=AF.Copy),
                  reads=[bankB[7]], writes=[kT_b[blk]])
            dma("sp", xbuf[ti_global % 2][:, 1, :], cv[blk * 128:(blk + 1) * 128, :], [], [xbuf_b[ti_global % 2]])
            P.add("dve", lambda e, blk=blk: e.tensor_copy(out=Vb[:, blk, :], in_=xbuf[ti_global % 2][:, 1, :]), reads=[xbuf_b[ti_global % 2]], writes=[Vb_b[blk]])
        dma("sp", hstate[:, :], slru[:, :], [], [hst_b])
        dma("sp", hist_t[:], sconv[:, :, :], [], [hist_b])
        P.add("pool", lambda e: e.tensor_copy(out=xrT3[:, :, 0:3], in_=hist_t[:]), reads=[hist_b], writes=[xrT_b])
        run_tile(xs[:, :], ys[:, :], (ti_global + 1) % 2, [(0, 16)], 4, 0, (ks_o, vs_o, 0), convs_o, lrus_o, True)

    P.emit(es)
    es.close()
    return nc


def _slot_images(w_in, w_ba, w_bl, w_o, wq, u, v):
    imgs = np.empty((NGRP, 128, SLOT), np.float32)

    def wgroups(W):
        C = W.shape[1]
        Wr = W.reshape(8, 128, C // 512, 512).transpose(2, 1, 0, 3)
        return Wr.reshape(C // 512, 128, SLOT)

    g = 0
    for W in (w_in, w_ba, w_bl, w_o, wq):
        im = wgroups(W)
        imgs[g:g + im.shape[0]] = im
        g += im.shape[0]
    assert g == NGRP_W
    u4 = u.reshape(128, 64, 2, 8, 128)
    imgs[NGRP_W:, :, 0:2048] = u4.transpose(1, 4, 3, 2, 0).reshape(64, 128, 2048)
    v4 = v.reshape(128, 64, 2, 1024)
    imgs[NGRP_W:, :, 2048:4096] = v4.transpose(1, 0, 2, 3).reshape(64, 128, 2048)
    return imgs.reshape(NGRP * 256, 2048)


def _bias_tables(rel_table):
    ko = np.arange(640)[:, None]
    qq = np.arange(128)[None, :]
    rel = np.clip(512 + qq - ko, -128, 128) + 128
    kc = ko // 64
    cq = qq // 64
    valid = (kc >= cq) & (kc <= cq + 8)
    tb = np.empty((128, 3, 16, 128), np.float32)
    for ti, r in enumerate((0, 3, 4)):
        sl = slice(r * 128, (r + 1) * 128)
        vals = rel_table[:, rel[sl]]
        vals = np.where(valid[sl][None], vals, np.float32(MASKV))
        tb[:, ti] = vals.transpose(1, 0, 2)
    cvec = np.repeat(rel_table[:, 256][:, None], 128, axis=1).reshape(1, 16 * 128)
    return tb.reshape(128, 3 * 16 * 128), np.ascontiguousarray(cvec)


def _fm(vec):
    return np.ascontiguousarray(vec.reshape(8, 128).T)


def _prep_shared(inputs):
    f = lambda k: np.asarray(inputs[k], np.float32)
    sh = {}
    sh["img"] = _slot_images(f("w_in")[0], f("w_branch_attn")[0], f("w_branch_lru")[0], f("w_out")[0], f("peer_wq")[0],
                             f("peer_u")[0], f("peer_v")[0])
    tb, cvec = _bias_tables(f("rel_table")[0])
    sh["tb"], sh["cvec"] = tb, cvec
    pv = np.empty((128, 10, 8), np.float32)
    cw = f("conv_w")[0]
    for i in range(4):
        pv[:, i, :] = _fm(cw[i])
    pv[:, 4, :] = _fm(f("conv_b")[0])
    pv[:, 5, :] = _fm(f("lru_br")[0])
    pv[:, 6, :] = _fm(f("lru_bi")[0])
    pv[:, 7, :] = _fm(f("lru_lambda")[0])
    pv[:, 8, :] = _fm(f("norm_mix")[0])
    pv[:, 9, :] = _fm(f("norm_ffn")[0])
    sh["pvec"] = pv
    sh["g3"] = f("norm_final").reshape(1, D)
    k1 = f("peer_keys1")[0]
    k2 = f("peer_keys2")[0]
    kt = np.stack([k1, k2], 0).transpose(3, 0, 1, 2)
    sh["keysT"] = np.ascontiguousarray(kt).reshape(128, 2 * 8 * 128)
    bd = np.zeros((128, 2, 8, 128), np.float32)
    for s_, key in enumerate(("lru_wr", "lru_wi")):
        w = f(key)[0]
        for c in range(8):
            bd[0:64, s_, c, 0:64] = w[2 * c]
            bd[64:128, s_, c, 64:128] = w[2 * c + 1]
    sh["bd"] = bd.reshape(128, 2 * 8 * 128)
    sh["ident"] = np.eye(128, dtype=np.float32)
    sh["iota"] = np.tile(np.arange(128, dtype=np.float32)[None, :], (128, 1))
    return sh


_NC_CACHE = {}


def kernel(**inputs):
    return _run(inputs, 8)


def _run(inputs, n_cores, **bkw):
    xpr = np.asarray(inputs["x_prompt"], np.float32)
    xsm = np.asarray(inputs["x_sample"], np.float32)
    B, S, _ = xpr.shape
    n_seq = B // n_cores
    sh = _prep_shared(inputs)
    key = (n_seq, S, tuple(sorted(bkw.items())))
    if key not in _NC_CACHE:
        _NC_CACHE[key] = build_nc(n_seq, S, **bkw)
    nc = _NC_CACHE[key]
    ck = np.asarray(inputs["cache_k"], np.float32)[0]
    cv = np.asarray(inputs["cache_v"], np.float32)[0]
    sc = np.asarray(inputs["state_conv"], np.float32)[0]
    sl = np.asarray(inputs["state_lru"], np.float32)[0]
    in_maps = []
    for c in range(n_cores):
        m = dict(sh)
        m["xp"] = np.ascontiguousarray(xpr[c * n_seq:(c + 1) * n_seq].reshape(n_seq * S, D))
        m["xs"] = np.ascontiguousarray(xsm[c])
        m["ck"] = np.ascontiguousarray(ck[c].reshape(512, D))
        m["cv"] = np.ascontiguousarray(cv[c].reshape(512, D))
        m["sconv"] = np.ascontiguousarray(sc[c].reshape(3, 8, 128).transpose(2, 1, 0))
        m["slru"] = _fm(sl[c])
        in_maps.append(m)
    res = run_bass_kernel_spmd(nc, in_maps, core_ids=list(range(n_cores)))
    R = res.results
    y_prompt = np.concatenate([r["yp"].reshape(n_seq, S, D) for r in R], 0)
    y_sample = np.stack([r["ys"] for r in R], 0)
    rows = min(512, S)
    k_prompt = np.concatenate([r["kp"].reshape(n_seq, rows, NH, DH) for r in R], 0)[None]
    v_prompt = np.concatenate([r["vp"].reshape(n_seq, rows, NH, DH) for r in R], 0)[None]
    conv_prompt = np.concatenate([r["convp"].transpose(0, 3, 2, 1).reshape(n_seq, 3, D) for r in R], 0)[None]
    lru_prompt = np.concatenate([r["lrup"].transpose(0, 2, 1).reshape(n_seq, D) for r in R], 0)[None]
    k_sample = np.stack([r["ks"].reshape(16, NH, DH) for r in R], 0)[None]
    v_sample = np.stack([r["vs"].reshape(16, NH, DH) for r in R], 0)[None]
    conv_sample = np.stack([r["convs"].transpose(2, 1, 0).reshape(3, D) for r in R], 0)[None]
    lru_sample = np.stack([r["lrus"].T.reshape(D) for r in R], 0)[None]
    f32 = lambda a: np.ascontiguousarray(a, dtype=np.float32)
    return tuple(f32(a) for a in (y_prompt, y_sample, k_prompt, v_prompt, conv_prompt, lru_prompt,
                                  k_sample, v_sample, conv_sample, lru_sample))
```

```python
import os
import numpy as np
from contextlib import ExitStack
import concourse.bass as bass
import concourse.mybir as mybir
from concourse.bass_utils import run_bass_kernel_spmd

F32 = mybir.dt.float32
BF16 = mybir.dt.bfloat16
U32 = mybir.dt.uint32
AF = mybir.ActivationFunctionType
OP = mybir.AluOpType
AX = mybir.AxisListType

D = 1024
NH = 16
DH = 64
EPS = 1e-6
NGRP_W = 24
NGRP_UV = 64
NGRP = NGRP_W + NGRP_UV
SLOT = 4096
MASKV = -30000.0
CH = 30000
MINGAP = int(os.environ.get('K_MINGAP', '4'))


class SemGroup:
    def __init__(self):
        self.sem = None
        self.cnt = 0
        self.last = None


class Buf:
    def __init__(self, name, const=False, sg=None, region=None):
        self.name = name
        self.last_w = None
        self.readers = []
        self.const = const
        self.sg = sg if sg is not None else SemGroup()
        self.region = region
        self.aliases = []


class Op:
    __slots__ = ("eng", "fn", "dma", "sig", "deps", "ord", "sg", "cum", "pos")


class Prog:
    ENGS = ("sp", "act", "dve", "pool", "pe")

    def __init__(self, nc):
        self.nc = nc
        self.ops = []
        self.region_bufs = []
        self.out_dmas = []
        self.eng_n = {}

    def buf(self, name, const=False, sg=None, region=None):
        b = Buf(name, const, sg, region)
        if region is not None:
            for o in self.region_bufs:
                if o.region[0] == region[0] and o.region[1] < region[2] and region[1] < o.region[2]:
                    o.aliases.append(b)
                    b.aliases.append(o)
            self.region_bufs.append(b)
        return b

    def add(self, eng, fn, reads=(), writes=(), dma=False, out=False):
        op = Op()
        op.eng, op.fn, op.dma, op.sig, op.deps, op.ord = eng, fn, dma, False, [], 0
        op.sg, op.cum = None, 0
        self.eng_n[eng] = self.eng_n.get(eng, 0) + 1
        op.pos = self.eng_n[eng]
        strong = {}
        weak = {}
        for b in reads:
            if b.last_w is not None:
                strong[id(b.last_w)] = b.last_w
            for a in b.aliases:
                if a.last_w is not None:
                    strong[id(a.last_w)] = a.last_w
        for b in writes:
            for bb in [b] + b.aliases:
                if bb.last_w is not None:
                    strong[id(bb.last_w)] = bb.last_w
                for r in bb.readers:
                    weak[id(r)] = r
        if dma:
            pb = (list(writes) + list(reads))[0]
            sg = pb.sg
            if sg.last is not None and sg.last is not op:
                strong[id(sg.last)] = sg.last
        for k, d in weak.items():
            if k in strong:
                continue
            strong[k] = d
        best = {}
        for d in strong.values():
            if d is op:
                continue
            key, rank = (("d", id(d.sg)), d.cum) if d.dma else (("e", d.eng), d.pos)
            if key not in best or best[key][0] < rank:
                best[key] = (rank, d)
        for _, d in best.values():
            if (not d.dma) and (not dma) and d.eng == eng and eng == "pe":
                continue
            if (not d.dma) and (not dma) and d.eng == eng and eng in ("dve", "act") and op.pos - d.pos - 1 >= MINGAP:
                continue
            if not d.dma:
                d.sig = True
            op.deps.append(d)
        for b in reads:
            if not b.const:
                b.readers.append(op)
        for b in writes:
            b.last_w = op
            b.readers = []
        if dma:
            sg.cnt += 1
            sg.last = op
            op.sg, op.cum = sg, sg.cnt
            if out:
                self.out_dmas.append(op)
        self.ops.append(op)
        return op

    def emit(self, es):
        nc = self.nc
        cnt = {e: 0 for e in self.ENGS}
        sgs = {}
        for op in self.ops:
            if op.dma:
                sgs[id(op.sg)] = op.sg
            elif op.sig:
                cnt[op.eng] += 1
                op.ord = cnt[op.eng]
        esem = {}
        for e in self.ENGS:
            n = (cnt[e] + CH - 1) // CH
            esem[e] = [es.enter_context(nc.semaphore("se_%s_%d" % (e, i))) for i in range(n)]
        for i, sg in enumerate(sgs.values()):
            sg.sem = es.enter_context(nc.semaphore("sd_%d" % i))
        fin = Op()
        fin.eng, fin.fn, fin.dma, fin.sig, fin.deps, fin.ord = "sp", None, False, False, list(self.out_dmas), 0
        fin.sg, fin.cum, fin.pos = None, 0, 0
        self.ops.append(fin)
        block = es.enter_context(nc.Block())
        ops = self.ops

        def run(ename, eng):
            seen_e = {}
            seen_d = {}
            for op in ops:
                if op.eng != ename:
                    continue
                for d in op.deps:
                    if d.dma:
                        k = id(d.sg)
                        if seen_d.get(k, 0) >= d.cum:
                            continue
                        seen_d[k] = d.cum
                        eng.wait_ge(d.sg.sem, 16 * d.cum)
                    else:
                        if seen_e.get(d.eng, 0) >= d.ord:
                            continue
                        seen_e[d.eng] = d.ord
                        eng.wait_ge(esem[d.eng][(d.ord - 1) // CH], (d.ord - 1) % CH + 1)
                if op.fn is None:
                    continue
                inst = op.fn(eng)
                if op.dma:
                    inst.then_inc(op.sg.sem, 16)
                elif op.sig:
                    inst.then_inc(esem[ename][(op.ord - 1) // CH], 1)

        @block.sync
        def _(e):
            run("sp", e)

        @block.scalar
        def _(e):
            run("act", e)

        @block.vector
        def _(e):
            run("dve", e)

        @block.gpsimd
        def _(e):
            run("pool", e)

        @block.tensor
        def _(e):
            run("pe", e)


def build_nc(n_seq, seq_len, with_sample=True, do_peer=True, debug_stop=None, peer_stage=4):
    T = 256
    n_tiles = seq_len // T
    nc = bass.Bass("TRN2", target_bir_lowering=False)
    es = ExitStack()
    P = Prog(nc)

    def din(name, shape, dt=F32):
        return nc.dram_tensor(name, list(shape), dt, kind="ExternalInput").ap()

    def dout(name, shape, dt=F32):
        return nc.dram_tensor(name, list(shape), dt, kind="ExternalOutput").ap()

    xp = din("xp", [n_seq * seq_len, D])
    xs = din("xs", [16, D])
    ck = din("ck", [512, D])
    cv = din("cv", [512, D])
    sconv = din("sconv", [128, 8, 3])
    slru = din("slru", [128, 8])
    pvec_d = din("pvec", [128, 10, 8])
    g3_d = din("g3", [1, D])
    keysT_d = din("keysT", [128, 2 * 8 * 128])
    bd_d = din("bd", [128, 2 * 8 * 128])
    tb_d = din("tb", [128, 3 * 16 * 128])
    cvec_d = din("cvec", [1, 16 * 128])
    ident_d = din("ident", [128, 128])
    iota_d = din("iota", [128, 128])
    img = din("img", [NGRP * 256, 2048])

    yp = dout("yp", [n_seq * seq_len, D])
    ys = dout("ys", [16, D])
    kp_o = dout("kp", [n_seq * 512, D])
    vp_o = dout("vp", [n_seq * 512, D])
    convp_o = dout("convp", [n_seq, 128, 8, 3])
    lrup_o = dout("lrup", [n_seq, 128, 8])
    ks_o = dout("ks", [16, D])
    vs_o = dout("vs", [16, D])
    convs_o = dout("convs", [128, 8, 3])
    lrus_o = dout("lrus", [128, 8])

    scr = nc.dram_tensor("scr", [NGRP * 256, 2048], BF16, kind="Internal").ap()
    tbs = nc.dram_tensor("tbs", [128, 3 * 16 * 128], BF16, kind="Internal").ap()

    def sb(name, shape, dt):
        return es.enter_context(nc.sbuf_tensor("s_" + name, list(shape), dt))

    ident_f = sb("ident_f", [128, 128], F32)
    ident_b = sb("ident_b", [128, 128], BF16)
    iota_f = sb("iota_f", [128, 128], F32)
    ones_b = sb("ones_b", [128, 2], BF16)
    pv = sb("pv", [128, 10, 8], F32)
    nsp8 = sb("nsp8", [128, 8], F32)
    nsp16 = sb("nsp16", [128, 8], F32)
    g3bc = sb("g3bc", [128, D], F32)
    keysT = sb("keysT", [128, 2, 8, 128], BF16)
    bdb = sb("bdb", [128, 2, 8, 128], BF16)
    cvec = sb("cvec", [1, 16, 128], BF16)
    hstate = sb("hstate", [128, 8], F32)
    scratch1 = sb("scratch1", [128, 4], F32)
    NSLOT = 4
    ring = [sb("ring%d" % i, [128, SLOT], BF16) for i in range(NSLOT)]
    xbuf = [sb("xbuf%d" % i, [128, 2, D], F32) for i in range(2)]
    xnT = sb("xnT", [128, 8, T], BF16)
    kT = sb("kT", [128, 8, 6 * 128], BF16)
    Vb = sb("Vb", [128, 6, D], BF16)
    arA = sb("arA", [128, 16384], F32)
    BW = 12288
    arB = sb("arB", [128, BW], F32)

    bank = [es.enter_context(nc.psum_tensor("bank%d" % i, [128, 512], F32)) for i in range(8)]
    bankB = [P.buf("bank%d" % i) for i in range(8)]

    class V:
        pass

    def viewA(off, n, dt, name):
        ap = arA[:, off:off + n]
        if dt == BF16:
            ap = ap.bitcast(BF16)
        return ap, P.buf(name, region=("A", off, off + n))

    def viewB(off, n, dt, name):
        assert off + n <= BW, (name, off, n)
        ap = arB[:, off:off + n]
        if dt != F32:
            ap = ap.bitcast(dt)
        return ap, P.buf(name, region=("B", off, off + n))

    xc, xc_b = viewA(0, 2048, F32, "xc")
    rr, rr_b = viewA(2048, 2048, F32, "rr")
    gi, gi_b = viewA(4096, 2048, F32, "gi")
    t1, t1_b = viewA(6144, 2048, F32, "t1")
    gg, gg_b = viewA(8192, 2048, F32, "gg")
    sga, sga_b = viewA(10240, 2048, F32, "sga")
    sgr, sgr_b = viewA(12288, 2048, F32, "sgr")
    xcb, xcb_b = viewA(14336, 1024, BF16, "xcb")
    olT, olT_b = viewA(15360, 1024, BF16, "olT")
    Gsb, Gsb_b = viewA(0, 16384, BF16, "Gsb")

    def r3(ap, a):
        return ap.rearrange("p (a b) -> p a b", a=a)

    xc3, rr3, gi3, t13, gg3, sga3, sgr3, xcb3, olT3 = [r3(a, 8) for a in (xc, rr, gi, t1, gg, sga, sgr, xcb, olT)]
    Gsb3 = Gsb.rearrange("p (t j) -> p t j", j=128)

    o = 0
    xn, xn_b = viewB(o, 512, BF16, "xn"); o += 512
    junk, junk_b = viewB(o, 512, BF16, "junk"); o += 512
    qT, qT_b = viewB(o, 1024, BF16, "qT"); o += 1024
    oat, oat_b = viewB(o, 512, BF16, "oat"); o += 512
    oaT, oaT_b = viewB(o, 1024, BF16, "oaT"); o += 1024
    PT0, PT0_b = viewB(o, 256, BF16, "PT0"); o += 256
    PT1, PT1_b = viewB(o, 256, BF16, "PT1"); o += 256
    mT, mT_b = viewB(o, 1024, BF16, "mT"); o += 1024
    xrT, xrT_b = viewB(o, 8 * 260, F32, "xrT"); o += 8 * 260
    kvst, kvst_b = viewB(o, 1024, F32, "kvst"); o += 1024
    rden, rden_b = viewB(o, 16, F32, "rden"); o += 16
    tbb, tbb_b = viewB(o, 3072, BF16, "tbb"); o += 3072
    mixer_B_end = o
    qT3, oaT3, mT3 = r3(qT, 8), r3(oaT, 8), r3(mT, 8)
    xrT3 = xrT.rearrange("p (c t) -> p c t", c=8)
    tbb4 = tbb.rearrange("p (r h q) -> p r h q", r=3, h=16)
    PT = [(PT0, PT0_b), (PT1, PT1_b)]
    o = 1024
    qpT, qpT_b = viewB(o, 2048, BF16, "qpT"); o += 2048
    ssb, ssb_b = viewB(o, 1024, F32, "ssb"); o += 1024; ssb_off = o - 1024
    swork, swork_b = viewB(o, 256, F32, "swork"); o += 256
    v12, v12_b = viewB(o, 256, F32, "v12"); o += 256; v12_off = o - 256
    i12u, i12u_b = viewB(o, 256, U32, "i12u"); o += 256; i12u_off = o - 256
    i12f, i12f_b = viewB(o, 256, F32, "i12f"); o += 256
    cand, cand_b = viewB(o, 2048, F32, "cand"); o += 2048; cand_off = o - 2048
    tsv, tsv_b = viewB(o, 128, F32, "tsv"); o += 128; tsv_off = o - 128
    posu, posu_b = viewB(o, 128, U32, "posu"); o += 128; posu_off = o - 128
    abu, abu_b = viewB(o, 256, U32, "abu"); o += 256
    abf, abf_b = viewB(o, 256, F32, "abf"); o += 256
    eq, eq_b = viewB(o, 1024, BF16, "eq"); o += 1024
    sel, sel_b = viewB(o, 384, F32, "sel"); o += 384
    dd, dd_b = viewB(o, 128, F32, "dd"); o += 128
    zz, zz_b = viewB(o, 16, F32, "zz"); o += 16
    idxT, idxT_b = viewB(o, 768, F32, "idxT"); o += 768
    NOH = 8
    OHi, OHi_b = viewB(o, NOH * 64, BF16, "OHi"); o += NOH * 64
    OHj, OHj_b = viewB(o, NOH * 64, BF16, "OHj"); o += NOH * 64
    ge0, ge0_b = viewB(o, 128, BF16, "ge0"); o += 128
    ge1, ge1_b = viewB(o, 128, BF16, "ge1"); o += 128
    wd0, wd0_b = viewB(o, 128, BF16, "wd0"); o += 128
    wd1, wd1_b = viewB(o, 128, BF16, "wd1"); o += 128
    rstd, rstd_b = viewB(o, 8, F32, "rstd"); o += 8
    ssq, ssq_b = viewB(o, 8, F32, "ssq"); o += 8
    peer_B_end = o
    assert max(mixer_B_end, peer_B_end) <= BW
    qpT3 = r3(qpT, 16)
    ssb3 = r3(ssb, 8)
    v124 = v12.rearrange("p (s h k) -> p s h k", s=2, h=8)
    i12u4 = i12u.rearrange("p (s h k) -> p s h k", s=2, h=8)
    i12f4 = i12f.rearrange("p (s h k) -> p s h k", s=2, h=8)
    cand3 = cand.rearrange("p (h c) -> p h c", h=8)
    cand4 = cand.rearrange("p (h a b) -> p h a b", h=8, a=16)
    tsv3 = r3(tsv, 8)
    posu3 = r3(posu, 8)
    abu3 = r3(abu, 2)
    abf3 = r3(abf, 2)
    eq4 = eq.rearrange("p (h k a) -> p h k a", h=8, k=16)
    sel3 = r3(sel, 3)
    dd3 = r3(dd, 8)
    idxT3 = r3(idxT, 3)
    OHi3 = OHi.rearrange("p (t i) -> p t i", i=128)
    OHj3 = OHj.rearrange("p (t i) -> p t i", i=128)
    GE = [(ge0, ge0_b), (ge1, ge1_b)]
    WD = [(wd0, wd0_b), (wd1, wd1_b)]
    def hb(name, off, n):
        return P.buf(name, region=("B", off, off + n))

    ssbh_b = [hb("ssbh%d" % h, ssb_off + h * 128, 128) for h in range(8)]
    v12h_b = [[hb("v12h%d_%d" % (sd, h), v12_off + sd * 128 + h * 16, 16) for h in range(8)] for sd in range(2)]
    i12h_b = [[hb("i12h%d_%d" % (sd, h), i12u_off + sd * 128 + h * 16, 16) for h in range(8)] for sd in range(2)]
    candh_b = [hb("candh%d" % h, cand_off + h * 256, 256) for h in range(8)]
    tsvh_b = [hb("tsvh%d" % h, tsv_off + h * 16, 16) for h in range(8)]
    posh_b = [hb("posh%d" % h, posu_off + h * 16, 16) for h in range(8)]
    rs_t = sb("rs_t", [128, 8], F32)
    rstd, ssq = rs_t[:, 0:4], rs_t[:, 4:8]
    rstd_b, ssq_b = P.buf("rstd"), P.buf("ssq")

    consts_b = P.buf("consts", const=True)
    ring_b = [P.buf("ring%d" % i) for i in range(NSLOT)]
    xbuf_b = [P.buf("xbuf%d" % i) for i in range(2)]
    xnT_b = P.buf("xnT")
    kT_b = [P.buf("kT%d" % i) for i in range(6)]
    Vb_b = [P.buf("Vb%d" % i) for i in range(6)]
    hst_b = P.buf("hstate")
    hist_b = P.buf("xr_hist")
    g3_b = P.buf("g3bc", const=True)
    cast_sg = [SemGroup() for _ in range(6)]
    scr_b = [P.buf("scr%d" % g, sg=cast_sg[g % 6]) for g in range(NGRP)]
    tbs_b = P.buf("tbs", sg=cast_sg[0])
    misc_sg = SemGroup()

    def dma(q, out_ap, in_ap, reads, writes, out=False):
        return P.add(q, lambda e: e.dma_start(out=out_ap, in_=in_ap), reads=reads, writes=writes, dma=True, out=out)

    cst = P.buf("cst_load")
    dma("sp", ident_f[:], ident_d[:, :], [], [cst])
    dma("sp", iota_f[:], iota_d[:, :], [], [cst])
    dma("sp", pv[:], pvec_d[:, :, :], [], [cst])
    dma("sp", g3bc[:], g3_d.partition_broadcast(128), [], [cst])
    P.add("dve", lambda e: e.tensor_copy(out=ident_b[:], in_=ident_f[:]), reads=[cst], writes=[consts_b])
    P.add("dve", lambda e: e.memset(ones_b[:], 1.0), reads=[], writes=[consts_b])
    P.add("act", lambda e: e.activation(out=nsp8[:], in_=pv[:, 7, :], func=AF.Exp, scale=-1.0), reads=[cst], writes=[consts_b])
    P.add("act", lambda e: e.activation(out=nsp16[:], in_=nsp8[:], func=AF.Ln, bias=1.0), reads=[consts_b], writes=[consts_b])
    P.add("dve", lambda e: e.tensor_scalar(out=nsp8[:], in0=nsp16[:], scalar1=-8.0, scalar2=None, op0=OP.mult), reads=[consts_b], writes=[consts_b])
    P.add("dve", lambda e: e.tensor_scalar(out=nsp16[:], in0=nsp16[:], scalar1=-16.0, scalar2=None, op0=OP.mult), reads=[consts_b], writes=[consts_b])
    cstc = P.buf("cst_cast")
    dma("pool", keysT[:].rearrange("p s h n -> p (s h n)"), keysT_d[:, :], [consts_b], [cstc])
    dma("pool", bdb[:].rearrange("p s h n -> p (s h n)"), bd_d[:, :], [consts_b], [cstc])
    dma("pool", cvec[:].rearrange("p h n -> p (h n)"), cvec_d[:, :], [consts_b], [cstc])
    dma("pool", tbs.rearrange("p (a b) -> (p a) b", b=2048), tb_d.rearrange("p (a b) -> (p a) b", b=2048), [], [tbs_b])
    for g in range(NGRP):
        dma("pool", scr[g * 256:(g + 1) * 256, :], img[g * 256:(g + 1) * 256, :], [], [scr_b[g]])
    castdone = P.buf("castdone")
    dma("sp", scratch1[0:1, 0:1], ident_d[0:1, 0:1], scr_b + [tbs_b, cstc, cst, consts_b], [castdone])
    CR = [cst, cstc, consts_b, castdone]

    stream = {"next": 0, "total": 0, "plan": []}

    def stream_issue():
        n = stream["next"]
        if n >= stream["total"]:
            return
        g = stream["plan"][n]
        s = n % NSLOT
        src = scr[g * 256:(g + 1) * 256, :].rearrange("(p two) c -> p (two c)", two=2)
        dma("sp", ring[s][:], src, [scr_b[g]], [ring_b[s]])
        stream["next"] = n + 1

    use = {"n": 0}

    def stream_get(hold=0):
        n = use["n"]
        while stream["next"] < min(n + NSLOT - hold, stream["total"]):
            stream_issue()
        use["n"] = n + 1
        s = n % NSLOT
        return ring[s], ring_b[s]

    mmflip = {"i": 0}

    def mm_bank():
        i = mmflip["i"]
        mmflip["i"] = 1 - i
        return i

    def pe_mm(out_ap, lhsT, rhs, start, stop, reads, writes):
        P.add("pe", lambda e: e.matmul(out_ap, lhsT=lhsT, rhs=rhs, start=start, stop=stop), reads=reads, writes=writes)

    def pe_tr(out_ap, in_ap, ident_ap, reads, writes):
        P.add("pe", lambda e: e.transpose(out_ap, in_ap, ident_ap), reads=reads, writes=writes)

    def rmsnorm_T(xb, xb_buf, subs, gidx, dstT, dstT_buf):
        for si, (p0, npn) in enumerate(subs):
            P.add("act", lambda e, si=si, npn=npn: e.activation(out=junk[0:npn, :], in_=xb[0:npn, si, :], func=AF.Square,
                                                                accum_out=ssq[0:npn, si:si + 1]),
                  reads=[xb_buf], writes=[junk_b, ssq_b])
            P.add("act", lambda e, si=si, npn=npn: e.activation(out=rstd[0:npn, si:si + 1], in_=ssq[0:npn, si:si + 1], func=AF.Sqrt,
                                                                scale=1.0 / D, bias=eps_t[0:npn, 0:1]),
                  reads=[ssq_b] + CR, writes=[rstd_b])
            P.add("dve", lambda e, si=si, npn=npn: e.reciprocal(out=rstd[0:npn, si:si + 1], in_=rstd[0:npn, si:si + 1]),
                  reads=[rstd_b], writes=[rstd_b])
            P.add("dve", lambda e, si=si, npn=npn: e.tensor_scalar(out=xn[0:npn, :], in0=xb[0:npn, si, :], scalar1=rstd[0:npn, si:si + 1],
                                                                   scalar2=None, op0=OP.mult),
                  reads=[xb_buf, rstd_b], writes=[xn_b])
            tbs2 = [bank[7][:, :].bitcast(BF16), bank[6][:, :].bitcast(BF16)]
            for k in range(8):
                pe_tr(tbs2[k % 2][:, k * 128:k * 128 + npn], xn[0:npn, k * 128:(k + 1) * 128], ident_b[0:npn, 0:npn],
                      [xn_b] + CR, [bankB[7 - k % 2]])
            for k in range(8):
                tb = tbs2[k % 2]
                P.add("act" if k % 2 else "dve",
                      (lambda e, k=k, npn=npn, p0=p0, tb=tb: e.activation(out=dstT[:, k, p0:p0 + npn], in_=tb[:, k * 128:k * 128 + npn],
                                                                          func=AF.Copy, scale=pv[:, gidx, k:k + 1])) if k % 2 else
                      (lambda e, k=k, npn=npn, p0=p0, tb=tb: e.tensor_scalar(out=dstT[:, k, p0:p0 + npn], in0=tb[:, k * 128:k * 128 + npn],
                                                                             scalar1=pv[:, gidx, k:k + 1], scalar2=None, op0=OP.mult)),
                      reads=[bankB[7 - k % 2]] + CR, writes=[dstT_buf])

    eps_t = sb("eps_t", [128, 2], F32)
    P.add("dve", lambda e: e.memset(eps_t[:], EPS), reads=[castdone], writes=[consts_b])

    def proj_fm(srcT, srcT_buf, NT, epilogue):
        for half in range(2):
            W, Wb = stream_get()
            W3 = W[:, :].rearrange("p (k c) -> p k c", k=8)
            for cc in range(4):
                c = half * 4 + cc
                bi = mm_bank()
                ps = bank[bi][:, 0:NT]
                for k in range(8):
                    pe_mm(ps, W3[:, k, cc * 128:(cc + 1) * 128], srcT[:, k, 0:NT], k == 0, k == 7,
                          [Wb, srcT_buf], [bankB[bi]])
                epilogue(c, ps, bankB[bi])

    def proj_tm(srcT, srcT_buf, subs, W2, epilogue):
        for si, (p0, npn) in enumerate(subs):
            for half in range(2):
                W, Wb = W2[half]
                W3 = W[:, :].rearrange("p (k c) -> p k c", k=8)
                bi = mm_bank()
                ps = bank[bi][0:npn, :]
                for k in range(8):
                    pe_mm(ps, srcT[:, k, p0:p0 + npn], W3[:, k, :], k == 0, k == 7, [Wb, srcT_buf], [bankB[bi]])
                epilogue(si, half, ps, bankB[bi])

    pre = {"loaded": None, "next": None, "cur": -1}

    def run_tile(xsrc, ydst, xbi, subs, gblk0, first_valid_blk, kv_out, conv_out, lru_out, is_last_of_seq):
        NT = sum(n for _, n in subs)
        xb, xb_buf = xbuf[xbi], xbuf_b[xbi]
        nsub = len(subs)
        if pre["loaded"] != pre["cur"]:
            if nsub == 2:
                dma("sp", xb[:], xsrc.rearrange("(s p) d -> p s d", p=128), [castdone], [xb_buf])
            else:
                dma("sp", xb[0:NT, 0, :], xsrc, [castdone], [xb_buf])
        dma("sp", tbb, tbs[:, :], [tbs_b], [tbb_b])
        rmsnorm_T(xb, xb_buf, subs, 8, xnT, xnT_b)

        def ep_q(c, ps, bb):
            P.add("act", lambda e: e.activation(out=qT3[:, c, 0:NT], in_=ps, func=AF.Copy, scale=0.125), reads=[bb], writes=[qT_b])

        proj_fm(xnT, xnT_b, NT, ep_q)
        kgroups = []
        slots_k = [(gblk0 + i) % 6 for i in range(nsub)]

        def ep_k(c, ps, bb):
            for si, (p0, npn) in enumerate(subs):
                s = slots_k[si]
                P.add("dve", lambda e, s=s, p0=p0, npn=npn: e.tensor_copy(out=kT[:, c, s * 128:s * 128 + npn], in_=ps[:, p0:p0 + npn]),
                      reads=[bb], writes=[kT_b[s]])

        for half in range(2):
            W, Wb = stream_get()
            kgroups.append((W, Wb))
            W3 = W[:, :].rearrange("p (k c) -> p k c", k=8)
            for cc in range(4):
                c = half * 4 + cc
                bi = mm_bank()
                ps = bank[bi][:, 0:NT]
                for k in range(8):
                    pe_mm(ps, W3[:, k, cc * 128:(cc + 1) * 128], xnT[:, k, 0:NT], k == 0, k == 7, [Wb, xnT_b], [bankB[bi]])
                ep_k(c, ps, bankB[bi])
            if kv_out is not None:
                for si, (p0, npn) in enumerate(subs):
                    bi = mm_bank()
                    ps = bank[bi][0:npn, :]
                    for k in range(8):
                        pe_mm(ps, xnT[:, k, p0:p0 + npn], W3[:, k, :], k == 0, k == 7, [Wb, xnT_b], [bankB[bi]])
                    P.add("act", lambda e, ps=ps, npn=npn: e.activation(out=kvst[0:npn, 0:512], in_=ps, func=AF.Copy), reads=[bankB[bi]], writes=[kvst_b])
                    dst = kv_out[0][kv_out[2] + p0:kv_out[2] + p0 + npn, half * 512:(half + 1) * 512]
                    dma("sp", dst, kvst[0:npn, 0:512], [kvst_b], [], out=True)
        vg = [stream_get(), stream_get(1)]

        def ep_v(si, half, ps, bb):
            p0, npn = subs[si]
            s = slots_k[si]
            P.add("act", lambda e: e.activation(out=Vb[0:npn, s, half * 512:(half + 1) * 512], in_=ps, func=AF.Copy), reads=[bb], writes=[Vb_b[s]])
            if kv_out is not None:
                P.add("act", lambda e: e.activation(out=kvst[0:npn, 512:1024], in_=ps, func=AF.Copy), reads=[bb], writes=[kvst_b])
                dst = kv_out[1][kv_out[2] + p0:kv_out[2] + p0 + npn, half * 512:(half + 1) * 512]
                dma("sp", dst, kvst[0:npn, 512:1024], [kvst_b], [], out=True)

        proj_tm(xnT, xnT_b, subs, vg, ep_v)

        def ep_xr(c, ps, bb):
            P.add("dve", lambda e: e.tensor_copy(out=xrT3[:, c, 3:3 + NT], in_=ps), reads=[bb], writes=[xrT_b])

        proj_fm(xnT, xnT_b, NT, ep_xr)

        def ep_g(c, ps, bb):
            P.add("act", lambda e: e.activation(out=gg3[:, c, 0:NT], in_=ps, func=AF.Gelu_apprx_tanh), reads=[bb], writes=[gg_b])

        proj_fm(xnT, xnT_b, NT, ep_g)

        def ep_ga(c, ps, bb):
            P.add("act", lambda e: e.activation(out=sga3[:, c, 0:NT], in_=ps, func=AF.Sigmoid), reads=[bb], writes=[sga_b])

        proj_fm(xnT, xnT_b, NT, ep_ga)

        def ep_gr(c, ps, bb):
            P.add("act", lambda e: e.activation(out=sgr3[:, c, 0:NT], in_=ps, func=AF.Sigmoid), reads=[bb], writes=[sgr_b])

        proj_fm(xnT, xnT_b, NT, ep_gr)

        for c in range(8):
            P.add("dve", lambda e, c=c: e.tensor_scalar(out=xc3[:, c, 0:NT], in0=xrT3[:, c, 3:3 + NT], scalar1=pv[:, 3, c:c + 1],
                                                        scalar2=pv[:, 4, c:c + 1], op0=OP.mult, op1=OP.add),
                  reads=[xrT_b] + CR, writes=[xc_b])
            for i in range(3):
                P.add("dve", lambda e, c=c, i=i: e.scalar_tensor_tensor(out=xc3[:, c, 0:NT], in0=xrT3[:, c, i:i + NT], scalar=pv[:, i, c:c + 1],
                                                                        in1=xc3[:, c, 0:NT], op0=OP.mult, op1=OP.add),
                      reads=[xrT_b, xc_b] + CR, writes=[xc_b])
        if conv_out is not None:
            P.add("act", lambda e: e.activation(out=kvst[:, 0:24].rearrange("p (c t) -> p c t", c=8), in_=xrT3[:, :, NT:NT + 3], func=AF.Copy),
                  reads=[xrT_b], writes=[kvst_b])
            dma("sp", conv_out, kvst[:, 0:24].rearrange("p (c t) -> p c t", c=8), [kvst_b], [], out=True)
        if not is_last_of_seq:
            P.add("pool", lambda e: e.tensor_copy(out=hist_t[:], in_=xrT3[:, :, NT:NT + 3]), reads=[xrT_b], writes=[hist_b])
        P.add("act", lambda e: e.activation(out=xcb3[:, :, 0:NT], in_=xc3[:, :, 0:NT], func=AF.Copy), reads=[xc_b], writes=[xcb_b])
        for c in range(8):
            bi = mm_bank()
            pe_mm(bank[bi][:, 0:NT], bdb[:, 0, c, :], xcb3[:, c, 0:NT], True, True, [xcb_b] + CR, [bankB[bi]])
            pe_mm(bank[bi][:, 256:256 + NT], bdb[:, 1, c, :], xcb3[:, c, 0:NT], True, True, [xcb_b] + CR, [bankB[bi]])
            P.add("act", lambda e, c=c, bi=bi: e.activation(out=rr3[:, c, 0:NT], in_=bank[bi][:, 0:NT], func=AF.Sigmoid, bias=pv[:, 5, c:c + 1]),
                  reads=[bankB[bi]] + CR, writes=[rr_b])
            P.add("act", lambda e, c=c, bi=bi: e.activation(out=gi3[:, c, 0:NT], in_=bank[bi][:, 256:256 + NT], func=AF.Sigmoid, bias=pv[:, 6, c:c + 1]),
                  reads=[bankB[bi]] + CR, writes=[gi_b])
        for c in range(8):
            P.add("act", lambda e, c=c: e.activation(out=t13[:, c, 0:NT], in_=rr3[:, c, 0:NT], func=AF.Exp, scale=nsp16[:, c:c + 1]),
                  reads=[rr_b] + CR, writes=[t1_b])
        for c in range(8):
            P.add("act", lambda e, c=c: e.activation(out=rr3[:, c, 0:NT], in_=rr3[:, c, 0:NT], func=AF.Exp, scale=nsp8[:, c:c + 1]),
                  reads=[rr_b, t1_b] + CR, writes=[rr_b])
        P.add("dve", lambda e: e.tensor_scalar(out=t13[:, :, 0:NT], in0=t13[:, :, 0:NT], scalar1=-1.0, scalar2=1.0, op0=OP.mult, op1=OP.add),
              reads=[t1_b], writes=[t1_b])
        P.add("dve", lambda e: e.tensor_scalar(out=t13[:, :, 0:NT], in0=t13[:, :, 0:NT], scalar1=1e-30, scalar2=None, op0=OP.max),
              reads=[t1_b], writes=[t1_b])
        P.add("act", lambda e: e.activation(out=t13[:, :, 0:NT], in_=t13[:, :, 0:NT], func=AF.Sqrt), reads=[t1_b], writes=[t1_b])
        P.add("dve", lambda e: e.tensor_tensor(out=t13[:, :, 0:NT], in0=t13[:, :, 0:NT], in1=gi3[:, :, 0:NT], op=OP.mult), reads=[t1_b, gi_b], writes=[t1_b])
        P.add("dve", lambda e: e.tensor_tensor(out=t13[:, :, 0:NT], in0=t13[:, :, 0:NT], in1=xc3[:, :, 0:NT], op=OP.mult), reads=[t1_b, xc_b], writes=[t1_b])
        for c in range(8):
            P.add("dve", lambda e, c=c: e.tensor_tensor_scan(out=gi3[:, c, 0:NT], data0=rr3[:, c, 0:NT], data1=t13[:, c, 0:NT],
                                                             initial=hstate[:, c:c + 1], op0=OP.mult, op1=OP.add),
                  reads=[rr_b, t1_b, hst_b, gi_b], writes=[gi_b])
        P.add("dve", lambda e: e.tensor_copy(out=hstate[:, :], in_=gi3[:, :, NT - 1]), reads=[gi_b], writes=[hst_b])
        if lru_out is not None:
            dma("sp", lru_out, hstate[:, :], [hst_b], [], out=True)
        P.add("dve", lambda e: e.tensor_tensor(out=olT3[:, :, 0:NT], in0=gi3[:, :, 0:NT], in1=gg3[:, :, 0:NT], op=OP.mult), reads=[gi_b, gg_b], writes=[olT_b])

        expn = {"n": 0}
        for qi, (p0, npn) in enumerate(subs):
            gq = gblk0 + qi
            blocks = []
            for r in range(5):
                gb = gq - 4 + r
                if gb < first_valid_blk:
                    continue
                nk = npn if r == 4 else 128
                blocks.append((r, gb % 6, nk))
            items = [(h, r, s, nk) for h in range(NH) for (r, s, nk) in blocks]
            groups = [items[i:i + 4] for i in range(0, len(items), 4)]
            firstr = blocks[0][0]
            lastr = blocks[-1][0]
            def emit_pv(grp, pt, pt_b):
                for gi_, (h, r, s, nk) in enumerate(grp):
                    ob = 4 + h // 8
                    pe_mm(bank[ob][0:npn, (h % 8) * 64:(h % 8 + 1) * 64], pt[0:nk, gi_ * npn:(gi_ + 1) * npn], Vb[0:nk, s, h * 64:(h + 1) * 64],
                          r == firstr, r == lastr, [pt_b, Vb_b[s]], [bankB[ob]])
                    pe_mm(bank[6][0:npn, h:h + 1], pt[0:nk, gi_ * npn:(gi_ + 1) * npn], ones_b[0:nk, 0:1],
                          r == firstr, r == lastr, [pt_b] + CR, [bankB[6]])

            pend = None
            for grp in groups:
                bi = 2 + (expn["n"] % 2)
                pt, pt_b = PT[expn["n"] % 2]
                expn["n"] += 1
                for gi_, (h, r, s, nk) in enumerate(grp):
                    c, base = h // 2, (h % 2) * 64
                    out_ap = bank[bi][0:nk, gi_ * npn:(gi_ + 1) * npn]
                    pe_mm(out_ap, kT[base:base + 64, c, s * 128:s * 128 + nk], qT3[base:base + 64, c, p0:p0 + npn], True, False,
                          [kT_b[s], qT_b], [bankB[bi]])
                    if r in (0, 3, 4):
                        ti = {0: 0, 3: 1, 4: 2}[r]
                        pe_mm(out_ap, ident_b[:, 0:nk], tbb4[:, ti, h, 0:npn], False, True, [tbb_b] + CR, [bankB[bi]])
                    else:
                        pe_mm(out_ap, onesrow[0:1, 0:nk], cvec[0:1, h, 0:npn], False, True, CR, [bankB[bi]])
                ng = len(grp)
                nkmax = max(nk for (_, _, _, nk) in grp)
                P.add("act", lambda e, bi=bi, pt=pt, ng=ng, nkmax=nkmax, npn=npn: e.activation(out=pt[0:nkmax, 0:ng * npn], in_=bank[bi][0:nkmax, 0:ng * npn], func=AF.Exp),
                      reads=[bankB[bi]], writes=[pt_b])
                if pend is not None:
                    emit_pv(*pend)
                pend = (grp, pt, pt_b)
            emit_pv(*pend)
            P.add("dve", lambda e, npn=npn: e.reciprocal(out=rden[0:npn, 0:16], in_=bank[6][0:npn, 0:16]), reads=[bankB[6]], writes=[rden_b])
            for hb in range(2):
                P.add("dve", lambda e, npn=npn, hb=hb: e.tensor_tensor(
                    out=oat[0:npn, hb * 512:(hb + 1) * 512].rearrange("p (h d) -> p h d", h=8),
                    in0=bank[4 + hb][0:npn, :].rearrange("p (h d) -> p h d", h=8),
                    in1=rden[0:npn, hb * 8:(hb + 1) * 8].unsqueeze(2).to_broadcast([npn, 8, 64]), op=OP.mult),
                    reads=[bankB[4 + hb], rden_b], writes=[oat_b])
            tb = bank[7][:, :].bitcast(BF16)
            for k in range(8):
                pe_tr(tb[:, k * 128:k * 128 + npn], oat[0:npn, k * 128:(k + 1) * 128], ident_b[0:npn, 0:npn], [oat_b] + CR, [bankB[7]])
            P.add("act", lambda e, npn=npn, p0=p0: e.activation(out=oaT3[:, :, p0:p0 + npn], in_=tb.rearrange("p (k t) -> p k t", k=8)[:, :, 0:npn], func=AF.Copy),
                  reads=[bankB[7]], writes=[oaT_b])

        def ep_a(c, ps, bb):
            P.add("dve", lambda e: e.tensor_tensor(out=sga3[:, c, 0:NT], in0=ps, in1=sga3[:, c, 0:NT], op=OP.mult), reads=[bb, sga_b], writes=[sga_b])

        proj_fm(oaT3, oaT_b, NT, ep_a)

        def ep_l(c, ps, bb):
            P.add("dve", lambda e: e.tensor_tensor(out=sgr3[:, c, 0:NT], in0=ps, in1=sgr3[:, c, 0:NT], op=OP.mult), reads=[bb, sgr_b], writes=[sgr_b])

        proj_fm(olT3, olT_b, NT, ep_l)
        P.add("dve", lambda e: e.tensor_tensor(out=mT3[:, :, 0:NT], in0=sga3[:, :, 0:NT], in1=sgr3[:, :, 0:NT], op=OP.add), reads=[sga_b, sgr_b], writes=[mT_b])
        wo = [stream_get(), stream_get(1)]

        def ep_o(si, half, ps, bb):
            p0, npn = subs[si]
            P.add("dve", lambda e: e.tensor_tensor(out=xb[0:npn, si, half * 512:(half + 1) * 512], in0=ps, in1=xb[0:npn, si, half * 512:(half + 1) * 512], op=OP.add),
                  reads=[bb, xb_buf], writes=[xb_buf])

        proj_tm(mT3, mT_b, subs, wo, ep_o)

        if do_peer:
            peer(xb, xb_buf, subs, NT)
        else:
            for _ in range(4 + NGRP_UV):
                stream_get()

        for si, (p0, npn) in enumerate(subs):
            P.add("act", lambda e, si=si, npn=npn: e.activation(out=junk[0:npn, :], in_=xb[0:npn, si, :], func=AF.Square, accum_out=ssq[0:npn, si:si + 1]),
                  reads=[xb_buf], writes=[junk_b, ssq_b])
            P.add("act", lambda e, si=si, npn=npn: e.activation(out=rstd[0:npn, si:si + 1], in_=ssq[0:npn, si:si + 1], func=AF.Sqrt, scale=1.0 / D, bias=eps_t[0:npn, 0:1]),
                  reads=[ssq_b] + CR, writes=[rstd_b])
            P.add("dve", lambda e, si=si, npn=npn: e.reciprocal(out=rstd[0:npn, si:si + 1], in_=rstd[0:npn, si:si + 1]), reads=[rstd_b], writes=[rstd_b])
            P.add("dve", lambda e, si=si, npn=npn: e.scalar_tensor_tensor(out=xb[0:npn, si, :], in0=xb[0:npn, si, :], scalar=rstd[0:npn, si:si + 1], in1=g3bc[0:npn, :],
                                                                          op0=OP.mult, op1=OP.mult),
                  reads=[xb_buf, rstd_b] + CR, writes=[xb_buf])
        if nsub == 2:
            dma("sp", ydst.rearrange("(s p) d -> p s d", p=128), xb[:], [xb_buf], [], out=True)
        else:
            dma("sp", ydst, xb[0:NT, 0, :], [xb_buf], [], out=True)

    hist_t = sb("hist_t", [128, 8, 3], F32)
    onesrow = sb("onesrow", [1, 128], BF16)
    P.add("dve", lambda e: e.memset(onesrow[:], 1.0), reads=[castdone], writes=[consts_b])

    def peer(xb, xb_buf, subs, NT):
        rmsnorm_T(xb, xb_buf, subs, 9, xnT, xnT_b)
        cbase = {"c": 0}

        def ep_qp(c, ps, bb):
            cc = cbase["c"] + c
            P.add("act" if c % 2 else "dve",
                  (lambda e: e.activation(out=qpT3[:, cc, 0:NT], in_=ps, func=AF.Copy)) if c % 2 else
                  (lambda e: e.tensor_copy(out=qpT3[:, cc, 0:NT], in_=ps)), reads=[bb], writes=[qpT_b])

        proj_fm(xnT, xnT_b, NT, ep_qp)
        cbase["c"] = 8
        proj_fm(xnT, xnT_b, NT, ep_qp)
        if peer_stage < 2:
            for _ in range(NGRP_UV):
                stream_get()
            return
        for si, (p0, npn) in enumerate(subs):
            for side in range(2):
                for hh in range(2):
                    bi = mm_bank()
                    for h4 in range(4):
                        h = hh * 4 + h4
                        pe_mm(bank[bi][0:npn, h4 * 128:(h4 + 1) * 128], qpT3[:, 2 * h + side, p0:p0 + npn], keysT[:, side, h, :], True, True,
                              [qpT_b] + CR, [bankB[bi]])
                    P.add("act", lambda e, bi=bi, hh=hh, npn=npn: e.activation(out=ssb[0:npn, hh * 512:(hh + 1) * 512], in_=bank[bi][0:npn, :], func=AF.Copy),
                          reads=[bankB[bi]], writes=ssbh_b[hh * 4:(hh + 1) * 4])
                for lo in (0, 8):
                    for h in range(8):
                        P.add("dve", lambda e, h=h, npn=npn, side=side, lo=lo: e.max(out=v124[0:npn, side, h, lo:lo + 8], in_=ssb3[0:npn, h, :]),
                              reads=[ssbh_b[h]], writes=[v12h_b[side][h]])
                    for h in range(8):
                        P.add("dve", lambda e, h=h, npn=npn, side=side, lo=lo: e.max_index(out=i12u4[0:npn, side, h, lo:lo + 8], in_max=v124[0:npn, side, h, lo:lo + 8], in_values=ssb3[0:npn, h, :]),
                              reads=[ssbh_b[h], v12h_b[side][h]], writes=[i12h_b[side][h]])
                    if lo == 0:
                        for h in range(8):
                            P.add("dve", lambda e, h=h, npn=npn, side=side: e.match_replace(out=ssb3[0:npn, h, :], in_to_replace=v124[0:npn, side, h, 0:8], in_values=ssb3[0:npn, h, :], imm_value=-1e30),
                                  reads=[ssbh_b[h], v12h_b[side][h]], writes=[ssbh_b[h]])
            allv = [b for sd in v12h_b for b in sd]
            alli = [b for sd in i12h_b for b in sd]
            P.add("dve", lambda e, npn=npn: e.tensor_copy(out=i12f[0:npn, :], in_=i12u[0:npn, :]), reads=alli, writes=[i12f_b])
            P.add("dve", lambda e, npn=npn: e.tensor_tensor(out=cand4[0:npn], in0=v124[0:npn, 0].unsqueeze(3).to_broadcast([npn, 8, 16, 16]),
                                                            in1=v124[0:npn, 1].unsqueeze(2).to_broadcast([npn, 8, 16, 16]), op=OP.add),
                  reads=allv, writes=candh_b)
            for lo in (0, 8):
                for h in range(8):
                    P.add("dve", lambda e, h=h, npn=npn, lo=lo: e.max(out=tsv3[0:npn, h, lo:lo + 8], in_=cand3[0:npn, h, :]), reads=[candh_b[h]], writes=[tsvh_b[h]])
                for h in range(8):
                    P.add("dve", lambda e, h=h, npn=npn, lo=lo: e.max_index(out=posu3[0:npn, h, lo:lo + 8], in_max=tsv3[0:npn, h, lo:lo + 8], in_values=cand3[0:npn, h, :]),
                          reads=[candh_b[h], tsvh_b[h]], writes=[posh_b[h]])
                if lo == 0:
                    for h in range(8):
                        P.add("dve", lambda e, h=h, npn=npn: e.match_replace(out=cand3[0:npn, h, :], in_to_replace=tsv3[0:npn, h, 0:8], in_values=cand3[0:npn, h, :], imm_value=-1e30),
                              reads=[candh_b[h], tsvh_b[h]], writes=[candh_b[h]])
            P.add("dve", lambda e, npn=npn: e.tensor_scalar(out=abu3[0:npn, 0, :], in0=posu[0:npn, :], scalar1=4, scalar2=None, op0=OP.logical_shift_right),
                  reads=posh_b, writes=[abu_b])
            P.add("dve", lambda e, npn=npn: e.tensor_scalar(out=abu3[0:npn, 1, :], in0=posu[0:npn, :], scalar1=15, scalar2=None, op0=OP.bitwise_and),
                  reads=posh_b, writes=[abu_b])
            P.add("dve", lambda e, npn=npn: e.tensor_copy(out=abf[0:npn, :], in_=abu[0:npn, :]), reads=[abu_b], writes=[abf_b])
            for side in range(2):
                P.add("dve", lambda e, npn=npn, side=side: e.tensor_tensor(
                    out=eq4[0:npn], in0=iota_f[0:npn, 0:16].unsqueeze(1).unsqueeze(1).to_broadcast([npn, 8, 16, 16]),
                    in1=abf3[0:npn, side, :].rearrange("p (h k) -> p h k", h=8).unsqueeze(3).to_broadcast([npn, 8, 16, 16]), op=OP.is_equal),
                    reads=[abf_b] + CR, writes=[eq_b])
                P.add("dve", lambda e, npn=npn, side=side: e.tensor_tensor(
                    out=eq4[0:npn], in0=eq4[0:npn], in1=i12f4[0:npn, side].unsqueeze(2).to_broadcast([npn, 8, 16, 16]), op=OP.mult),
                    reads=[eq_b, i12f_b], writes=[eq_b])
                P.add("dve", lambda e, npn=npn, side=side: e.tensor_reduce(out=sel3[0:npn, side, :].rearrange("p (h k) -> p h k", h=8), in_=eq4[0:npn], axis=AX.X, op=OP.add),
                      reads=[eq_b], writes=[sel_b])
            P.add("dve", lambda e, npn=npn: e.tensor_tensor(out=dd3[0:npn], in0=tsv3[0:npn], in1=tsv3[0:npn, :, 0:1].to_broadcast([npn, 8, 16]), op=OP.subtract),
                  reads=tsvh_b, writes=[dd_b])
            P.add("act", lambda e, npn=npn: e.activation(out=dd[0:npn, :], in_=dd[0:npn, :], func=AF.Exp), reads=[dd_b], writes=[dd_b])
            P.add("dve", lambda e, npn=npn: e.tensor_reduce(out=zz[0:npn, 0:8], in_=dd3[0:npn], axis=AX.X, op=OP.add), reads=[dd_b], writes=[zz_b])
            P.add("dve", lambda e, npn=npn: e.reciprocal(out=zz[0:npn, 8:16], in_=zz[0:npn, 0:8]), reads=[zz_b], writes=[zz_b])
            P.add("dve", lambda e, npn=npn: e.tensor_tensor(out=sel3[0:npn, 2, :].rearrange("p (h k) -> p h k", h=8), in0=dd3[0:npn],
                                                            in1=zz[0:npn, 8:16].unsqueeze(2).to_broadcast([npn, 8, 16]), op=OP.mult),
                  reads=[dd_b, zz_b], writes=[sel_b])
            for q in range(3):
                pe_tr(bank[6][:, q * 128:q * 128 + npn], sel3[0:npn, q, :], ident_f[0:npn, 0:npn], [sel_b] + CR, [bankB[6]])
            P.add("act", lambda e, npn=npn, p0=p0: e.activation(out=idxT3[:, :, p0:p0 + npn], in_=bank[6][:, 0:384].rearrange("p (q t) -> p q t", q=3)[:, :, 0:npn], func=AF.Copy),
                  reads=[bankB[6]], writes=[idxT_b])
        if peer_stage < 3:
            for _ in range(NGRP_UV):
                stream_get()
            return
        gflip = 0
        for t0 in range(0, NT, NOH):
            nt = min(NOH, NT - t0)
            P.add("dve", lambda e, t0=t0, nt=nt: e.tensor_tensor(out=OHi3[:, 0:nt, :], in0=iota_f[:, :].unsqueeze(1).to_broadcast([128, nt, 128]),
                                                                   in1=idxT3[:, 0, t0:t0 + nt].unsqueeze(2).to_broadcast([128, nt, 128]), op=OP.is_equal),
                  reads=[idxT_b] + CR, writes=[OHi_b])
            P.add("dve", lambda e, t0=t0, nt=nt: e.tensor_tensor(out=OHj3[:, 0:nt, :], in0=iota_f[:, :].unsqueeze(1).to_broadcast([128, nt, 128]),
                                                                  in1=idxT3[:, 1, t0:t0 + nt].unsqueeze(2).to_broadcast([128, nt, 128]), op=OP.is_equal),
                  reads=[idxT_b] + CR, writes=[OHj_b])
            P.add("dve", lambda e, t0=t0, nt=nt: e.tensor_tensor(out=OHj3[:, 0:nt, :], in0=OHj3[:, 0:nt, :],
                                                                  in1=idxT3[:, 2, t0:t0 + nt].unsqueeze(2).to_broadcast([128, nt, 128]), op=OP.mult),
                  reads=[idxT_b, OHj_b], writes=[OHj_b])
            for tq in range(0, nt, 4):
                bi = 6 + gflip
                gflip = 1 - gflip
                n4 = min(4, nt - tq)
                for q in range(n4):
                    pe_mm(bank[bi][:, q * 128:(q + 1) * 128], OHi3[:, tq + q, :], OHj3[:, tq + q, :], True, True, [OHi_b, OHj_b], [bankB[bi]])
                P.add("act", lambda e, bi=bi, n4=n4, tt=t0 + tq: e.activation(out=Gsb3[:, tt:tt + n4, :], in_=bank[bi][:, 0:n4 * 128].rearrange("p (t j) -> p t j", j=128), func=AF.Copy),
                      reads=[bankB[bi]], writes=[Gsb_b])
        if peer_stage < 4:
            for _ in range(NGRP_UV):
                stream_get()
            return
        nsub = len(subs)
        if pre["next"] is not None:
            nx_key, nx_src, nx_bi = pre["next"]
            dma("sp", xbuf[nx_bi][:], nx_src.rearrange("(s p) d -> p s d", p=128), [castdone], [xbuf_b[nx_bi]])
            pre["loaded"] = nx_key
        def emit_out(j, wd, wd_b, V3, jj, Wb):
            for si, (p0, npn) in enumerate(subs):
                for half in range(2):
                    ob = 2 + si * 2 + half
                    pe_mm(bank[ob][0:npn, :], wd[:, p0:p0 + npn], V3[:, jj, half * 512:(half + 1) * 512], j == 0, j == 127,
                          [wd_b, Wb], [bankB[ob]])

        pending = None
        for gidx in range(NGRP_UV):
            W, Wb = stream_get(1 if pending is not None else 0)
            U4 = W[:, 0:2048].rearrange("p (k j i) -> p k j i", k=8, j=2)
            V3 = W[:, 2048:4096].rearrange("p (j d) -> p j d", j=2)
            for jj in range(2):
                j = gidx * 2 + jj
                bi = j % 2
                ge, ge_b = GE[j % 2]
                wd, wd_b = WD[j % 2]
                ps = bank[bi][:, 0:NT]
                for k in range(8):
                    pe_mm(ps, U4[:, k, jj, :], xnT[:, k, 0:NT], k == 0, k == 7, [Wb, xnT_b], [bankB[bi]])
                P.add("act", lambda e, ps=ps, ge=ge: e.activation(out=ge[:, 0:NT], in_=ps, func=AF.Gelu_apprx_tanh), reads=[bankB[bi]], writes=[ge_b])
                P.add("dve", lambda e, ge=ge, wd=wd, j=j: e.tensor_tensor(out=wd[:, 0:NT], in0=ge[:, 0:NT], in1=Gsb3[:, 0:NT, j], op=OP.mult),
                      reads=[ge_b, Gsb_b], writes=[wd_b])
                if pending is not None:
                    emit_out(*pending)
                pending = (j, wd, wd_b, V3, jj, Wb)
        emit_out(*pending)
        for si, (p0, npn) in enumerate(subs):
            for half in range(2):
                ob = 2 + si * 2 + half
                P.add("dve", lambda e, ob=ob, si=si, npn=npn, half=half: e.tensor_tensor(out=xb[0:npn, si, half * 512:(half + 1) * 512], in0=bank[ob][0:npn, :],
                                                                                         in1=xb[0:npn, si, half * 512:(half + 1) * 512], op=OP.add),
                      reads=[bankB[ob], xb_buf], writes=[xb_buf])

    n_total_tiles = n_seq * n_tiles + (1 if with_sample else 0)
    stream["plan"] = [g for _ in range(n_total_tiles) for g in range(NGRP)]
    stream["total"] = len(stream["plan"])

    ti_global = 0
    if debug_stop == 'setup':
        P.add('sp', None, reads=scr_b + [tbs_b, cstc, cst, consts_b])
        n_seq = 0
        with_sample = False
    for s in range(n_seq):
        P.add("dve", lambda e: e.memset(hstate[:], 0.0), reads=[castdone], writes=[hst_b])
        P.add("dve", lambda e: e.memset(xrT3[:, :, 0:3], 0.0), reads=[castdone], writes=[xrT_b])
        for ti in range(n_tiles):
            tok0 = s * seq_len + ti * T
            last = ti == n_tiles - 1
            kv_out = None
            if seq_len - (ti + 1) * T < 512:
                row0 = s * 512 + (ti * T - (seq_len - 512)) if seq_len >= 512 else None
                kv_out = (kp_o, vp_o, row0)
            if ti > 0:
                P.add("pool", lambda e: e.tensor_copy(out=xrT3[:, :, 0:3], in_=hist_t[:]), reads=[hist_b], writes=[xrT_b])
            ntok = tok0 + T
            pre["cur"] = tok0
            pre["next"] = (ntok, xp[ntok:ntok + T, :], (ti_global + 1) % 2) if (ntok < n_seq * seq_len and do_peer and peer_stage >= 4) else None
            run_tile(xp[tok0:tok0 + T, :], yp[tok0:tok0 + T, :], ti_global % 2, [(0, 128), (128, 128)], 2 * ti, 0,
                     kv_out, convp_o[s] if last else None, lrup_o[s] if last else None, last)
            ti_global += 1
    if with_sample:
        for blk in range(4):
            dma("sp", xbuf[ti_global % 2][:, 0, :], ck[blk * 128:(blk + 1) * 128, :], [], [xbuf_b[ti_global % 2]])
            P.add("dve", lambda e: e.tensor_copy(out=xn[:, :], in_=xbuf[ti_global % 2][:, 0, :]), reads=[xbuf_b[ti_global % 2]], writes=[xn_b])
            tb = bank[7][:, :].bitcast(BF16)
            for k in range(8):
                pe_tr(tb[:, k * 128:(k + 1) * 128], xn[:, k * 128:(k + 1) * 128], ident_b[:, :], [xn_b] + CR, [bankB[7]])
            P.add("act", lambda e, blk=blk: e.activation(out=kT[:, :, blk * 128:(blk + 1) * 128], in_=tb.rearrange("p (k t) -> p k t", k=8), func=AF.Copy),
                  reads=[bankB[7]], writes=[kT_b[blk]])
            dma("sp", xbuf[ti_global % 2][:, 1, :], cv[blk * 128:(blk + 1) * 128, :], [], [xbuf_b[ti_global % 2]])
            P.add("dve", lambda e, blk=blk: e.tensor_copy(out=Vb[:, blk, :], in_=xbuf[ti_global % 2][:, 1, :]), reads=[xbuf_b[ti_global % 2]], writes=[Vb_b[blk]])
        dma("sp", hstate[:, :], slru[:, :], [], [hst_b])
        dma("sp", hist_t[:], sconv[:, :, :], [], [hist_b])
        P.add("pool", lambda e: e.tensor_copy(out=xrT3[:, :, 0:3], in_=hist_t[:]), reads=[hist_b], writes=[xrT_b])
        pre["next"] = None
        pre["cur"] = -1
        run_tile(xs[:, :], ys[:, :], (ti_global + 1) % 2, [(0, 16)], 4, 0, (ks_o, vs_o, 0), convs_o, lrus_o, True)

    P.emit(es)
    es.close()
    return nc


def _slot_images(w_in, w_ba, w_bl, w_o, wq, u, v):
    imgs = np.empty((NGRP, 128, SLOT), np.float32)

    def wgroups(W):
        C = W.shape[1]
        Wr = W.reshape(8, 128, C // 512, 512).transpose(2, 1, 0, 3)
        return Wr.reshape(C // 512, 128, SLOT)

    g = 0
    for W in (w_in, w_ba, w_bl, w_o, wq):
        im = wgroups(W)
        imgs[g:g + im.shape[0]] = im
        g += im.shape[0]
    assert g == NGRP_W
    u4 = u.reshape(128, 64, 2, 8, 128)
    imgs[NGRP_W:, :, 0:2048] = u4.transpose(1, 4, 3, 2, 0).reshape(64, 128, 2048)
    v4 = v.reshape(128, 64, 2, 1024)
    imgs[NGRP_W:, :, 2048:4096] = v4.transpose(1, 0, 2, 3).reshape(64, 128, 2048)
    return imgs.reshape(NGRP * 256, 2048)


def _bias_tables(rel_table):
    ko = np.arange(640)[:, None]
    qq = np.arange(128)[None, :]
    rel = np.clip(512 + qq - ko, -128, 128) + 128
    kc = ko // 64
    cq = qq // 64
    valid = (kc >= cq) & (kc <= cq + 8)
    tb = np.empty((128, 3, 16, 128), np.float32)
    for ti, r in enumerate((0, 3, 4)):
        sl = slice(r * 128, (r + 1) * 128)
        vals = rel_table[:, rel[sl]]
        vals = np.where(valid[sl][None], vals, np.float32(MASKV))
        tb[:, ti] = vals.transpose(1, 0, 2)
    cvec = np.repeat(rel_table[:, 256][:, None], 128, axis=1).reshape(1, 16 * 128)
    return tb.reshape(128, 3 * 16 * 128), np.ascontiguousarray(cvec)


def _fm(vec):
    return np.ascontiguousarray(vec.reshape(8, 128).T)


def _prep_shared(inputs):
    f = lambda k: np.asarray(inputs[k], np.float32)
    sh = {}
    sh["img"] = _slot_images(f("w_in")[0], f("w_branch_attn")[0], f("w_branch_lru")[0], f("w_out")[0], f("peer_wq")[0],
                             f("peer_u")[0], f("peer_v")[0])
    tb, cvec = _bias_tables(f("rel_table")[0])
    sh["tb"], sh["cvec"] = tb, cvec
    pv = np.empty((128, 10, 8), np.float32)
    cw = f("conv_w")[0]
    for i in range(4):
        pv[:, i, :] = _fm(cw[i])
    pv[:, 4, :] = _fm(f("conv_b")[0])
    pv[:, 5, :] = _fm(f("lru_br")[0])
    pv[:, 6, :] = _fm(f("lru_bi")[0])
    pv[:, 7, :] = _fm(f("lru_lambda")[0])
    pv[:, 8, :] = _fm(f("norm_mix")[0])
    pv[:, 9, :] = _fm(f("norm_ffn")[0])
    sh["pvec"] = pv
    sh["g3"] = f("norm_final").reshape(1, D)
    k1 = f("peer_keys1")[0]
    k2 = f("peer_keys2")[0]
    kt = np.stack([k1, k2], 0).transpose(3, 0, 1, 2)
    sh["keysT"] = np.ascontiguousarray(kt).reshape(128, 2 * 8 * 128)
    bd = np.zeros((128, 2, 8, 128), np.float32)
    for s_, key in enumerate(("lru_wr", "lru_wi")):
        w = f(key)[0]
        for c in range(8):
            bd[0:64, s_, c, 0:64] = w[2 * c]
            bd[64:128, s_, c, 64:128] = w[2 * c + 1]
    sh["bd"] = bd.reshape(128, 2 * 8 * 128)
    sh["ident"] = np.eye(128, dtype=np.float32)
    sh["iota"] = np.tile(np.arange(128, dtype=np.float32)[None, :], (128, 1))
    return sh


_NC_CACHE = {}


def kernel(**inputs):
    return _run(inputs, 8)


def _run(inputs, n_cores, **bkw):
    xpr = np.asarray(inputs["x_prompt"], np.float32)
    xsm = np.asarray(inputs["x_sample"], np.float32)
    B, S, _ = xpr.shape
    n_seq = B // n_cores
    sh = _prep_shared(inputs)
    key = (n_seq, S, tuple(sorted(bkw.items())))
    if key not in _NC_CACHE:
        _NC_CACHE[key] = build_nc(n_seq, S, **bkw)
    nc = _NC_CACHE[key]
    ck = np.asarray(inputs["cache_k"], np.float32)[0]
    cv = np.asarray(inputs["cache_v"], np.float32)[0]
    sc = np.asarray(inputs["state_conv"], np.float32)[0]
    sl = np.asarray(inputs["state_lru"], np.float32)[0]
    in_maps = []
    for c in range(n_cores):
        m = dict(sh)
        m["xp"] = np.ascontiguousarray(xpr[c * n_seq:(c + 1) * n_seq].reshape(n_seq * S, D))
        m["xs"] = np.ascontiguousarray(xsm[c])
        m["ck"] = np.ascontiguousarray(ck[c].reshape(512, D))
        m["cv"] = np.ascontiguousarray(cv[c].reshape(512, D))
        m["sconv"] = np.ascontiguousarray(sc[c].reshape(3, 8, 128).transpose(2, 1, 0))
        m["slru"] = _fm(sl[c])
        in_maps.append(m)
    res = run_bass_kernel_spmd(nc, in_maps, core_ids=list(range(n_cores)))
    R = res.results
    y_prompt = np.concatenate([r["yp"].reshape(n_seq, S, D) for r in R], 0)
    y_sample = np.stack([r["ys"] for r in R], 0)
    rows = min(512, S)
    k_prompt = np.concatenate([r["kp"].reshape(n_seq, rows, NH, DH) for r in R], 0)[None]
    v_prompt = np.concatenate([r["vp"].reshape(n_seq, rows, NH, DH) for r in R], 0)[None]
    conv_prompt = np.concatenate([r["convp"].transpose(0, 3, 2, 1).reshape(n_seq, 3, D) for r in R], 0)[None]
    lru_prompt = np.concatenate([r["lrup"].transpose(0, 2, 1).reshape(n_seq, D) for r in R], 0)[None]
    k_sample = np.stack([r["ks"].reshape(16, NH, DH) for r in R], 0)[None]
    v_sample = np.stack([r["vs"].reshape(16, NH, DH) for r in R], 0)[None]
    conv_sample = np.stack([r["convs"].transpose(2, 1, 0).reshape(3, D) for r in R], 0)[None]
    lru_sample = np.stack([r["lrus"].T.reshape(D) for r in R], 0)[None]
    f32 = lambda a: np.ascontiguousarray(a, dtype=np.float32)
    return tuple(f32(a) for a in (y_prompt, y_sample, k_prompt, v_prompt, conv_prompt, lru_prompt,
                                  k_sample, v_sample, conv_sample, lru_sample))
```

```python
import os
import numpy as np
from contextlib import ExitStack
import concourse.bass as bass
import concourse.mybir as mybir
from concourse.bass_utils import run_bass_kernel_spmd

F32 = mybir.dt.float32
BF16 = mybir.dt.bfloat16
U32 = mybir.dt.uint32
AF = mybir.ActivationFunctionType
OP = mybir.AluOpType
AX = mybir.AxisListType

D = 1024
NH = 16
DH = 64
EPS = 1e-6
NGRP_W = 24
NGRP_UV = 64
NGRP = NGRP_W + NGRP_UV
SLOT = 4096
MASKV = -30000.0
CH = 30000
GMUL_ENG = os.environ.get('K_GMUL', 'dve')
MINGAP = int(os.environ.get('K_MINGAP', '1000000000'))


class SemGroup:
    def __init__(self):
        self.sem = None
        self.cnt = 0
        self.last = None


class Buf:
    def __init__(self, name, const=False, sg=None, region=None):
        self.name = name
        self.last_w = None
        self.readers = []
        self.const = const
        self.sg = sg if sg is not None else SemGroup()
        self.region = region
        self.aliases = []


class Op:
    __slots__ = ("eng", "fn", "dma", "sig", "deps", "ord", "sg", "cum", "pos")


class Prog:
    ENGS = ("sp", "act", "dve", "pool", "pe")

    def __init__(self, nc):
        self.nc = nc
        self.ops = []
        self.region_bufs = []
        self.out_dmas = []
        self.eng_n = {}

    def buf(self, name, const=False, sg=None, region=None):
        b = Buf(name, const, sg, region)
        if region is not None:
            for o in self.region_bufs:
                if o.region[0] == region[0] and o.region[1] < region[2] and region[1] < o.region[2]:
                    o.aliases.append(b)
                    b.aliases.append(o)
            self.region_bufs.append(b)
        return b

    def add(self, eng, fn, reads=(), writes=(), dma=False, out=False):
        op = Op()
        op.eng, op.fn, op.dma, op.sig, op.deps, op.ord = eng, fn, dma, False, [], 0
        op.sg, op.cum = None, 0
        self.eng_n[eng] = self.eng_n.get(eng, 0) + 1
        op.pos = self.eng_n[eng]
        strong = {}
        weak = {}
        for b in reads:
            if b.last_w is not None:
                strong[id(b.last_w)] = b.last_w
            for a in b.aliases:
                if a.last_w is not None:
                    strong[id(a.last_w)] = a.last_w
        for b in writes:
            for bb in [b] + b.aliases:
                if bb.last_w is not None:
                    strong[id(bb.last_w)] = bb.last_w
                for r in bb.readers:
                    weak[id(r)] = r
        if dma:
            pb = (list(writes) + list(reads))[0]
            sg = pb.sg
            if sg.last is not None and sg.last is not op:
                strong[id(sg.last)] = sg.last
        for k, d in weak.items():
            if k in strong:
                continue
            strong[k] = d
        best = {}
        for d in strong.values():
            if d is op:
                continue
            key, rank = (("d", id(d.sg)), d.cum) if d.dma else (("e", d.eng), d.pos)
            if key not in best or best[key][0] < rank:
                best[key] = (rank, d)
        for _, d in best.values():
            if (not d.dma) and (not dma) and d.eng == eng and eng == "pe":
                continue
            if (not d.dma) and (not dma) and d.eng == eng and eng in ("dve", "act") and op.pos - d.pos - 1 >= MINGAP:
                continue
            if not d.dma:
                d.sig = True
            op.deps.append(d)
        for b in reads:
            if not b.const:
                b.readers.append(op)
        for b in writes:
            b.last_w = op
            b.readers = []
        if dma:
            sg.cnt += 1
            sg.last = op
            op.sg, op.cum = sg, sg.cnt
            if out:
                self.out_dmas.append(op)
        self.ops.append(op)
        return op

    def emit(self, es):
        nc = self.nc
        cnt = {e: 0 for e in self.ENGS}
        sgs = {}
        for op in self.ops:
            if op.dma:
                sgs[id(op.sg)] = op.sg
            elif op.sig:
                cnt[op.eng] += 1
                op.ord = cnt[op.eng]
        esem = {}
        for e in self.ENGS:
            n = (cnt[e] + CH - 1) // CH
            esem[e] = [es.enter_context(nc.semaphore("se_%s_%d" % (e, i))) for i in range(n)]
        for i, sg in enumerate(sgs.values()):
            sg.sem = es.enter_context(nc.semaphore("sd_%d" % i))
        fin = Op()
        fin.eng, fin.fn, fin.dma, fin.sig, fin.deps, fin.ord = "sp", None, False, False, list(self.out_dmas), 0
        fin.sg, fin.cum, fin.pos = None, 0, 0
        self.ops.append(fin)
        block = es.enter_context(nc.Block())
        ops = self.ops

        def run(ename, eng):
            seen_e = {}
            seen_d = {}
            for op in ops:
                if op.eng != ename:
                    continue
                for d in op.deps:
                    if d.dma:
                        k = id(d.sg)
                        if seen_d.get(k, 0) >= d.cum:
                            continue
                        seen_d[k] = d.cum
                        eng.wait_ge(d.sg.sem, 16 * d.cum)
                    else:
                        if seen_e.get(d.eng, 0) >= d.ord:
                            continue
                        seen_e[d.eng] = d.ord
                        eng.wait_ge(esem[d.eng][(d.ord - 1) // CH], (d.ord - 1) % CH + 1)
                if op.fn is None:
                    continue
                inst = op.fn(eng)
                if op.dma:
                    inst.then_inc(op.sg.sem, 16)
                elif op.sig:
                    inst.then_inc(esem[ename][(op.ord - 1) // CH], 1)

        @block.sync
        def _(e):
            run("sp", e)

        @block.scalar
        def _(e):
            run("act", e)

        @block.vector
        def _(e):
            run("dve", e)

        @block.gpsimd
        def _(e):
            run("pool", e)

        @block.tensor
        def _(e):
            run("pe", e)


def build_nc(n_seq, seq_len, with_sample=True, do_peer=True, debug_stop=None, peer_stage=4):
    T = 256
    n_tiles = seq_len // T
    nc = bass.Bass("TRN2", target_bir_lowering=False)
    es = ExitStack()
    P = Prog(nc)

    def din(name, shape, dt=F32):
        return nc.dram_tensor(name, list(shape), dt, kind="ExternalInput").ap()

    def dout(name, shape, dt=F32):
        return nc.dram_tensor(name, list(shape), dt, kind="ExternalOutput").ap()

    xp = din("xp", [n_seq * seq_len, D])
    xs = din("xs", [16, D])
    ck = din("ck", [512, D])
    cv = din("cv", [512, D])
    sconv = din("sconv", [128, 8, 3])
    slru = din("slru", [128, 8])
    pvec_d = din("pvec", [128, 10, 8])
    g3_d = din("g3", [1, D])
    keysT_d = din("keysT", [128, 2 * 8 * 128])
    bd_d = din("bd", [128, 2 * 8 * 128])
    tb_d = din("tb", [128, 3 * 16 * 128])
    cvec_d = din("cvec", [1, 16 * 128])
    ident_d = din("ident", [128, 128])
    iota_d = din("iota", [128, 128])
    img = din("img", [NGRP * 256, 2048])

    yp = dout("yp", [n_seq * seq_len, D])
    ys = dout("ys", [16, D])
    kp_o = dout("kp", [n_seq * 512, D])
    vp_o = dout("vp", [n_seq * 512, D])
    convp_o = dout("convp", [n_seq, 128, 8, 3])
    lrup_o = dout("lrup", [n_seq, 128, 8])
    ks_o = dout("ks", [16, D])
    vs_o = dout("vs", [16, D])
    convs_o = dout("convs", [128, 8, 3])
    lrus_o = dout("lrus", [128, 8])

    scr = nc.dram_tensor("scr", [NGRP * 256, 2048], BF16, kind="Internal").ap()
    tbs = nc.dram_tensor("tbs", [128, 3 * 16 * 128], BF16, kind="Internal").ap()

    def sb(name, shape, dt):
        return es.enter_context(nc.sbuf_tensor("s_" + name, list(shape), dt))

    ident_f = sb("ident_f", [128, 128], F32)
    ident_b = sb("ident_b", [128, 128], BF16)
    iota_f = sb("iota_f", [128, 128], F32)
    ones_b = sb("ones_b", [128, 2], BF16)
    pv = sb("pv", [128, 10, 8], F32)
    nsp8 = sb("nsp8", [128, 8], F32)
    nsp16 = sb("nsp16", [128, 8], F32)
    g3bc = sb("g3bc", [128, D], F32)
    keysT = sb("keysT", [128, 2, 8, 128], BF16)
    bdb = sb("bdb", [128, 2, 8, 128], BF16)
    cvec = sb("cvec", [1, 16, 128], BF16)
    hstate = sb("hstate", [128, 8], F32)
    scratch1 = sb("scratch1", [128, 4], F32)
    NSLOT = 4
    ring = [sb("ring%d" % i, [128, SLOT], BF16) for i in range(NSLOT)]
    xbuf = [sb("xbuf%d" % i, [128, 2, D], F32) for i in range(2)]
    xnT = sb("xnT", [128, 8, T], BF16)
    kT = sb("kT", [128, 8, 6 * 128], BF16)
    Vb = sb("Vb", [128, 6, D], BF16)
    arA = sb("arA", [128, 16384], F32)
    BW = 12288
    arB = sb("arB", [128, BW], F32)

    bank = [es.enter_context(nc.psum_tensor("bank%d" % i, [128, 512], F32)) for i in range(8)]
    bankB = [P.buf("bank%d" % i) for i in range(8)]

    class V:
        pass

    def viewA(off, n, dt, name):
        ap = arA[:, off:off + n]
        if dt == BF16:
            ap = ap.bitcast(BF16)
        return ap, P.buf(name, region=("A", off, off + n))

    def viewB(off, n, dt, name):
        assert off + n <= BW, (name, off, n)
        ap = arB[:, off:off + n]
        if dt != F32:
            ap = ap.bitcast(dt)
        return ap, P.buf(name, region=("B", off, off + n))

    xc, xc_b = viewA(0, 2048, F32, "xc")
    rr, rr_b = viewA(2048, 2048, F32, "rr")
    gi, gi_b = viewA(4096, 2048, F32, "gi")
    t1, t1_b = viewA(6144, 2048, F32, "t1")
    gg, gg_b = viewA(8192, 2048, F32, "gg")
    sga, sga_b = viewA(10240, 2048, F32, "sga")
    sgr, sgr_b = viewA(12288, 2048, F32, "sgr")
    xcb, xcb_b = viewA(14336, 1024, BF16, "xcb")
    olT, olT_b = viewA(15360, 1024, BF16, "olT")
    Gsb, Gsb_b = viewA(0, 16384, BF16, "Gsb")

    def r3(ap, a):
        return ap.rearrange("p (a b) -> p a b", a=a)

    xc3, rr3, gi3, t13, gg3, sga3, sgr3, xcb3, olT3 = [r3(a, 8) for a in (xc, rr, gi, t1, gg, sga, sgr, xcb, olT)]
    Gsb3 = Gsb.rearrange("p (t j) -> p t j", j=128)

    o = 0
    xn, xn_b = viewB(o, 512, BF16, "xn"); o += 512
    junk, junk_b = viewB(o, 512, BF16, "junk"); o += 512
    qT, qT_b = viewB(o, 1024, BF16, "qT"); o += 1024
    oat, oat_b = viewB(o, 512, BF16, "oat"); o += 512
    oaT, oaT_b = viewB(o, 1024, BF16, "oaT"); o += 1024
    PT0, PT0_b = viewB(o, 256, BF16, "PT0"); o += 256
    PT1, PT1_b = viewB(o, 256, BF16, "PT1"); o += 256
    mT, mT_b = viewB(o, 1024, BF16, "mT"); o += 1024
    xrT, xrT_b = viewB(o, 8 * 260, F32, "xrT"); o += 8 * 260
    kvst, kvst_b = viewB(o, 1024, F32, "kvst"); o += 1024
    rden, rden_b = viewB(o, 16, F32, "rden"); o += 16
    tbb, tbb_b = viewB(o, 3072, BF16, "tbb"); o += 3072
    mixer_B_end = o
    qT3, oaT3, mT3 = r3(qT, 8), r3(oaT, 8), r3(mT, 8)
    xrT3 = xrT.rearrange("p (c t) -> p c t", c=8)
    tbb4 = tbb.rearrange("p (r h q) -> p r h q", r=3, h=16)
    PT = [(PT0, PT0_b), (PT1, PT1_b)]
    o = 1024
    qpT, qpT_b = viewB(o, 2048, BF16, "qpT"); o += 2048
    ssb, ssb_b = viewB(o, 1024, F32, "ssb"); o += 1024; ssb_off = o - 1024
    swork, swork_b = viewB(o, 256, F32, "swork"); o += 256
    v12, v12_b = viewB(o, 256, F32, "v12"); o += 256; v12_off = o - 256
    i12u, i12u_b = viewB(o, 256, U32, "i12u"); o += 256; i12u_off = o - 256
    i12f, i12f_b = viewB(o, 256, F32, "i12f"); o += 256
    cand, cand_b = viewB(o, 2048, F32, "cand"); o += 2048; cand_off = o - 2048
    tsv, tsv_b = viewB(o, 128, F32, "tsv"); o += 128; tsv_off = o - 128
    posu, posu_b = viewB(o, 128, U32, "posu"); o += 128; posu_off = o - 128
    abu, abu_b = viewB(o, 256, U32, "abu"); o += 256
    abf, abf_b = viewB(o, 256, F32, "abf"); o += 256
    eq, eq_b = viewB(o, 1024, BF16, "eq"); o += 1024
    sel, sel_b = viewB(o, 384, F32, "sel"); o += 384
    dd, dd_b = viewB(o, 128, F32, "dd"); o += 128
    zz, zz_b = viewB(o, 16, F32, "zz"); o += 16
    idxT, idxT_b = viewB(o, 768, F32, "idxT"); o += 768
    NOH = 8
    OHi, OHi_b = viewB(o, NOH * 64, BF16, "OHi"); o += NOH * 64
    OHj, OHj_b = viewB(o, NOH * 64, BF16, "OHj"); o += NOH * 64
    ge0, ge0_b = viewB(o, 128, BF16, "ge0"); o += 128
    ge1, ge1_b = viewB(o, 128, BF16, "ge1"); o += 128
    wd0, wd0_b = viewB(o, 128, BF16, "wd0"); o += 128
    wd1, wd1_b = viewB(o, 128, BF16, "wd1"); o += 128
    rstd, rstd_b = viewB(o, 8, F32, "rstd"); o += 8
    ssq, ssq_b = viewB(o, 8, F32, "ssq"); o += 8
    peer_B_end = o
    assert max(mixer_B_end, peer_B_end) <= BW
    qpT3 = r3(qpT, 16)
    ssb3 = r3(ssb, 8)
    v124 = v12.rearrange("p (s h k) -> p s h k", s=2, h=8)
    i12u4 = i12u.rearrange("p (s h k) -> p s h k", s=2, h=8)
    i12f4 = i12f.rearrange("p (s h k) -> p s h k", s=2, h=8)
    cand3 = cand.rearrange("p (h c) -> p h c", h=8)
    cand4 = cand.rearrange("p (h a b) -> p h a b", h=8, a=16)
    tsv3 = r3(tsv, 8)
    posu3 = r3(posu, 8)
    abu3 = r3(abu, 2)
    abf3 = r3(abf, 2)
    eq4 = eq.rearrange("p (h k a) -> p h k a", h=8, k=16)
    sel3 = r3(sel, 3)
    dd3 = r3(dd, 8)
    idxT3 = r3(idxT, 3)
    OHib, OHib_b = viewB(cand_off, NOH * 64, BF16, "OHib")
    OHjb, OHjb_b = viewB(cand_off + NOH * 64, NOH * 64, BF16, "OHjb")
    OHi3b = OHib.rearrange("p (t i) -> p t i", i=128)
    OHj3b = OHjb.rearrange("p (t i) -> p t i", i=128)
    OHi3 = OHi.rearrange("p (t i) -> p t i", i=128)
    OHj3 = OHj.rearrange("p (t i) -> p t i", i=128)
    GE = [(ge0, ge0_b), (ge1, ge1_b)]
    WD = [(wd0, wd0_b), (wd1, wd1_b)]
    def hb(name, off, n):
        return P.buf(name, region=("B", off, off + n))

    ssbh_b = [hb("ssbh%d" % h, ssb_off + h * 128, 128) for h in range(8)]
    v12h_b = [[hb("v12h%d_%d" % (sd, h), v12_off + sd * 128 + h * 16, 16) for h in range(8)] for sd in range(2)]
    i12h_b = [[hb("i12h%d_%d" % (sd, h), i12u_off + sd * 128 + h * 16, 16) for h in range(8)] for sd in range(2)]
    candh_b = [hb("candh%d" % h, cand_off + h * 256, 256) for h in range(8)]
    tsvh_b = [hb("tsvh%d" % h, tsv_off + h * 16, 16) for h in range(8)]
    posh_b = [hb("posh%d" % h, posu_off + h * 16, 16) for h in range(8)]
    rs_t = sb("rs_t", [128, 8], F32)
    rstd, ssq = rs_t[:, 0:4], rs_t[:, 4:8]
    rstd_b, ssq_b = P.buf("rstd"), P.buf("ssq")

    consts_b = P.buf("consts", const=True)
    ring_b = [P.buf("ring%d" % i) for i in range(NSLOT)]
    xbuf_b = [P.buf("xbuf%d" % i) for i in range(2)]
    xnT_b = P.buf("xnT")
    kT_b = [P.buf("kT%d" % i) for i in range(6)]
    Vb_b = [P.buf("Vb%d" % i) for i in range(6)]
    hst_b = P.buf("hstate")
    hist_b = P.buf("xr_hist")
    g3_b = P.buf("g3bc", const=True)
    cast_sg = [SemGroup() for _ in range(6)]
    scr_b = [P.buf("scr%d" % g, sg=cast_sg[g % 6]) for g in range(NGRP)]
    tbs_b = P.buf("tbs", sg=cast_sg[0])
    misc_sg = SemGroup()

    def dma(q, out_ap, in_ap, reads, writes, out=False):
        return P.add(q, lambda e: e.dma_start(out=out_ap, in_=in_ap), reads=reads, writes=writes, dma=True, out=out)

    cst = P.buf("cst_load")
    dma("sp", ident_f[:], ident_d[:, :], [], [cst])
    dma("sp", iota_f[:], iota_d[:, :], [], [cst])
    dma("sp", pv[:], pvec_d[:, :, :], [], [cst])
    dma("sp", g3bc[:], g3_d.partition_broadcast(128), [], [cst])
    P.add("dve", lambda e: e.tensor_copy(out=ident_b[:], in_=ident_f[:]), reads=[cst], writes=[consts_b])
    P.add("dve", lambda e: e.memset(ones_b[:], 1.0), reads=[], writes=[consts_b])
    P.add("act", lambda e: e.activation(out=nsp8[:], in_=pv[:, 7, :], func=AF.Exp, scale=-1.0), reads=[cst], writes=[consts_b])
    P.add("act", lambda e: e.activation(out=nsp16[:], in_=nsp8[:], func=AF.Ln, bias=1.0), reads=[consts_b], writes=[consts_b])
    P.add("dve", lambda e: e.tensor_scalar(out=nsp8[:], in0=nsp16[:], scalar1=-8.0, scalar2=None, op0=OP.mult), reads=[consts_b], writes=[consts_b])
    P.add("dve", lambda e: e.tensor_scalar(out=nsp16[:], in0=nsp16[:], scalar1=-16.0, scalar2=None, op0=OP.mult), reads=[consts_b], writes=[consts_b])
    cstc = P.buf("cst_cast")
    dma("pool", keysT[:].rearrange("p s h n -> p (s h n)"), keysT_d[:, :], [consts_b], [cstc])
    dma("pool", bdb[:].rearrange("p s h n -> p (s h n)"), bd_d[:, :], [consts_b], [cstc])
    dma("pool", cvec[:].rearrange("p h n -> p (h n)"), cvec_d[:, :], [consts_b], [cstc])
    dma("pool", tbs.rearrange("p (a b) -> (p a) b", b=2048), tb_d.rearrange("p (a b) -> (p a) b", b=2048), [], [tbs_b])
    for g in range(NGRP):
        dma("pool", scr[g * 256:(g + 1) * 256, :], img[g * 256:(g + 1) * 256, :], [], [scr_b[g]])
    castdone = P.buf("castdone")
    dma("sp", scratch1[0:1, 0:1], ident_d[0:1, 0:1], scr_b + [tbs_b, cstc, cst, consts_b], [castdone])
    CR = [cst, cstc, consts_b, castdone]

    stream = {"next": 0, "total": 0, "plan": []}

    def stream_issue():
        n = stream["next"]
        if n >= stream["total"]:
            return
        g = stream["plan"][n]
        s = n % NSLOT
        src = scr[g * 256:(g + 1) * 256, :].rearrange("(p two) c -> p (two c)", two=2)
        dma("sp", ring[s][:], src, [scr_b[g]], [ring_b[s]])
        stream["next"] = n + 1

    use = {"n": 0}

    def stream_get(hold=0):
        n = use["n"]
        while stream["next"] < min(n + NSLOT - hold, stream["total"]):
            stream_issue()
        use["n"] = n + 1
        s = n % NSLOT
        return ring[s], ring_b[s]

    mmflip = {"i": 0}

    def mm_bank():
        i = mmflip["i"]
        mmflip["i"] = 1 - i
        return i

    def pe_mm(out_ap, lhsT, rhs, start, stop, reads, writes):
        P.add("pe", lambda e: e.matmul(out_ap, lhsT=lhsT, rhs=rhs, start=start, stop=stop), reads=reads, writes=writes)

    def pe_tr(out_ap, in_ap, ident_ap, reads, writes):
        P.add("pe", lambda e: e.transpose(out_ap, in_ap, ident_ap), reads=reads, writes=writes)

    def rmsnorm_T(xb, xb_buf, subs, gidx, dstT, dstT_buf):
        for si, (p0, npn) in enumerate(subs):
            P.add("act", lambda e, si=si, npn=npn: e.activation(out=junk[0:npn, :], in_=xb[0:npn, si, :], func=AF.Square,
                                                                accum_out=ssq[0:npn, si:si + 1]),
                  reads=[xb_buf], writes=[junk_b, ssq_b])
            P.add("act", lambda e, si=si, npn=npn: e.activation(out=rstd[0:npn, si:si + 1], in_=ssq[0:npn, si:si + 1], func=AF.Sqrt,
                                                                scale=1.0 / D, bias=eps_t[0:npn, 0:1]),
                  reads=[ssq_b] + CR, writes=[rstd_b])
            P.add("dve", lambda e, si=si, npn=npn: e.reciprocal(out=rstd[0:npn, si:si + 1], in_=rstd[0:npn, si:si + 1]),
                  reads=[rstd_b], writes=[rstd_b])
            P.add("dve", lambda e, si=si, npn=npn: e.tensor_scalar(out=xn[0:npn, :], in0=xb[0:npn, si, :], scalar1=rstd[0:npn, si:si + 1],
                                                                   scalar2=None, op0=OP.mult),
                  reads=[xb_buf, rstd_b], writes=[xn_b])
            tbs2 = [bank[7][:, :].bitcast(BF16), bank[6][:, :].bitcast(BF16)]
            for k in range(8):
                pe_tr(tbs2[k % 2][:, k * 128:k * 128 + npn], xn[0:npn, k * 128:(k + 1) * 128], ident_b[0:npn, 0:npn],
                      [xn_b] + CR, [bankB[7 - k % 2]])
            for k in range(8):
                tb = tbs2[k % 2]
                P.add("act" if k % 2 else "dve",
                      (lambda e, k=k, npn=npn, p0=p0, tb=tb: e.activation(out=dstT[:, k, p0:p0 + npn], in_=tb[:, k * 128:k * 128 + npn],
                                                                          func=AF.Copy, scale=pv[:, gidx, k:k + 1])) if k % 2 else
                      (lambda e, k=k, npn=npn, p0=p0, tb=tb: e.tensor_scalar(out=dstT[:, k, p0:p0 + npn], in0=tb[:, k * 128:k * 128 + npn],
                                                                             scalar1=pv[:, gidx, k:k + 1], scalar2=None, op0=OP.mult)),
                      reads=[bankB[7 - k % 2]] + CR, writes=[dstT_buf])

    eps_t = sb("eps_t", [128, 2], F32)
    P.add("dve", lambda e: e.memset(eps_t[:], EPS), reads=[castdone], writes=[consts_b])

    def proj_fm(srcT, srcT_buf, NT, epilogue):
        for half in range(2):
            W, Wb = stream_get()
            W3 = W[:, :].rearrange("p (k c) -> p k c", k=8)
            for cc in range(4):
                c = half * 4 + cc
                bi = mm_bank()
                ps = bank[bi][:, 0:NT]
                for k in range(8):
                    pe_mm(ps, W3[:, k, cc * 128:(cc + 1) * 128], srcT[:, k, 0:NT], k == 0, k == 7,
                          [Wb, srcT_buf], [bankB[bi]])
                epilogue(c, ps, bankB[bi])

    def proj_tm(srcT, srcT_buf, subs, W2, epilogue):
        for si, (p0, npn) in enumerate(subs):
            for half in range(2):
                W, Wb = W2[half]
                W3 = W[:, :].rearrange("p (k c) -> p k c", k=8)
                bi = mm_bank()
                ps = bank[bi][0:npn, :]
                for k in range(8):
                    pe_mm(ps, srcT[:, k, p0:p0 + npn], W3[:, k, :], k == 0, k == 7, [Wb, srcT_buf], [bankB[bi]])
                epilogue(si, half, ps, bankB[bi])

    pre = {"loaded": None, "next": None, "cur": -1}

    def run_tile(xsrc, ydst, xbi, subs, gblk0, first_valid_blk, kv_out, conv_out, lru_out, is_last_of_seq):
        NT = sum(n for _, n in subs)
        xb, xb_buf = xbuf[xbi], xbuf_b[xbi]
        nsub = len(subs)
        if pre["loaded"] != pre["cur"]:
            if nsub == 2:
                dma("sp", xb[:], xsrc.rearrange("(s p) d -> p s d", p=128), [castdone], [xb_buf])
            else:
                dma("sp", xb[0:NT, 0, :], xsrc, [castdone], [xb_buf])
        dma("sp", tbb, tbs[:, :], [tbs_b], [tbb_b])
        rmsnorm_T(xb, xb_buf, subs, 8, xnT, xnT_b)

        def ep_q(c, ps, bb):
            P.add("act", lambda e: e.activation(out=qT3[:, c, 0:NT], in_=ps, func=AF.Copy, scale=0.125), reads=[bb], writes=[qT_b])

        proj_fm(xnT, xnT_b, NT, ep_q)
        kgroups = []
        slots_k = [(gblk0 + i) % 6 for i in range(nsub)]

        def ep_k(c, ps, bb):
            for si, (p0, npn) in enumerate(subs):
                s = slots_k[si]
                P.add("dve", lambda e, s=s, p0=p0, npn=npn: e.tensor_copy(out=kT[:, c, s * 128:s * 128 + npn], in_=ps[:, p0:p0 + npn]),
                      reads=[bb], writes=[kT_b[s]])

        for half in range(2):
            W, Wb = stream_get()
            kgroups.append((W, Wb))
            W3 = W[:, :].rearrange("p (k c) -> p k c", k=8)
            for cc in range(4):
                c = half * 4 + cc
                bi = mm_bank()
                ps = bank[bi][:, 0:NT]
                for k in range(8):
                    pe_mm(ps, W3[:, k, cc * 128:(cc + 1) * 128], xnT[:, k, 0:NT], k == 0, k == 7, [Wb, xnT_b], [bankB[bi]])
                ep_k(c, ps, bankB[bi])
            if kv_out is not None:
                for si, (p0, npn) in enumerate(subs):
                    bi = mm_bank()
                    ps = bank[bi][0:npn, :]
                    for k in range(8):
                        pe_mm(ps, xnT[:, k, p0:p0 + npn], W3[:, k, :], k == 0, k == 7, [Wb, xnT_b], [bankB[bi]])
                    P.add("act", lambda e, ps=ps, npn=npn: e.activation(out=kvst[0:npn, 0:512], in_=ps, func=AF.Copy), reads=[bankB[bi]], writes=[kvst_b])
                    dst = kv_out[0][kv_out[2] + p0:kv_out[2] + p0 + npn, half * 512:(half + 1) * 512]
                    dma("sp", dst, kvst[0:npn, 0:512], [kvst_b], [], out=True)
        vg = [stream_get(), stream_get(1)]

        def ep_v(si, half, ps, bb):
            p0, npn = subs[si]
            s = slots_k[si]
            P.add("act", lambda e: e.activation(out=Vb[0:npn, s, half * 512:(half + 1) * 512], in_=ps, func=AF.Copy), reads=[bb], writes=[Vb_b[s]])
            if kv_out is not None:
                P.add("act", lambda e: e.activation(out=kvst[0:npn, 512:1024], in_=ps, func=AF.Copy), reads=[bb], writes=[kvst_b])
                dst = kv_out[1][kv_out[2] + p0:kv_out[2] + p0 + npn, half * 512:(half + 1) * 512]
                dma("sp", dst, kvst[0:npn, 512:1024], [kvst_b], [], out=True)

        proj_tm(xnT, xnT_b, subs, vg, ep_v)

        def ep_xr(c, ps, bb):
            P.add("dve", lambda e: e.tensor_copy(out=xrT3[:, c, 3:3 + NT], in_=ps), reads=[bb], writes=[xrT_b])

        proj_fm(xnT, xnT_b, NT, ep_xr)

        def ep_g(c, ps, bb):
            P.add("act", lambda e: e.activation(out=gg3[:, c, 0:NT], in_=ps, func=AF.Gelu_apprx_tanh), reads=[bb], writes=[gg_b])

        proj_fm(xnT, xnT_b, NT, ep_g)

        def ep_ga(c, ps, bb):
            P.add("act", lambda e: e.activation(out=sga3[:, c, 0:NT], in_=ps, func=AF.Sigmoid), reads=[bb], writes=[sga_b])

        proj_fm(xnT, xnT_b, NT, ep_ga)

        def ep_gr(c, ps, bb):
            P.add("act", lambda e: e.activation(out=sgr3[:, c, 0:NT], in_=ps, func=AF.Sigmoid), reads=[bb], writes=[sgr_b])

        proj_fm(xnT, xnT_b, NT, ep_gr)

        for c in range(8):
            P.add("dve", lambda e, c=c: e.tensor_scalar(out=xc3[:, c, 0:NT], in0=xrT3[:, c, 3:3 + NT], scalar1=pv[:, 3, c:c + 1],
                                                        scalar2=pv[:, 4, c:c + 1], op0=OP.mult, op1=OP.add),
                  reads=[xrT_b] + CR, writes=[xc_b])
            for i in range(3):
                P.add("dve", lambda e, c=c, i=i: e.scalar_tensor_tensor(out=xc3[:, c, 0:NT], in0=xrT3[:, c, i:i + NT], scalar=pv[:, i, c:c + 1],
                                                                        in1=xc3[:, c, 0:NT], op0=OP.mult, op1=OP.add),
                      reads=[xrT_b, xc_b] + CR, writes=[xc_b])
        if conv_out is not None:
            P.add("act", lambda e: e.activation(out=kvst[:, 0:24].rearrange("p (c t) -> p c t", c=8), in_=xrT3[:, :, NT:NT + 3], func=AF.Copy),
                  reads=[xrT_b], writes=[kvst_b])
            dma("sp", conv_out, kvst[:, 0:24].rearrange("p (c t) -> p c t", c=8), [kvst_b], [], out=True)
        if not is_last_of_seq:
            P.add("pool", lambda e: e.tensor_copy(out=hist_t[:], in_=xrT3[:, :, NT:NT + 3]), reads=[xrT_b], writes=[hist_b])
        P.add("act", lambda e: e.activation(out=xcb3[:, :, 0:NT], in_=xc3[:, :, 0:NT], func=AF.Copy), reads=[xc_b], writes=[xcb_b])
        for c in range(8):
            bi = mm_bank()
            pe_mm(bank[bi][:, 0:NT], bdb[:, 0, c, :], xcb3[:, c, 0:NT], True, True, [xcb_b] + CR, [bankB[bi]])
            pe_mm(bank[bi][:, 256:256 + NT], bdb[:, 1, c, :], xcb3[:, c, 0:NT], True, True, [xcb_b] + CR, [bankB[bi]])
            P.add("act", lambda e, c=c, bi=bi: e.activation(out=rr3[:, c, 0:NT], in_=bank[bi][:, 0:NT], func=AF.Sigmoid, bias=pv[:, 5, c:c + 1]),
                  reads=[bankB[bi]] + CR, writes=[rr_b])
            P.add("act", lambda e, c=c, bi=bi: e.activation(out=gi3[:, c, 0:NT], in_=bank[bi][:, 256:256 + NT], func=AF.Sigmoid, bias=pv[:, 6, c:c + 1]),
                  reads=[bankB[bi]] + CR, writes=[gi_b])
        for c in range(8):
            P.add("act", lambda e, c=c: e.activation(out=t13[:, c, 0:NT], in_=rr3[:, c, 0:NT], func=AF.Exp, scale=nsp16[:, c:c + 1]),
                  reads=[rr_b] + CR, writes=[t1_b])
        for c in range(8):
            P.add("act", lambda e, c=c: e.activation(out=rr3[:, c, 0:NT], in_=rr3[:, c, 0:NT], func=AF.Exp, scale=nsp8[:, c:c + 1]),
                  reads=[rr_b, t1_b] + CR, writes=[rr_b])
        P.add("dve", lambda e: e.tensor_scalar(out=t13[:, :, 0:NT], in0=t13[:, :, 0:NT], scalar1=-1.0, scalar2=1.0, op0=OP.mult, op1=OP.add),
              reads=[t1_b], writes=[t1_b])
        P.add("dve", lambda e: e.tensor_scalar(out=t13[:, :, 0:NT], in0=t13[:, :, 0:NT], scalar1=1e-30, scalar2=None, op0=OP.max),
              reads=[t1_b], writes=[t1_b])
        P.add("act", lambda e: e.activation(out=t13[:, :, 0:NT], in_=t13[:, :, 0:NT], func=AF.Sqrt), reads=[t1_b], writes=[t1_b])
        P.add("dve", lambda e: e.tensor_tensor(out=t13[:, :, 0:NT], in0=t13[:, :, 0:NT], in1=gi3[:, :, 0:NT], op=OP.mult), reads=[t1_b, gi_b], writes=[t1_b])
        P.add("dve", lambda e: e.tensor_tensor(out=t13[:, :, 0:NT], in0=t13[:, :, 0:NT], in1=xc3[:, :, 0:NT], op=OP.mult), reads=[t1_b, xc_b], writes=[t1_b])
        for c in range(8):
            P.add("dve", lambda e, c=c: e.tensor_tensor_scan(out=gi3[:, c, 0:NT], data0=rr3[:, c, 0:NT], data1=t13[:, c, 0:NT],
                                                             initial=hstate[:, c:c + 1], op0=OP.mult, op1=OP.add),
                  reads=[rr_b, t1_b, hst_b, gi_b], writes=[gi_b])
        P.add("dve", lambda e: e.tensor_copy(out=hstate[:, :], in_=gi3[:, :, NT - 1]), reads=[gi_b], writes=[hst_b])
        if lru_out is not None:
            dma("sp", lru_out, hstate[:, :], [hst_b], [], out=True)
        P.add("dve", lambda e: e.tensor_tensor(out=olT3[:, :, 0:NT], in0=gi3[:, :, 0:NT], in1=gg3[:, :, 0:NT], op=OP.mult), reads=[gi_b, gg_b], writes=[olT_b])

        expn = {"n": 0}
        for qi, (p0, npn) in enumerate(subs):
            gq = gblk0 + qi
            blocks = []
            for r in range(5):
                gb = gq - 4 + r
                if gb < first_valid_blk:
                    continue
                nk = npn if r == 4 else 128
                blocks.append((r, gb % 6, nk))
            items = [(h, r, s, nk) for h in range(NH) for (r, s, nk) in blocks]
            groups = [items[i:i + 4] for i in range(0, len(items), 4)]
            firstr = blocks[0][0]
            lastr = blocks[-1][0]
            def emit_pv(grp, pt, pt_b):
                for gi_, (h, r, s, nk) in enumerate(grp):
                    ob = 4 + h // 8
                    pe_mm(bank[ob][0:npn, (h % 8) * 64:(h % 8 + 1) * 64], pt[0:nk, gi_ * npn:(gi_ + 1) * npn], Vb[0:nk, s, h * 64:(h + 1) * 64],
                          r == firstr, r == lastr, [pt_b, Vb_b[s]], [bankB[ob]])
                    pe_mm(bank[6][0:npn, h:h + 1], pt[0:nk, gi_ * npn:(gi_ + 1) * npn], ones_b[0:nk, 0:1],
                          r == firstr, r == lastr, [pt_b] + CR, [bankB[6]])

            pend = None
            for grp in groups:
                bi = 2 + (expn["n"] % 2)
                pt, pt_b = PT[expn["n"] % 2]
                expn["n"] += 1
                for gi_, (h, r, s, nk) in enumerate(grp):
                    c, base = h // 2, (h % 2) * 64
                    out_ap = bank[bi][0:nk, gi_ * npn:(gi_ + 1) * npn]
                    pe_mm(out_ap, kT[base:base + 64, c, s * 128:s * 128 + nk], qT3[base:base + 64, c, p0:p0 + npn], True, False,
                          [kT_b[s], qT_b], [bankB[bi]])
                    if r in (0, 3, 4):
                        ti = {0: 0, 3: 1, 4: 2}[r]
                        pe_mm(out_ap, ident_b[:, 0:nk], tbb4[:, ti, h, 0:npn], False, True, [tbb_b] + CR, [bankB[bi]])
                    else:
                        pe_mm(out_ap, onesrow[0:1, 0:nk], cvec[0:1, h, 0:npn], False, True, CR, [bankB[bi]])
                ng = len(grp)
                nkmax = max(nk for (_, _, _, nk) in grp)
                P.add("act", lambda e, bi=bi, pt=pt, ng=ng, nkmax=nkmax, npn=npn: e.activation(out=pt[0:nkmax, 0:ng * npn], in_=bank[bi][0:nkmax, 0:ng * npn], func=AF.Exp),
                      reads=[bankB[bi]], writes=[pt_b])
                if pend is not None:
                    emit_pv(*pend)
                pend = (grp, pt, pt_b)
            emit_pv(*pend)
            P.add("dve", lambda e, npn=npn: e.reciprocal(out=rden[0:npn, 0:16], in_=bank[6][0:npn, 0:16]), reads=[bankB[6]], writes=[rden_b])
            for hb in range(2):
                P.add("dve", lambda e, npn=npn, hb=hb: e.tensor_tensor(
                    out=oat[0:npn, hb * 512:(hb + 1) * 512].rearrange("p (h d) -> p h d", h=8),
                    in0=bank[4 + hb][0:npn, :].rearrange("p (h d) -> p h d", h=8),
                    in1=rden[0:npn, hb * 8:(hb + 1) * 8].unsqueeze(2).to_broadcast([npn, 8, 64]), op=OP.mult),
                    reads=[bankB[4 + hb], rden_b], writes=[oat_b])
            tb = bank[7][:, :].bitcast(BF16)
            for k in range(8):
                pe_tr(tb[:, k * 128:k * 128 + npn], oat[0:npn, k * 128:(k + 1) * 128], ident_b[0:npn, 0:npn], [oat_b] + CR, [bankB[7]])
            P.add("act", lambda e, npn=npn, p0=p0: e.activation(out=oaT3[:, :, p0:p0 + npn], in_=tb.rearrange("p (k t) -> p k t", k=8)[:, :, 0:npn], func=AF.Copy),
                  reads=[bankB[7]], writes=[oaT_b])

        def ep_a(c, ps, bb):
            P.add("dve", lambda e: e.tensor_tensor(out=sga3[:, c, 0:NT], in0=ps, in1=sga3[:, c, 0:NT], op=OP.mult), reads=[bb, sga_b], writes=[sga_b])

        proj_fm(oaT3, oaT_b, NT, ep_a)

        def ep_l(c, ps, bb):
            P.add("dve", lambda e: e.tensor_tensor(out=sgr3[:, c, 0:NT], in0=ps, in1=sgr3[:, c, 0:NT], op=OP.mult), reads=[bb, sgr_b], writes=[sgr_b])

        proj_fm(olT3, olT_b, NT, ep_l)
        P.add("dve", lambda e: e.tensor_tensor(out=mT3[:, :, 0:NT], in0=sga3[:, :, 0:NT], in1=sgr3[:, :, 0:NT], op=OP.add), reads=[sga_b, sgr_b], writes=[mT_b])
        wo = [stream_get(), stream_get(1)]

        def ep_o(si, half, ps, bb):
            p0, npn = subs[si]
            P.add("dve", lambda e: e.tensor_tensor(out=xb[0:npn, si, half * 512:(half + 1) * 512], in0=ps, in1=xb[0:npn, si, half * 512:(half + 1) * 512], op=OP.add),
                  reads=[bb, xb_buf], writes=[xb_buf])

        proj_tm(mT3, mT_b, subs, wo, ep_o)

        if do_peer:
            peer(xb, xb_buf, subs, NT)
        else:
            for _ in range(4 + NGRP_UV):
                stream_get()

        for si, (p0, npn) in enumerate(subs):
            P.add("act", lambda e, si=si, npn=npn: e.activation(out=junk[0:npn, :], in_=xb[0:npn, si, :], func=AF.Square, accum_out=ssq[0:npn, si:si + 1]),
                  reads=[xb_buf], writes=[junk_b, ssq_b])
            P.add("act", lambda e, si=si, npn=npn: e.activation(out=rstd[0:npn, si:si + 1], in_=ssq[0:npn, si:si + 1], func=AF.Sqrt, scale=1.0 / D, bias=eps_t[0:npn, 0:1]),
                  reads=[ssq_b] + CR, writes=[rstd_b])
            P.add("dve", lambda e, si=si, npn=npn: e.reciprocal(out=rstd[0:npn, si:si + 1], in_=rstd[0:npn, si:si + 1]), reads=[rstd_b], writes=[rstd_b])
            P.add("dve", lambda e, si=si, npn=npn: e.scalar_tensor_tensor(out=xb[0:npn, si, :], in0=xb[0:npn, si, :], scalar=rstd[0:npn, si:si + 1], in1=g3bc[0:npn, :],
                                                                          op0=OP.mult, op1=OP.mult),
                  reads=[xb_buf, rstd_b] + CR, writes=[xb_buf])
        if nsub == 2:
            dma("sp", ydst.rearrange("(s p) d -> p s d", p=128), xb[:], [xb_buf], [], out=True)
        else:
            dma("sp", ydst, xb[0:NT, 0, :], [xb_buf], [], out=True)

    hist_t = sb("hist_t", [128, 8, 3], F32)
    onesrow = sb("onesrow", [1, 128], BF16)
    P.add("dve", lambda e: e.memset(onesrow[:], 1.0), reads=[castdone], writes=[consts_b])

    def peer(xb, xb_buf, subs, NT):
        rmsnorm_T(xb, xb_buf, subs, 9, xnT, xnT_b)
        cbase = {"c": 0}

        def ep_qp(c, ps, bb):
            cc = cbase["c"] + c
            P.add("act" if c % 2 else "dve",
                  (lambda e: e.activation(out=qpT3[:, cc, 0:NT], in_=ps, func=AF.Copy)) if c % 2 else
                  (lambda e: e.tensor_copy(out=qpT3[:, cc, 0:NT], in_=ps)), reads=[bb], writes=[qpT_b])

        proj_fm(xnT, xnT_b, NT, ep_qp)
        cbase["c"] = 8
        proj_fm(xnT, xnT_b, NT, ep_qp)
        if peer_stage < 2:
            for _ in range(NGRP_UV):
                stream_get()
            return
        for si, (p0, npn) in enumerate(subs):
            for side in range(2):
                for hh in range(2):
                    bi = mm_bank()
                    for h4 in range(4):
                        h = hh * 4 + h4
                        pe_mm(bank[bi][0:npn, h4 * 128:(h4 + 1) * 128], qpT3[:, 2 * h + side, p0:p0 + npn], keysT[:, side, h, :], True, True,
                              [qpT_b] + CR, [bankB[bi]])
                    P.add("act", lambda e, bi=bi, hh=hh, npn=npn: e.activation(out=ssb[0:npn, hh * 512:(hh + 1) * 512], in_=bank[bi][0:npn, :], func=AF.Copy),
                          reads=[bankB[bi]], writes=ssbh_b[hh * 4:(hh + 1) * 4])
                for lo in (0, 8):
                    for h in range(8):
                        P.add("dve", lambda e, h=h, npn=npn, side=side, lo=lo: e.max(out=v124[0:npn, side, h, lo:lo + 8], in_=ssb3[0:npn, h, :]),
                              reads=[ssbh_b[h]], writes=[v12h_b[side][h]])
                    for h in range(8):
                        P.add("dve", lambda e, h=h, npn=npn, side=side, lo=lo: e.max_index(out=i12u4[0:npn, side, h, lo:lo + 8], in_max=v124[0:npn, side, h, lo:lo + 8], in_values=ssb3[0:npn, h, :]),
                              reads=[ssbh_b[h], v12h_b[side][h]], writes=[i12h_b[side][h]])
                    if lo == 0:
                        for h in range(8):
                            P.add("dve", lambda e, h=h, npn=npn, side=side: e.match_replace(out=ssb3[0:npn, h, :], in_to_replace=v124[0:npn, side, h, 0:8], in_values=ssb3[0:npn, h, :], imm_value=-1e30),
                                  reads=[ssbh_b[h], v12h_b[side][h]], writes=[ssbh_b[h]])
            allv = [b for sd in v12h_b for b in sd]
            alli = [b for sd in i12h_b for b in sd]
            P.add("dve", lambda e, npn=npn: e.tensor_copy(out=i12f[0:npn, :], in_=i12u[0:npn, :]), reads=alli, writes=[i12f_b])
            P.add("dve", lambda e, npn=npn: e.tensor_tensor(out=cand4[0:npn], in0=v124[0:npn, 0].unsqueeze(3).to_broadcast([npn, 8, 16, 16]),
                                                            in1=v124[0:npn, 1].unsqueeze(2).to_broadcast([npn, 8, 16, 16]), op=OP.add),
                  reads=allv, writes=candh_b)
            for lo in (0, 8):
                for h in range(8):
                    P.add("dve", lambda e, h=h, npn=npn, lo=lo: e.max(out=tsv3[0:npn, h, lo:lo + 8], in_=cand3[0:npn, h, :]), reads=[candh_b[h]], writes=[tsvh_b[h]])
                for h in range(8):
                    P.add("dve", lambda e, h=h, npn=npn, lo=lo: e.max_index(out=posu3[0:npn, h, lo:lo + 8], in_max=tsv3[0:npn, h, lo:lo + 8], in_values=cand3[0:npn, h, :]),
                          reads=[candh_b[h], tsvh_b[h]], writes=[posh_b[h]])
                if lo == 0:
                    for h in range(8):
                        P.add("dve", lambda e, h=h, npn=npn: e.match_replace(out=cand3[0:npn, h, :], in_to_replace=tsv3[0:npn, h, 0:8], in_values=cand3[0:npn, h, :], imm_value=-1e30),
                              reads=[candh_b[h], tsvh_b[h]], writes=[candh_b[h]])
            P.add("dve", lambda e, npn=npn: e.tensor_scalar(out=abu3[0:npn, 0, :], in0=posu[0:npn, :], scalar1=4, scalar2=None, op0=OP.logical_shift_right),
                  reads=posh_b, writes=[abu_b])
            P.add("dve", lambda e, npn=npn: e.tensor_scalar(out=abu3[0:npn, 1, :], in0=posu[0:npn, :], scalar1=15, scalar2=None, op0=OP.bitwise_and),
                  reads=posh_b, writes=[abu_b])
            P.add("dve", lambda e, npn=npn: e.tensor_copy(out=abf[0:npn, :], in_=abu[0:npn, :]), reads=[abu_b], writes=[abf_b])
            for side in range(2):
                P.add("dve", lambda e, npn=npn, side=side: e.tensor_tensor(
                    out=eq4[0:npn], in0=iota_f[0:npn, 0:16].unsqueeze(1).unsqueeze(1).to_broadcast([npn, 8, 16, 16]),
                    in1=abf3[0:npn, side, :].rearrange("p (h k) -> p h k", h=8).unsqueeze(3).to_broadcast([npn, 8, 16, 16]), op=OP.is_equal),
                    reads=[abf_b] + CR, writes=[eq_b])
                P.add("dve", lambda e, npn=npn, side=side: e.tensor_tensor(
                    out=eq4[0:npn], in0=eq4[0:npn], in1=i12f4[0:npn, side].unsqueeze(2).to_broadcast([npn, 8, 16, 16]), op=OP.mult),
                    reads=[eq_b, i12f_b], writes=[eq_b])
                P.add("dve", lambda e, npn=npn, side=side: e.tensor_reduce(out=sel3[0:npn, side, :].rearrange("p (h k) -> p h k", h=8), in_=eq4[0:npn], axis=AX.X, op=OP.add),
                      reads=[eq_b], writes=[sel_b])
            P.add("dve", lambda e, npn=npn: e.tensor_tensor(out=dd3[0:npn], in0=tsv3[0:npn], in1=tsv3[0:npn, :, 0:1].to_broadcast([npn, 8, 16]), op=OP.subtract),
                  reads=tsvh_b, writes=[dd_b])
            P.add("act", lambda e, npn=npn: e.activation(out=dd[0:npn, :], in_=dd[0:npn, :], func=AF.Exp), reads=[dd_b], writes=[dd_b])
            P.add("dve", lambda e, npn=npn: e.tensor_reduce(out=zz[0:npn, 0:8], in_=dd3[0:npn], axis=AX.X, op=OP.add), reads=[dd_b], writes=[zz_b])
            P.add("dve", lambda e, npn=npn: e.reciprocal(out=zz[0:npn, 8:16], in_=zz[0:npn, 0:8]), reads=[zz_b], writes=[zz_b])
            P.add("dve", lambda e, npn=npn: e.tensor_tensor(out=sel3[0:npn, 2, :].rearrange("p (h k) -> p h k", h=8), in0=dd3[0:npn],
                                                            in1=zz[0:npn, 8:16].unsqueeze(2).to_broadcast([npn, 8, 16]), op=OP.mult),
                  reads=[dd_b, zz_b], writes=[sel_b])
            for q in range(3):
                pe_tr(bank[6][:, q * 128:q * 128 + npn], sel3[0:npn, q, :], ident_f[0:npn, 0:npn], [sel_b] + CR, [bankB[6]])
            P.add("act", lambda e, npn=npn, p0=p0: e.activation(out=idxT3[:, :, p0:p0 + npn], in_=bank[6][:, 0:384].rearrange("p (q t) -> p q t", q=3)[:, :, 0:npn], func=AF.Copy),
                  reads=[bankB[6]], writes=[idxT_b])
        if peer_stage < 3:
            for _ in range(NGRP_UV):
                stream_get()
            return
        gflip = 0
        OHS = [(OHi3, OHi_b, OHj3, OHj_b), (OHi3b, OHib_b, OHj3b, OHjb_b)]
        for ci, t0 in enumerate(range(0, NT, NOH)):
            nt = min(NOH, NT - t0)
            oi3, oi_b, oj3, oj_b = OHS[ci % 2]
            P.add("dve", lambda e, t0=t0, nt=nt, oi3=oi3: e.tensor_tensor(out=oi3[:, 0:nt, :], in0=iota_f[:, :].unsqueeze(1).to_broadcast([128, nt, 128]),
                                                                            in1=idxT3[:, 0, t0:t0 + nt].unsqueeze(2).to_broadcast([128, nt, 128]), op=OP.is_equal),
                  reads=[idxT_b] + CR, writes=[oi_b])
            P.add("dve", lambda e, t0=t0, nt=nt, oj3=oj3: e.tensor_tensor(out=oj3[:, 0:nt, :], in0=iota_f[:, :].unsqueeze(1).to_broadcast([128, nt, 128]),
                                                                            in1=idxT3[:, 1, t0:t0 + nt].unsqueeze(2).to_broadcast([128, nt, 128]), op=OP.is_equal),
                  reads=[idxT_b] + CR, writes=[oj_b])
            P.add(GMUL_ENG, lambda e, t0=t0, nt=nt, oj3=oj3: e.tensor_tensor(out=oj3[:, 0:nt, :], in0=oj3[:, 0:nt, :],
                                                                               in1=idxT3[:, 2, t0:t0 + nt].unsqueeze(2).to_broadcast([128, nt, 128]), op=OP.mult),
                  reads=[idxT_b, oj_b], writes=[oj_b])
            for tq in range(0, nt, 4):
                bi = 6 + gflip
                gflip = 1 - gflip
                n4 = min(4, nt - tq)
                for q in range(n4):
                    pe_mm(bank[bi][:, q * 128:(q + 1) * 128], oi3[:, tq + q, :], oj3[:, tq + q, :], True, True, [oi_b, oj_b], [bankB[bi]])
                P.add("act", lambda e, bi=bi, n4=n4, tt=t0 + tq: e.activation(out=Gsb3[:, tt:tt + n4, :], in_=bank[bi][:, 0:n4 * 128].rearrange("p (t j) -> p t j", j=128), func=AF.Copy),
                      reads=[bankB[bi]], writes=[Gsb_b])
        if peer_stage < 4:
            for _ in range(NGRP_UV):
                stream_get()
            return
        nsub = len(subs)
        if pre["next"] is not None:
            nx_key, nx_src, nx_bi = pre["next"]
            dma("sp", xbuf[nx_bi][:], nx_src.rearrange("(s p) d -> p s d", p=128), [castdone], [xbuf_b[nx_bi]])
            pre["loaded"] = nx_key
        def emit_out(j, wd, wd_b, V3, jj, Wb):
            for si, (p0, npn) in enumerate(subs):
                for half in range(2):
                    ob = 2 + si * 2 + half
                    pe_mm(bank[ob][0:npn, :], wd[:, p0:p0 + npn], V3[:, jj, half * 512:(half + 1) * 512], j == 0, j == 127,
                          [wd_b, Wb], [bankB[ob]])

        pending = None
        for gidx in range(NGRP_UV):
            W, Wb = stream_get(1 if pending is not None else 0)
            U4 = W[:, 0:2048].rearrange("p (k j i) -> p k j i", k=8, j=2)
            V3 = W[:, 2048:4096].rearrange("p (j d) -> p j d", j=2)
            for jj in range(2):
                j = gidx * 2 + jj
                bi = j % 2
                ge, ge_b = GE[j % 2]
                wd, wd_b = WD[j % 2]
                ps = bank[bi][:, 0:NT]
                for k in range(8):
                    pe_mm(ps, U4[:, k, jj, :], xnT[:, k, 0:NT], k == 0, k == 7, [Wb, xnT_b], [bankB[bi]])
                P.add("act", lambda e, ps=ps, ge=ge: e.activation(out=ge[:, 0:NT], in_=ps, func=AF.Gelu_apprx_tanh), reads=[bankB[bi]], writes=[ge_b])
                P.add("dve", lambda e, ge=ge, wd=wd, j=j: e.tensor_tensor(out=wd[:, 0:NT], in0=ge[:, 0:NT], in1=Gsb3[:, 0:NT, j], op=OP.mult),
                      reads=[ge_b, Gsb_b], writes=[wd_b])
                if pending is not None:
                    emit_out(*pending)
                pending = (j, wd, wd_b, V3, jj, Wb)
        emit_out(*pending)
        for si, (p0, npn) in enumerate(subs):
            for half in range(2):
                ob = 2 + si * 2 + half
                P.add("dve", lambda e, ob=ob, si=si, npn=npn, half=half: e.tensor_tensor(out=xb[0:npn, si, half * 512:(half + 1) * 512], in0=bank[ob][0:npn, :],
                                                                                         in1=xb[0:npn, si, half * 512:(half + 1) * 512], op=OP.add),
                      reads=[bankB[ob], xb_buf], writes=[xb_buf])

    n_total_tiles = n_seq * n_tiles + (1 if with_sample else 0)
    stream["plan"] = [g for _ in range(n_total_tiles) for g in range(NGRP)]
    stream["total"] = len(stream["plan"])

    ti_global = 0
    if debug_stop == 'setup':
        P.add('sp', None, reads=scr_b + [tbs_b, cstc, cst, consts_b])
        n_seq = 0
        with_sample = False
    for s in range(n_seq):
        P.add("dve", lambda e: e.memset(hstate[:], 0.0), reads=[castdone], writes=[hst_b])
        P.add("dve", lambda e: e.memset(xrT3[:, :, 0:3], 0.0), reads=[castdone], writes=[xrT_b])
        for ti in range(n_tiles):
            tok0 = s * seq_len + ti * T
            last = ti == n_tiles - 1
            kv_out = None
            if seq_len - (ti + 1) * T < 512:
                row0 = s * 512 + (ti * T - (seq_len - 512)) if seq_len >= 512 else None
                kv_out = (kp_o, vp_o, row0)
            if ti > 0:
                P.add("pool", lambda e: e.tensor_copy(out=xrT3[:, :, 0:3], in_=hist_t[:]), reads=[hist_b], writes=[xrT_b])
            ntok = tok0 + T
            pre["cur"] = tok0
            pre["next"] = (ntok, xp[ntok:ntok + T, :], (ti_global + 1) % 2) if (ntok < n_seq * seq_len and do_peer and peer_stage >= 4) else None
            run_tile(xp[tok0:tok0 + T, :], yp[tok0:tok0 + T, :], ti_global % 2, [(0, 128), (128, 128)], 2 * ti, 0,
                     kv_out, convp_o[s] if last else None, lrup_o[s] if last else None, last)
            ti_global += 1
    if with_sample:
        for blk in range(4):
            dma("sp", xbuf[ti_global % 2][:, 0, :], ck[blk * 128:(blk + 1) * 128, :], [], [xbuf_b[ti_global % 2]])
            P.add("dve", lambda e: e.tensor_copy(out=xn[:, :], in_=xbuf[ti_global % 2][:, 0, :]), reads=[xbuf_b[ti_global % 2]], writes=[xn_b])
            tb = bank[7][:, :].bitcast(BF16)
            for k in range(8):
                pe_tr(tb[:, k * 128:(k + 1) * 128], xn[:, k * 128:(k + 1) * 128], ident_b[:, :], [xn_b] + CR, [bankB[7]])
            P.add("act", lambda e, blk=blk: e.activation(out=kT[:, :, blk * 128:(blk + 1) * 128], in_=tb.rearrange("p (k t) -> p k t", k=8), func=AF.Copy),
                  reads=[bankB[7]], writes=[kT_b[blk]])
            dma("sp", xbuf[ti_global % 2][:, 1, :], cv[blk * 128:(blk + 1) * 128, :], [], [xbuf_b[ti_global % 2]])
            P.add("dve", lambda e, blk=blk: e.tensor_copy(out=Vb[:, blk, :], in_=xbuf[ti_global % 2][:, 1, :]), reads=[xbuf_b[ti_global % 2]], writes=[Vb_b[blk]])
        dma("sp", hstate[:, :], slru[:, :], [], [hst_b])
        dma("sp", hist_t[:], sconv[:, :, :], [], [hist_b])
        P.add("pool", lambda e: e.tensor_copy(out=xrT3[:, :, 0:3], in_=hist_t[:]), reads=[hist_b], writes=[xrT_b])
        pre["next"] = None
        pre["cur"] = -1
        run_tile(xs[:, :], ys[:, :], (ti_global + 1) % 2, [(0, 16)], 4, 0, (ks_o, vs_o, 0), convs_o, lrus_o, True)

    P.emit(es)
    es.close()
    return nc


def _slot_images(w_in, w_ba, w_bl, w_o, wq, u, v):
    imgs = np.empty((NGRP, 128, SLOT), np.float32)

    def wgroups(W):
        C = W.shape[1]
        Wr = W.reshape(8, 128, C // 512, 512).transpose(2, 1, 0, 3)
        return Wr.reshape(C // 512, 128, SLOT)

    g = 0
    for W in (w_in, w_ba, w_bl, w_o, wq):
        im = wgroups(W)
        imgs[g:g + im.shape[0]] = im
        g += im.shape[0]
    assert g == NGRP_W
    u4 = u.reshape(128, 64, 2, 8, 128)
    imgs[NGRP_W:, :, 0:2048] = u4.transpose(1, 4, 3, 2, 0).reshape(64, 128, 2048)
    v4 = v.reshape(128, 64, 2, 1024)
    imgs[NGRP_W:, :, 2048:4096] = v4.transpose(1, 0, 2, 3).reshape(64, 128, 2048)
    return imgs.reshape(NGRP * 256, 2048)


def _bias_tables(rel_table):
    ko = np.arange(640)[:, None]
    qq = np.arange(128)[None, :]
    rel = np.clip(512 + qq - ko, -128, 128) + 128
    kc = ko // 64
    cq = qq // 64
    valid = (kc >= cq) & (kc <= cq + 8)
    tb = np.empty((128, 3, 16, 128), np.float32)
    for ti, r in enumerate((0, 3, 4)):
        sl = slice(r * 128, (r + 1) * 128)
        vals = rel_table[:, rel[sl]]
        vals = np.where(valid[sl][None], vals, np.float32(MASKV))
        tb[:, ti] = vals.transpose(1, 0, 2)
    cvec = np.repeat(rel_table[:, 256][:, None], 128, axis=1).reshape(1, 16 * 128)
    return tb.reshape(128, 3 * 16 * 128), np.ascontiguousarray(cvec)


def _fm(vec):
    return np.ascontiguousarray(vec.reshape(8, 128).T)


def _prep_shared(inputs):
    f = lambda k: np.asarray(inputs[k], np.float32)
    sh = {}
    sh["img"] = _slot_images(f("w_in")[0], f("w_branch_attn")[0], f("w_branch_lru")[0], f("w_out")[0], f("peer_wq")[0],
                             f("peer_u")[0], f("peer_v")[0])
    tb, cvec = _bias_tables(f("rel_table")[0])
    sh["tb"], sh["cvec"] = tb, cvec
    pv = np.empty((128, 10, 8), np.float32)
    cw = f("conv_w")[0]
    for i in range(4):
        pv[:, i, :] = _fm(cw[i])
    pv[:, 4, :] = _fm(f("conv_b")[0])
    pv[:, 5, :] = _fm(f("lru_br")[0])
    pv[:, 6, :] = _fm(f("lru_bi")[0])
    pv[:, 7, :] = _fm(f("lru_lambda")[0])
    pv[:, 8, :] = _fm(f("norm_mix")[0])
    pv[:, 9, :] = _fm(f("norm_ffn")[0])
    sh["pvec"] = pv
    sh["g3"] = f("norm_final").reshape(1, D)
    k1 = f("peer_keys1")[0]
    k2 = f("peer_keys2")[0]
    kt = np.stack([k1, k2], 0).transpose(3, 0, 1, 2)
    sh["keysT"] = np.ascontiguousarray(kt).reshape(128, 2 * 8 * 128)
    bd = np.zeros((128, 2, 8, 128), np.float32)
    for s_, key in enumerate(("lru_wr", "lru_wi")):
        w = f(key)[0]
        for c in range(8):
            bd[0:64, s_, c, 0:64] = w[2 * c]
            bd[64:128, s_, c, 64:128] = w[2 * c + 1]
    sh["bd"] = bd.reshape(128, 2 * 8 * 128)
    sh["ident"] = np.eye(128, dtype=np.float32)
    sh["iota"] = np.tile(np.arange(128, dtype=np.float32)[None, :], (128, 1))
    return sh


_NC_CACHE = {}


def kernel(**inputs):
    return _run(inputs, 8)


def _run(inputs, n_cores, **bkw):
    xpr = np.asarray(inputs["x_prompt"], np.float32)
    xsm = np.asarray(inputs["x_sample"], np.float32)
    B, S, _ = xpr.shape
    n_seq = B // n_cores
    sh = _prep_shared(inputs)
    key = (n_seq, S, tuple(sorted(bkw.items())))
    if key not in _NC_CACHE:
        _NC_CACHE[key] = build_nc(n_seq, S, **bkw)
    nc = _NC_CACHE[key]
    ck = np.asarray(inputs["cache_k"], np.float32)[0]
    cv = np.asarray(inputs["cache_v"], np.float32)[0]
    sc = np.asarray(inputs["state_conv"], np.float32)[0]
    sl = np.asarray(inputs["state_lru"], np.float32)[0]
    in_maps = []
    for c in range(n_cores):
        m = dict(sh)
        m["xp"] = np.ascontiguousarray(xpr[c * n_seq:(c + 1) * n_seq].reshape(n_seq * S, D))
        m["xs"] = np.ascontiguousarray(xsm[c])
        m["ck"] = np.ascontiguousarray(ck[c].reshape(512, D))
        m["cv"] = np.ascontiguousarray(cv[c].reshape(512, D))
        m["sconv"] = np.ascontiguousarray(sc[c].reshape(3, 8, 128).transpose(2, 1, 0))
        m["slru"] = _fm(sl[c])
        in_maps.append(m)
    res = run_bass_kernel_spmd(nc, in_maps, core_ids=list(range(n_cores)))
    R = res.results
    y_prompt = np.concatenate([r["yp"].reshape(n_seq, S, D) for r in R], 0)
    y_sample = np.stack([r["ys"] for r in R], 0)
    rows = min(512, S)
    k_prompt = np.concatenate([r["kp"].reshape(n_seq, rows, NH, DH) for r in R], 0)[None]
    v_prompt = np.concatenate([r["vp"].reshape(n_seq, rows, NH, DH) for r in R], 0)[None]
    conv_prompt = np.concatenate([r["convp"].transpose(0, 3, 2, 1).reshape(n_seq, 3, D) for r in R], 0)[None]
    lru_prompt = np.concatenate([r["lrup"].transpose(0, 2, 1).reshape(n_seq, D) for r in R], 0)[None]
    k_sample = np.stack([r["ks"].reshape(16, NH, DH) for r in R], 0)[None]
    v_sample = np.stack([r["vs"].reshape(16, NH, DH) for r in R], 0)[None]
    conv_sample = np.stack([r["convs"].transpose(2, 1, 0).reshape(3, D) for r in R], 0)[None]
    lru_sample = np.stack([r["lrus"].T.reshape(D) for r in R], 0)[None]
    f32 = lambda a: np.ascontiguousarray(a, dtype=np.float32)
    return tuple(f32(a) for a in (y_prompt, y_sample, k_prompt, v_prompt, conv_prompt, lru_prompt,
                                  k_sample, v_sample, conv_sample, lru_sample))
```
